# Optimizing a Trainium2 kernel written in Bass

```python
import jax, jax.numpy as jnp
from jax import lax
import numpy as np

D_MODEL = 1024
BATCH = 16
SEQ = 4096
DEPTH = 2

N_EVEN = (DEPTH + 1) // 2
N_ODD = DEPTH // 2

HGRN_KDIM = 128
HGRN_WIDTH = D_MODEL // 2
HGRN_HEADS = HGRN_WIDTH // HGRN_KDIM
HGRN_VDIM = HGRN_WIDTH // HGRN_HEADS
HGRN_CHUNK = 64

FOX_HDIM = 64
FOX_WIDTH = D_MODEL // 2
FOX_HEADS = FOX_WIDTH // FOX_HDIM
FOX_BLOCK = 128

MIX_WIDTH = HGRN_WIDTH + FOX_WIDTH
AB_IN = 4 * HGRN_WIDTH + 4 * FOX_WIDTH + FOX_HEADS

RWKV_HDIM = 64
RWKV_HEADS = D_MODEL // RWKV_HDIM
DECAY_LORA = 64
AAA_LORA = 64
GATE_LORA = 128

D_FF = 4 * D_MODEL

RMS_EPS = 1e-6
GN_EPS = 64e-5

kernel_name = "hgrn2_fox_rwkv7_hybrid"


def rmsnorm(x, g, eps=RMS_EPS):
    xf = x.astype(jnp.float32)
    y = xf * lax.rsqrt(jnp.mean(xf * xf, axis=-1, keepdims=True) + eps)
    return (y * g.astype(jnp.float32)).astype(x.dtype)


def hgrn2_chunkwise(q, k, v, log_f):
    B, S, H, DK = q.shape
    DV = v.shape[-1]
    C = HGRN_CHUNK
    NC = S // C

    def to_chunks(t):
        return t.astype(jnp.float32).reshape(B, NC, C, H, t.shape[-1]).transpose(1, 0, 3, 2, 4)

    qc, kc, vc, gc = to_chunks(q), to_chunks(k), to_chunks(v), to_chunks(log_f)
    causal = jnp.tril(jnp.ones((C, C), dtype=bool))[:, :, None]

    def step(state, inp):
        qb, kb, vb, gb = inp
        b = jnp.cumsum(gb, axis=2)
        diff = b[:, :, :, None, :] - b[:, :, None, :, :]
        decay = jnp.exp(jnp.where(causal, diff, -jnp.inf))
        scores = jnp.einsum('bhtd,bhtsd,bhsd->bhts', qb, decay, kb)
        o = (jnp.einsum('bhts,bhsv->bhtv', scores, vb)
             + jnp.einsum('bhtd,bhdv->bhtv', qb * jnp.exp(b), state))
        b_last = b[:, :, -1:, :]
        state = (state * jnp.exp(b_last[:, :, 0, :])[..., None]
                 + jnp.einsum('bhsd,bhsv->bhdv', kb * jnp.exp(b_last - b), vb))
        return state, o

    s0 = jnp.zeros((B, H, DK, DV), jnp.float32)
    _, o = lax.scan(step, s0, (qc, kc, vc, gc))
    return o.transpose(1, 0, 3, 2, 4).reshape(B, S, H, DV)


def fox_attention(q, k, v, log_f):
    B, S, H, Dh = q.shape
    scale = Dh ** -0.5
    c = jnp.cumsum(log_f.astype(jnp.float32), axis=1).transpose(0, 2, 1)
    qh = q.transpose(0, 2, 1, 3)
    kh = k.transpose(0, 2, 1, 3)
    vh = v.transpose(0, 2, 1, 3)
    q_idx = jnp.arange(FOX_BLOCK)
    outs = []
    for i in range(S // FOX_BLOCK):
        q0 = i * FOX_BLOCK
        kv_len = q0 + FOX_BLOCK
        qb = qh[:, :, q0:kv_len]
        kb = kh[:, :, :kv_len]
        vb = vh[:, :, :kv_len]
        logits = (jnp.einsum('bhqd,bhkd->bhqk', qb, kb).astype(jnp.float32) * scale
                  + c[:, :, q0:kv_len, None] - c[:, :, None, :kv_len])
        mask = (q0 + q_idx)[:, None] >= jnp.arange(kv_len)[None, :]
        p = jax.nn.softmax(jnp.where(mask, logits, -jnp.inf), axis=-1)
        outs.append(jnp.einsum('bhqk,bhkd->bhqd', p.astype(vb.dtype), vb))
    return jnp.concatenate(outs, axis=2).transpose(0, 2, 1, 3)


def hgrn_fox_mixer(h, w_in, lb, hgrn_norm_g, fox_fb, fox_q_g, fox_k_g, w_out):
    B, S, _ = h.shape
    proj = h @ w_in
    sizes = [HGRN_WIDTH] * 4 + [FOX_WIDTH] * 4
    splits = np.cumsum(sizes).tolist()
    a_q, a_f, a_i, a_g, b_q, b_k, b_v, b_g, b_f = jnp.split(proj, splits, axis=-1)

    lbf = lb.astype(jnp.float32)
    f = lbf + (1.0 - lbf) * jax.nn.sigmoid(a_f.astype(jnp.float32))
    hd = (B, S, HGRN_HEADS, HGRN_KDIM)
    o_a = hgrn2_chunkwise(jax.nn.silu(a_q).reshape(hd), (1.0 - f).reshape(hd),
                          a_i.reshape(B, S, HGRN_HEADS, HGRN_VDIM), jnp.log(f).reshape(hd))
    o_a = rmsnorm(o_a, hgrn_norm_g.reshape(HGRN_HEADS, HGRN_VDIM))
    o_a = o_a * jax.nn.silu(a_g.astype(jnp.float32)).reshape(B, S, HGRN_HEADS, HGRN_VDIM)

    fd = (B, S, FOX_HEADS, FOX_HDIM)
    q = rmsnorm(b_q.reshape(fd), fox_q_g)
    k = rmsnorm(b_k.reshape(fd), fox_k_g)
    log_fg = jax.nn.log_sigmoid((b_f + fox_fb).astype(jnp.float32))
    o_b = fox_attention(q, k, b_v.reshape(fd), log_fg)
    o_b = o_b * jax.nn.sigmoid(b_g).reshape(fd)

    y = jnp.concatenate([o_a.reshape(B, S, HGRN_WIDTH).astype(h.dtype),
                         o_b.reshape(B, S, FOX_WIDTH).astype(h.dtype)], axis=-1)
    return y @ w_out


def wkv7_scan(r, w, k, v, a, b):
    B, S, H, N = r.shape

    def step(state, inp):
        r_t, w_t, k_t, v_t, a_t, b_t = inp
        sa = jnp.einsum('bhvk,bhk->bhv', state, a_t)
        state = (state * w_t[:, :, None, :] + sa[..., None] * b_t[:, :, None, :]
                 + v_t[..., None] * k_t[:, :, None, :])
        return state, jnp.einsum('bhvk,bhk->bhv', state, r_t)

    xs = tuple(t.astype(jnp.float32).transpose(1, 0, 2, 3) for t in (r, w, k, v, a, b))
    s0 = jnp.zeros((B, H, N, N), jnp.float32)
    _, y = lax.scan(step, s0, xs)
    return y.transpose(1, 0, 2, 3)


def rwkv7_mixer(h, mu, w_rkv, w0, w1, w2, a0, a1, a2, g1, g2, k_k, k_a, r_k, lnx_g, lnx_b, w_o):
    B, S, D = h.shape
    H, N = RWKV_HEADS, RWKV_HDIM
    xx = jnp.pad(h, ((0, 0), (1, 0), (0, 0)))[:, :-1] - h
    mixed = h[:, :, None, :] + xx[:, :, None, :] * mu
    xr, xw, xk, xv, xa, xg = [mixed[:, :, i] for i in range(6)]

    r = xr @ w_rkv[0]
    k = xk @ w_rkv[1]
    v = xv @ w_rkv[2]
    w_log = -jax.nn.softplus(-(w0 + jnp.tanh(xw @ w1) @ w2).astype(jnp.float32)) - 0.5
    decay = jnp.exp(-jnp.exp(w_log))
    a = jax.nn.sigmoid((a0 + (xa @ a1) @ a2).astype(jnp.float32))
    g = jax.nn.sigmoid(xg @ g1) @ g2

    hs = (B, S, H, N)
    kk = (k * k_k).astype(jnp.float32).reshape(hs)
    kk = kk * lax.rsqrt(jnp.maximum(jnp.sum(kk * kk, axis=-1, keepdims=True), 1e-24))
    k = k.astype(jnp.float32) * (1.0 + (a - 1.0) * k_a.astype(jnp.float32))
    r4, k4, v4 = r.astype(jnp.float32).reshape(hs), k.reshape(hs), v.astype(jnp.float32).reshape(hs)
    a4 = a.reshape(hs)

    y = wkv7_scan(r4, decay.reshape(hs), k4, v4, -kk, kk * a4)
    mean = jnp.mean(y, axis=-1, keepdims=True)
    var = jnp.mean(jnp.square(y - mean), axis=-1, keepdims=True)
    y = ((y - mean) * lax.rsqrt(var + GN_EPS)).reshape(B, S, D) * lnx_g + lnx_b
    bonus = jnp.sum(r4 * k4 * r_k.astype(jnp.float32), axis=-1, keepdims=True) * v4
    y = y + bonus.reshape(B, S, D)
    return (y * g).astype(h.dtype) @ w_o


def sqrelu_mlp(h, w_up, w_down):
    return jnp.square(jax.nn.relu(h @ w_up)) @ w_down


def setup_inputs(seed: int = 0) -> dict:
    key = jax.random.key(seed)
    ks = iter(jax.random.split(key, 40))
    D = D_MODEL

    def nrm(shape, scale):
        return jax.random.normal(next(ks), shape, jnp.float32) * scale

    def unif(shape, lo, hi):
        return jax.random.uniform(next(ks), shape, jnp.float32, lo, hi)

    return {
        "x": nrm((BATCH, SEQ, D), 1.0),
        "norm_mix_g": 1.0 + nrm((DEPTH, D), 0.02),
        "norm_ffn_g": 1.0 + nrm((DEPTH, D), 0.02),
        "ab_w_in": nrm((N_EVEN, D, AB_IN), D ** -0.5),
        "hgrn_lower_bounds": nrm((DEPTH + 1, HGRN_WIDTH), 0.1),
        "hgrn_norm_g": 1.0 + nrm((N_EVEN, HGRN_WIDTH), 0.02),
        "fox_forget_bias": 2.0 + nrm((N_EVEN, FOX_HEADS), 0.1),
        "fox_q_norm_g": 1.0 + nrm((N_EVEN, FOX_HDIM), 0.02),
        "fox_k_norm_g": 1.0 + nrm((N_EVEN, FOX_HDIM), 0.02),
        "ab_w_out": nrm((N_EVEN, MIX_WIDTH, D), MIX_WIDTH ** -0.5),
        "rwkv_mu": unif((N_ODD, 6, D), 0.0, 1.0),
        "rwkv_w_rkv": nrm((N_ODD, 3, D, D), D ** -0.5),
        "rwkv_w0": nrm((N_ODD, D), 0.5),
        "rwkv_w1": nrm((N_ODD, D, DECAY_LORA), D ** -0.5),
        "rwkv_w2": nrm((N_ODD, DECAY_LORA, D), 0.1 * DECAY_LORA ** -0.5),
        "rwkv_a0": nrm((N_ODD, D), 0.1),
        "rwkv_a1": nrm((N_ODD, D, AAA_LORA), D ** -0.5),
        "rwkv_a2": nrm((N_ODD, AAA_LORA, D), 0.1 * AAA_LORA ** -0.5),
        "rwkv_g1": nrm((N_ODD, D, GATE_LORA), D ** -0.5),
        "rwkv_g2": nrm((N_ODD, GATE_LORA, D), GATE_LORA ** -0.5),
        "rwkv_k_k": 0.85 + nrm((N_ODD, D), 0.02),
        "rwkv_k_a": 1.0 + nrm((N_ODD, D), 0.02),
        "rwkv_r_k": nrm((N_ODD, RWKV_HEADS, RWKV_HDIM), 0.1),
        "rwkv_lnx_g": 1.0 + nrm((N_ODD, D), 0.02),
        "rwkv_lnx_b": nrm((N_ODD, D), 0.02),
        "rwkv_w_o": nrm((N_ODD, D, D), D ** -0.5),
        "mlp_w_up": nrm((DEPTH, D, D_FF), D ** -0.5),
        "mlp_w_down": nrm((DEPTH, D_FF, D), D_FF ** -0.5),
    }


def reference(x, norm_mix_g, norm_ffn_g, ab_w_in, hgrn_lower_bounds, hgrn_norm_g,
              fox_forget_bias, fox_q_norm_g, fox_k_norm_g, ab_w_out, rwkv_mu, rwkv_w_rkv,
              rwkv_w0, rwkv_w1, rwkv_w2, rwkv_a0, rwkv_a1, rwkv_a2, rwkv_g1, rwkv_g2,
              rwkv_k_k, rwkv_k_a, rwkv_r_k, rwkv_lnx_g, rwkv_lnx_b, rwkv_w_o,
              mlp_w_up, mlp_w_down):
    lb_all = jnp.cumsum(jax.nn.softmax(hgrn_lower_bounds.astype(jnp.float32), axis=0), axis=0)
    h = x
    for layer in range(DEPTH):
        j = layer // 2
        hn = rmsnorm(h, norm_mix_g[layer])
        if layer % 2 == 0:
            mix = hgrn_fox_mixer(hn, ab_w_in[j], lb_all[layer], hgrn_norm_g[j],
                                 fox_forget_bias[j], fox_q_norm_g[j], fox_k_norm_g[j],
                                 ab_w_out[j])
        else:
            mix = rwkv7_mixer(hn, rwkv_mu[j], rwkv_w_rkv[j], rwkv_w0[j], rwkv_w1[j],
                              rwkv_w2[j], rwkv_a0[j], rwkv_a1[j], rwkv_a2[j], rwkv_g1[j],
                              rwkv_g2[j], rwkv_k_k[j], rwkv_k_a[j], rwkv_r_k[j],
                              rwkv_lnx_g[j], rwkv_lnx_b[j], rwkv_w_o[j])
        h = h + mix
        h = h + sqrelu_mlp(rmsnorm(h, norm_ffn_g[layer]), mlp_w_up[layer], mlp_w_down[layer])
    return h
```

```python
import numpy as np
import concourse.bass as bass
import concourse.mybir as mybir

F32 = mybir.dt.float32
BF16 = mybir.dt.bfloat16
AF = mybir.ActivationFunctionType
ALU = mybir.AluOpType
AX = mybir.AxisListType


class Buf:
    __slots__ = ("name", "lw", "rd")

    def __init__(self, name):
        self.name = name
        self.lw = None
        self.rd = {}


class V:
    __slots__ = ("ap", "bufs")

    def __init__(self, ap, bufs):
        self.ap = ap
        self.bufs = bufs

    def __getitem__(self, idx):
        return V(self.ap[idx], self.bufs)

    def rr(self, pat, **kw):
        return V(self.ap.rearrange(pat, **kw), self.bufs)

    def bc(self, shape):
        return V(self.ap.broadcast_to(shape), self.bufs)


class Tile:
    def __init__(self, t, name):
        self.t = t
        self.buf = Buf(name)

    def __getitem__(self, idx):
        return V(self.t[idx], (self.buf,))

    def v(self):
        return V(self.t[:], (self.buf,))

    def part(self, name):
        p = Tile.__new__(Tile)
        p.t = self.t
        p.buf = Buf(name)
        return p


class Sched:
    ENG = ("pe", "act", "dve", "pool", "sp")

    def __init__(self, nc, stack, n_dma_sems=24):
        self.nc = nc
        self.stack = stack
        self.eng = {"pe": nc.tensor, "act": nc.scalar, "dve": nc.vector,
                    "pool": nc.gpsimd, "sp": nc.sync}
        self.sem = {}
        self.cnt = {}
        for e in self.ENG:
            self.sem[e] = stack.enter_context(nc.semaphore("s_" + e))
            self.cnt[e] = 0
        self.dsem = [stack.enter_context(nc.semaphore("d%d" % i)) for i in range(n_dma_sems)]
        self.dcnt = [0] * n_dma_sems
        self.dnext = 0
        self.waited = {e: {} for e in self.ENG}
        self.ninst = 0

    def _semh(self, key):
        if isinstance(key, str):
            return self.sem[key]
        return self.dsem[key]

    def _wait(self, e, key, val):
        w = self.waited[e]
        if w.get(key, 0) >= val:
            return
        self.eng[e].wait_ge(self._semh(key), val)
        w[key] = val
        self.ninst += 1

    def _deps(self, e, reads, writes, skip_same_pe=True):
        need = {}

        def add(dep):
            if dep is None:
                return
            k, v = dep
            if need.get(k, 0) < v:
                need[k] = v
        for b in reads:
            add(b.lw)
        for b in writes:
            add(b.lw)
            for k, v in b.rd.items():
                add((k, v))
        for k, v in need.items():
            if e == "pe" and k == "pe":
                continue
            self._wait(e, k, v)

    def _mark(self, key, val, reads, writes):
        for b in reads:
            if b.rd.get(key, 0) < val:
                b.rd[key] = val
        for b in writes:
            b.lw = (key, val)
            b.rd = {}

    def op(self, e, fn, outs, ins):
        reads = [b for v in ins if v is not None for b in v.bufs]
        writes = [b for v in outs if v is not None for b in v.bufs]
        self._deps(e, reads, writes)
        self.cnt[e] += 1
        inst = fn()
        inst.then_inc(self.sem[e], 1)
        self._mark(e, self.cnt[e], reads, writes)
        self.ninst += 1
        return inst

    def dma(self, q, out, in_, **kw):
        reads = list(in_.bufs)
        writes = list(out.bufs)
        i = self.dnext
        self.dnext = (self.dnext + 1) % len(self.dsem)
        if self.dcnt[i] > 0:
            self._wait(q, i, self.dcnt[i])
        self._deps(q, reads, writes)
        self.dcnt[i] += 16
        self.eng[q].dma_start(out=out.ap, in_=in_.ap, **kw).then_inc(self.dsem[i], 16)
        self._mark(i, self.dcnt[i], reads, writes)
        self.ninst += 1

    def barrier(self):
        for e in self.ENG:
            for e2 in self.ENG:
                if e2 != e and self.cnt[e2] > 0:
                    self._wait(e, e2, self.cnt[e2])
            for i in range(len(self.dsem)):
                if self.dcnt[i] > 0:
                    self._wait(e, i, self.dcnt[i])

    def final_wait(self, e="sp"):
        for i in range(len(self.dsem)):
            if self.dcnt[i] > 0:
                self._wait(e, i, self.dcnt[i])
        for e2 in self.ENG:
            if e2 != e and self.cnt[e2] > 0:
                self._wait(e, e2, self.cnt[e2])

    def sbuf(self, stack, name, shape, dt):
        t = stack.enter_context(self.nc.sbuf_tensor(name, list(shape), dt))
        return Tile(t, name)

    def psum(self, stack, name, shape, dt):
        t = stack.enter_context(self.nc.psum_tensor(name, list(shape), dt))
        return Tile(t, name)

    def mm(self, out, lhsT, rhs, start=True, stop=True, **kw):
        return self.op("pe", lambda: self.nc.tensor.matmul(out.ap, lhsT.ap, rhs.ap, start=start, stop=stop, **kw),
                       [out], [lhsT, rhs])

    def tr(self, out, in_, ident):
        return self.op("pe", lambda: self.nc.tensor.transpose(out.ap, in_.ap, ident.ap), [out], [in_, ident])

    def act(self, out, in_, func, bias=None, scale=None, accum_out=None):
        kw = {}
        ins = [in_]
        if bias is not None:
            if isinstance(bias, V):
                kw["bias"] = bias.ap
                ins.append(bias)
            else:
                kw["bias"] = bias
        if scale is not None:
            if isinstance(scale, V):
                kw["scale"] = scale.ap
                ins.append(scale)
            else:
                kw["scale"] = scale
        outs = [out]
        if accum_out is not None:
            kw["accum_out"] = accum_out.ap
            outs.append(accum_out)
        return self.op("act", lambda: self.nc.scalar.activation(out.ap, in_.ap, func, **kw), outs, ins)

    def _ve(self, e):
        return self.nc.vector if e == "dve" else self.nc.gpsimd

    def tt(self, e, out, in0, in1, op):
        return self.op(e, lambda: self._ve(e).tensor_tensor(out.ap, in0.ap, in1.ap, op), [out], [in0, in1])

    def ts(self, e, out, in0, s1, op0, s2=None, op1=None, accum_out=None):
        ins = [in0]
        a1 = s1
        if isinstance(s1, V):
            ins.append(s1)
            a1 = s1.ap
        a2 = s2
        if isinstance(s2, V):
            ins.append(s2)
            a2 = s2.ap
        kw = {}
        outs = [out]
        if op1 is not None:
            kw["op1"] = op1
        if accum_out is not None:
            kw["accum_out"] = accum_out.ap
            outs.append(accum_out)
        return self.op(e, lambda: self._ve(e).tensor_scalar(out.ap, in0.ap, a1, a2, op0, **kw), outs, ins)

    def stt(self, out, in0, scalar, in1, op0, op1):
        ins = [in0, in1]
        sc = scalar
        if isinstance(scalar, V):
            ins.append(scalar)
            sc = scalar.ap
        return self.op("dve", lambda: self.nc.vector.scalar_tensor_tensor(out.ap, in0.ap, sc, in1.ap, op0, op1),
                       [out], ins)

    def copy(self, e, out, in_):
        if e == "act":
            return self.op("act", lambda: self.nc.scalar.copy(out.ap, in_.ap), [out], [in_])
        return self.op(e, lambda: self._ve(e).tensor_copy(out.ap, in_.ap), [out], [in_])

    def memset(self, e, out, val):
        return self.op(e, lambda: self._ve(e).memset(out.ap, val), [out], [])

    def reduce(self, out, in_, op=None, axis=None):
        op = op or ALU.add
        axis = axis or AX.X
        return self.op("dve", lambda: self.nc.vector.tensor_reduce(out.ap, in_.ap, axis, op), [out], [in_])

    def recip(self, out, in_):
        return self.op("dve", lambda: self.nc.vector.reciprocal(out.ap, in_.ap), [out], [in_])


import contextlib
from concourse.bass_utils import run_bass_kernel_spmd

D = 1024
DFF = 4096
ABIN = 4104
EPS = 1e-6
GN_EPS = 64e-5
NCORES = 8


class Ring:
    def __init__(self, S, st, name, shape, dt, n, psum=False):
        mk = S.psum if psum else S.sbuf
        self.tiles = [mk(st, "%s%d" % (name, i), shape, dt) for i in range(n)]
        self.i = 0

    def next(self):
        t = self.tiles[self.i % len(self.tiles)]
        self.i += 1
        return t


def dv(ap):
    return V(ap, (Buf("d"),))


class Ctx:
    pass


C_ID, C_LE, C_LT, C_GT, C_LE64, C_BD, C_ONES, C_SEL0, C_SEL1 = range(9)
NCONST = 9


def make_consts():
    p = np.arange(128)[:, None]
    f = np.arange(128)[None, :]
    blocks = [
        (p == f), (p <= f), (p < f), (p > f),
        (p <= f) & (p // 64 == f // 64), (p // 64 == f // 64),
        np.ones((128, 128), bool), (p // 64 == 0) & (f >= 0), (p // 64 == 1) & (f >= 0),
    ]
    return np.concatenate([b.astype(np.float32) for b in blocks], axis=1)


def load_w(S, dst, src_ap, K, N, q="pool"):
    src = src_ap.rearrange("(k p) n -> p k n", p=128)
    for k in range(K):
        for c0 in range(0, N, 1024):
            c1 = min(N, c0 + 1024)
            S.dma(q, dst[:, k, c0:c1], dv(src[:, k, c0:c1]))


def bload(S, dstv, src_row_ap, n):
    S.dma("sp", dstv, dv(src_row_ap.broadcast_to([128, n])))


def cblk(t, i):
    return t[:, i * 128:(i + 1) * 128]


def rms_rstd(S, C, ss, rstd, n, eps):
    S.ts("dve", rstd, ss, 1.0 / n, ALU.mult, eps, ALU.add)
    S.act(rstd, rstd, AF.Sqrt)
    S.recip(rstd, rstd)


def phase_A(C):
    S = C.S
    T, NSEQ = C.T, C.NSEQ
    NT = T // 128
    cf, cb = C.cf, C.cb
    with contextlib.ExitStack() as st:
        w_in = S.sbuf(st, "w_in", [128, 8, ABIN], BF16)
        load_w(S, w_in, C.ab_w_in[0], 8, ABIN)
        gb = S.sbuf(st, "gbA", [128, D], F32)
        bload(S, gb.v(), C.norm_mix_g[0:1, :], D)
        G = S.sbuf(st, "G", [128, 3, 512], F32)
        for l in range(3):
            bload(S, G[:, l, :], C.hgrn_lb[l:l + 1, :], 512)
        lbb = S.sbuf(st, "lbb", [128, 512], F32)
        oml = S.sbuf(st, "oml", [128, 512], F32)
        S.act(G.v(), G.v(), AF.Exp)
        S.tt("dve", lbb.v(), G[:, 0, :], G[:, 1, :], ALU.add)
        S.tt("dve", lbb.v(), lbb.v(), G[:, 2, :], ALU.add)
        S.recip(lbb.v(), lbb.v())
        S.tt("dve", lbb.v(), lbb.v(), G[:, 0, :], ALU.mult)
        S.ts("dve", oml.v(), lbb.v(), -1.0, ALU.mult, 1.0, ALU.add)
        hng = S.sbuf(st, "hng", [128, 512], F32)
        bload(S, hng.v(), C.hgrn_norm_g[0:1, :], 512)
        fqg = S.sbuf(st, "fqg", [128, 8, 64], F32)
        fkg = S.sbuf(st, "fkg", [128, 8, 64], F32)
        for h in range(8):
            bload(S, fqg[:, h, :], C.fox_q_g[0:1, :], 64)
            bload(S, fkg[:, h, :], C.fox_k_g[0:1, :], 64)
        fbb = S.sbuf(st, "fbb", [128, 8], F32)
        bload(S, fbb.v(), C.fox_fb[0:1, :], 8)
        m64x4 = S.sbuf(st, "m64x4", [128, 4, 128], F32)
        for h in range(4):
            S.copy("pool", m64x4[:, h, :], cblk(cf, C_LE64))
        Sf = S.sbuf(st, "Sf", [128, 512], F32)
        SbA = S.sbuf(st, "SbA", [128, 512], BF16)
        SbB = S.sbuf(st, "SbB", [128, 512], BF16)
        ctot = S.sbuf(st, "ctot", [128, 8], F32)
        r_x = Ring(S, st, "xA", [128, D], F32, 2)
        r_junk = Ring(S, st, "junkA", [128, D], BF16, 1)
        r_st = Ring(S, st, "stA", [128, 8], F32, 4)
        r_hn = Ring(S, st, "hnA", [128, D], BF16, 2)
        r_hnT = Ring(S, st, "hnTA", [128, 8, 128], BF16, 2)
        r_f512 = Ring(S, st, "f512A", [128, 512], F32, 14)
        r_b512 = Ring(S, st, "b512A", [128, 512], BF16, 10)
        r_qtT = Ring(S, st, "qtT", [128, 4, 128], BF16, 2)
        r_ktT = Ring(S, st, "ktT", [128, 4, 128], BF16, 2)
        r_q0 = Ring(S, st, "qtT0", [128, 4, 128], BF16, 2)
        r_q1 = Ring(S, st, "qtT1", [128, 4, 128], BF16, 2)
        for t in r_q0.tiles + r_q1.tiles:
            S.memset("pool", t.v(), 0.0)
        r_fT = Ring(S, st, "fT", [128, 4, 128], BF16, 4)
        tp = Ring(S, st, "tpA", [128, 1024], BF16, 2, psum=True)
        pp = Ring(S, st, "ppA", [128, 512], F32, 2, psum=True)
        gp = Ring(S, st, "gpA", [128, 512], F32, 4, psum=True)

        for s in range(NSEQ):
            S.memset("pool", Sf.v(), 0.0)
            S.memset("pool", SbA.v(), 0.0)
            S.memset("pool", ctot.v(), 0.0)
            for i in range(NT):
                r0 = s * T + i * 128
                xt = r_x.next()
                S.dma("sp", xt.v(), dv(C.x[r0:r0 + 128, :]))
                junk = r_junk.next()
                sv = r_st.next()
                S.act(junk.v(), xt.v(), AF.Square, accum_out=sv[:, 0:1])
                rms_rstd(S, C, sv[:, 0:1], sv[:, 1:2], D, EPS)
                hn = r_hn.next()
                S.stt(hn.v(), xt.v(), sv[:, 1:2], gb.v(), ALU.mult, ALU.mult)
                pT = tp.next()
                for k in range(8):
                    S.tr(pT[:, k * 128:(k + 1) * 128], hn[:, k * 128:(k + 1) * 128], cblk(cb, C_ID))
                hnT = r_hnT.next()
                S.copy("act", hnT.v().rr("p k t -> p (k t)"), pT.v())
                if C.cut == 1:
                    continue

                def proj(og, n=512):
                    ps = pp.next()
                    for k in range(8):
                        S.mm(ps[:, 0:n], hnT[:, k, :], w_in[:, k, og * 512:og * 512 + n], start=(k == 0), stop=(k == 7))
                    return ps
                ps = proj(0)
                qs = r_f512.next()
                S.act(qs.v(), ps.v(), AF.Silu)
                ps = proj(1)
                f = r_f512.next()
                S.act(f.v(), ps.v(), AF.Sigmoid)
                S.tt("dve", f.v(), f.v(), oml.v(), ALU.mult)
                S.tt("dve", f.v(), f.v(), lbb.v(), ALU.add)
                gl = r_f512.next()
                S.act(gl.v(), f.v(), AF.Ln)
                kf = r_f512.next()
                S.ts("pool", kf.v(), f.v(), -1.0, ALU.mult, 1.0, ALU.add)
                ps = proj(2)
                vt = r_b512.next()
                S.copy("act", vt.v(), ps.v())
                ps = proj(3)
                sgate = r_f512.next()
                S.act(sgate.v(), ps.v(), AF.Silu)

                def qknorm(og, gtile, dst_dram):
                    ps = proj(og)
                    bq = r_f512.next()
                    S.copy("act", bq.v(), ps.v())
                    sq = r_f512.next()
                    S.tt("pool", sq.v(), bq.v(), bq.v(), ALU.mult)
                    sv2 = r_st.next()
                    S.reduce(sv2.v(), sq.v().rr("p (h d) -> p h d", h=8))
                    rms_rstd(S, C, sv2.v(), sv2.v(), 64, EPS)
                    S.tt("dve", bq.v().rr("p (h d) -> p h d", h=8), bq.v().rr("p (h d) -> p h d", h=8),
                         V(sv2.v().ap.unsqueeze(2).broadcast_to([128, 8, 64]), sv2.v().bufs), ALU.mult)
                    qn = r_b512.next()
                    S.tt("pool", qn.v(), bq.v(), gtile.v().rr("p h d -> p (h d)"), ALU.mult)
                    pq = tp.next()
                    for hp in range(4):
                        S.tr(pq[:, hp * 128:(hp + 1) * 128], qn[:, hp * 128:(hp + 1) * 128], cblk(cb, C_ID))
                    qT = r_fT.next()
                    S.copy("dve", qT.v().rr("p a t -> p (a t)"), pq[:, 0:512])
                    S.dma("sp", dv(dst_dram[s, :, :, i * 128:(i + 1) * 128].rearrange("a p t -> p a t")), qT.v())
                qknorm(4, fqg, C.fq)
                qknorm(5, fkg, C.fk)
                ps = proj(6)
                fvt = r_b512.next()
                S.copy("act", fvt.v(), ps.v())
                S.dma("sp", dv(C.fv[r0:r0 + 128, :]), fvt.v())
                ps = proj(7)
                fgt = r_b512.next()
                S.act(fgt.v(), ps.v(), AF.Sigmoid)
                S.dma("sp", dv(C.fg[r0:r0 + 128, :]), fgt.v())
                ps = proj(8, 8)
                lf = r_st.next()
                S.tt("dve", lf.v(), ps[:, 0:8], fbb.v(), ALU.add)
                S.act(lf.v(), lf.v(), AF.Sigmoid)
                S.act(lf.v(), lf.v(), AF.Ln)
                cps = gp.next()
                S.mm(cps[:, 0:8], cblk(cf, C_LE), lf.v())
                S.mm(cps[:, 8:16], cblk(cf, C_ONES), lf.v())
                cs = r_st.next()
                S.tt("dve", cs.v(), cps[:, 0:8], ctot.v(), ALU.add)
                S.dma("sp", dv(C.fc[r0:r0 + 128, :]), cs.v())
                S.tt("dve", ctot.v(), ctot.v(), cps[:, 8:16], ALU.add)

                if C.cut == 2:
                    continue
                bps = gp.next()
                S.mm(bps.v(), cblk(cf, C_LE64), gl.v())
                eb = r_f512.next()
                S.act(eb.v(), bps.v(), AF.Exp)
                enb = r_f512.next()
                S.act(enb.v(), bps.v(), AF.Exp, scale=-1.0)
                qt = r_b512.next()
                S.tt("dve", qt.v(), qs.v(), eb.v(), ALU.mult)
                kt = r_b512.next()
                S.tt("pool", kt.v(), kf.v(), enb.v(), ALU.mult)
                if C.cut == 3:
                    continue
                ebl = []
                for c in range(2):
                    eps_ = gp.next()
                    for h in range(4):
                        S.mm(eps_[:, h * 128:(h + 1) * 128], gl[:, h * 128:(h + 1) * 128], cblk(cf, C_SEL0 + c))
                    e_ = r_f512.next()
                    S.act(e_.v(), eps_.v(), AF.Exp)
                    ebl.append(e_)
                if C.cut == 4:
                    continue
                pqq = tp.next()
                pkk = tp.next()
                for h in range(4):
                    S.tr(pqq[:, h * 128:(h + 1) * 128], qt[:, h * 128:(h + 1) * 128], cblk(cb, C_ID))
                for h in range(4):
                    S.tr(pkk[:, h * 128:(h + 1) * 128], kt[:, h * 128:(h + 1) * 128], cblk(cb, C_ID))
                qtT = r_qtT.next()
                ktT = r_ktT.next()
                q0 = r_q0.next()
                q1 = r_q1.next()
                S.copy("act", qtT.v().rr("p h t -> p (h t)"), pqq[:, 0:512])
                S.copy("dve", ktT.v().rr("p h t -> p (h t)"), pkk[:, 0:512])
                pq3 = pqq[:, 0:512].rr("p (h t) -> p h t", h=4)
                S.copy("act", q0[:, :, 0:64], pq3[:, :, 0:64])
                S.copy("act", q1[:, :, 64:128], pq3[:, :, 64:128])
                if C.cut == 5:
                    continue
                scp = gp.next()
                for h in range(4):
                    S.mm(scp[:, h * 128:(h + 1) * 128], ktT[:, h, :], qtT[:, h, :])
                scT = r_b512.next()
                S.tt("dve", scT.v(), scp.v(), m64x4.v().rr("p h t -> p (h t)"), ALU.mult)
                if C.cut == 6:
                    continue

                def state_update(c, dstb):
                    ups = gp.next()
                    for h in range(4):
                        S.mm(ups[:, h * 128:(h + 1) * 128], kt[c * 64:(c + 1) * 64, h * 128:(h + 1) * 128],
                             vt[c * 64:(c + 1) * 64, h * 128:(h + 1) * 128])
                    S.tt("dve", Sf.v(), Sf.v(), ups.v(), ALU.add)
                    S.tt("pool", Sf.v(), Sf.v(), ebl[c].v(), ALU.mult)
                    S.copy("act", dstb.v(), Sf.v())
                state_update(0, SbB)
                if C.cut == 7:
                    continue
                ops = gp.next()
                for h in range(4):
                    hs = slice(h * 128, (h + 1) * 128)
                    S.mm(ops[:, hs], scT[:, hs], vt[:, hs], start=True, stop=False)
                    S.mm(ops[:, hs], q0[:, h, :], SbA[:, hs], start=False, stop=False)
                    S.mm(ops[:, hs], q1[:, h, :], SbB[:, hs], start=False, stop=True)
                state_update(1, SbA)
                if C.cut == 8:
                    continue
                osb = r_f512.next()
                S.copy("act", osb.v(), ops.v())
                sq = r_f512.next()
                S.tt("pool", sq.v(), osb.v(), osb.v(), ALU.mult)
                sv3 = r_st.next()
                S.reduce(sv3[:, 0:4], sq.v().rr("p (h d) -> p h d", h=4))
                rms_rstd(S, C, sv3[:, 0:4], sv3[:, 0:4], 128, EPS)
                S.tt("dve", osb.v().rr("p (h d) -> p h d", h=4), osb.v().rr("p (h d) -> p h d", h=4),
                     V(sv3[:, 0:4].ap.unsqueeze(2).broadcast_to([128, 4, 128]), sv3.v().bufs), ALU.mult)
                S.tt("pool", osb.v(), osb.v(), hng.v(), ALU.mult)
                yat = r_b512.next()
                S.tt("dve", yat.v(), osb.v(), sgate.v(), ALU.mult)
                S.dma("sp", dv(C.ya[r0:r0 + 128, :]), yat.v())
        S.barrier()


def phase_B(C):
    S = C.S
    T, NSEQ = C.T, C.NSEQ
    NT = T // 128
    QG = min(4, NT)
    NG = NT // QG
    cf, cb = C.cf, C.cb
    with contextlib.ExitStack() as st:
        r_kT = Ring(S, st, "kTB", [128, T], BF16, 2)
        r_qT = Ring(S, st, "qTB", [128, T], BF16, 2)
        r_v = Ring(S, st, "vB", [128, NT, 2, 65], BF16, 2)
        for t in r_v.tiles:
            S.memset("pool", t.v(), 1.0)
        r_g = Ring(S, st, "gB", [128, NT, 128], BF16, 2)
        r_y = Ring(S, st, "yB", [128, NT, 128], BF16, 2)
        cT = S.sbuf(st, "cT", [128, NT, 8], F32)
        r_cref = Ring(S, st, "cref", [128, 8], F32, 2)
        r_bias = Ring(S, st, "biasB", [128, NT], F32, 3)
        r_p = Ring(S, st, "pB", [128, QG * 128], BF16, 4)
        r_rc = Ring(S, st, "rcB", [128, QG], F32, 2)
        lemask = S.sbuf(st, "lemaskB", [128, 128], BF16)
        S.copy("pool", lemask.v(), cblk(cf, C_LE))
        stp = Ring(S, st, "stB", [128, 512], F32, 4, psum=True)
        accp = Ring(S, st, "accB", [128, 512], F32, 2, psum=True)
        for s in range(NSEQ):
            S.dma("sp", cT.v(), dv(C.fc[s * T:(s + 1) * T, :].rearrange("(b p) h -> p b h", p=128)))
            for hp in range(4):
                kT = r_kT.next()
                qT = r_qT.next()
                vt = r_v.next()
                gt = r_g.next()
                yt = r_y.next()
                S.dma("sp", kT.v(), dv(C.fk[s, hp, :, :]))
                S.dma("sp", qT.v(), dv(C.fq[s, hp, :, :]))
                for hh_ in range(2):
                    c0_ = hp * 128 + hh_ * 64
                    S.dma("sp", vt[:, :, hh_, 0:64],
                          dv(C.fv[s * T:(s + 1) * T, c0_:c0_ + 64].rearrange("(b p) d -> p b d", p=128)))
                S.dma("sp", gt.v(),
                      dv(C.fg[s * T:(s + 1) * T, hp * 128:(hp + 1) * 128].rearrange("(b p) d -> p b d", p=128)))
                for qg in range(NG):
                    i0 = qg * QG
                    nj = i0 + QG
                    tok_ref = s * T + i0 * 128 + (QG * 128) // 2 - 1
                    cref = r_cref.next()
                    bload(S, cref.v(), C.fc[tok_ref:tok_ref + 1, :], 8)
                    for hh in range(2):
                        h = hp * 2 + hh
                        pb = hh * 64
                        bias = r_bias.next()
                        S.ts("dve", bias[:, 0:nj], cT[:, 0:nj, h], cref[:, h:h + 1], ALU.subtract, -1.0, ALU.mult)
                        acc = accp.next()[:, 0:QG * 65].rr("p (a d) -> p a d", a=QG)
                        first = True
                        for j in range(nj):
                            ilo = max(j, i0)
                            ncol = (i0 + QG - ilo) * 128
                            sp_ = stp.next()
                            S.mm(sp_[:, 0:ncol], kT[pb:pb + 64, j * 128:(j + 1) * 128],
                                 qT[pb:pb + 64, ilo * 128:(i0 + QG) * 128])
                            pt = r_p.next()
                            S.act(pt[:, 0:ncol], sp_[:, 0:ncol], AF.Exp, bias=bias[:, j:j + 1], scale=0.125)
                            if j >= i0:
                                S.tt("dve", pt[:, 0:128], pt[:, 0:128], lemask.v(), ALU.mult)
                            for qi in range(ilo, i0 + QG):
                                S.mm(acc[:, qi - i0, :], pt[:, (qi - ilo) * 128:(qi - ilo + 1) * 128], vt[:, j, hh, :],
                                     start=first, stop=(j == qi), skip_group_check=True)
                                first = False
                        rc = r_rc.next()
                        S.recip(rc.v(), acc[:, :, 64])
                        for qi in range(QG):
                            S.stt(yt[:, i0 + qi, pb:pb + 64], acc[:, qi, 0:64], rc[:, qi:qi + 1],
                                  gt[:, i0 + qi, pb:pb + 64], ALU.mult, ALU.mult)
                S.dma("sp", dv(C.yb[s * T:(s + 1) * T, hp * 128:(hp + 1) * 128].rearrange("(b p) d -> p b d", p=128)),
                      yt.v())
        S.barrier()


def mlp_tile(C, R, h, gbf, w_up, w_down, out):
    S = C.S
    cb = C.cb
    junk = R.junk.next()
    sv = R.st.next()
    S.act(junk.v(), h.v(), AF.Square, accum_out=sv[:, 0:1])
    rms_rstd(S, C, sv[:, 0:1], sv[:, 1:2], D, EPS)
    hn = R.hn.next()
    S.stt(hn.v(), h.v(), sv[:, 1:2], gbf.v(), ALU.mult, ALU.mult)
    pT = R.tp.next()
    for k in range(8):
        S.tr(pT[:, k * 128:(k + 1) * 128], hn[:, k * 128:(k + 1) * 128], cblk(cb, C_ID))
    hnT = R.hnT.next()
    S.copy("act", hnT.v().rr("p k t -> p (k t)"), pT.v())
    hid = R.hid.next()
    for fg in range(8):
        ps = R.pp.next()
        for fc in range(4):
            fcol = (fg * 4 + fc) * 128
            for k in range(8):
                S.mm(ps[:, fc * 128:(fc + 1) * 128], w_up[:, k, fcol:fcol + 128], hnT[:, k, :],
                     start=(k == 0), stop=(k == 7))
        rl = R.rl.next()
        S.act(rl.v(), ps.v(), AF.Relu)
        S.tt("pool" if fg % 2 == 0 else "dve", hid[:, fg * 4:(fg + 1) * 4, :].rr("p a t -> p (a t)"), rl.v(), rl.v(), ALU.mult)
    for n in range(2):
        ps = R.pp.next()
        for kf in range(32):
            S.mm(ps.v(), hid[:, kf, :], w_down[:, kf, n * 512:(n + 1) * 512], start=(kf == 0), stop=(kf == 31))
        S.tt("dve", out[:, n * 512:(n + 1) * 512], ps.v(), h[:, n * 512:(n + 1) * 512], ALU.add)


class MlpRings:
    def __init__(self, S, st, tp, pp, sfx=""):
        self.junk = Ring(S, st, "junkM" + sfx, [128, D], BF16, 1)
        self.st = Ring(S, st, "stM" + sfx, [128, 8], F32, 4)
        self.hn = Ring(S, st, "hnM" + sfx, [128, D], BF16, 2)
        self.hnT = Ring(S, st, "hnTM" + sfx, [128, 8, 128], BF16, 2)
        self.hid = Ring(S, st, "hidM" + sfx, [128, 32, 128], BF16, 1)
        self.rl = Ring(S, st, "rlM" + sfx, [128, 512], BF16, 3)
        self.tp = tp
        self.pp = pp


def phase_C(C):
    S = C.S
    T, NSEQ = C.T, C.NSEQ
    NT = T // 128
    cb = C.cb
    with contextlib.ExitStack() as st:
        w_out = S.sbuf(st, "w_out", [128, 8, D], BF16)
        load_w(S, w_out, C.ab_w_out[0], 8, D)
        w_up = S.sbuf(st, "w_up", [128, 8, DFF], BF16)
        load_w(S, w_up, C.mlp_w_up[0], 8, DFF)
        w_down = S.sbuf(st, "w_down", [128, 32, D], BF16)
        load_w(S, w_down, C.mlp_w_down[0], 32, D)
        gbf = S.sbuf(st, "gbfC", [128, D], F32)
        bload(S, gbf.v(), C.norm_ffn_g[0:1, :], D)
        tp = Ring(S, st, "tpC", [128, 1024], BF16, 2, psum=True)
        pp = Ring(S, st, "ppC", [128, 512], F32, 4, psum=True)
        R = MlpRings(S, st, tp, pp)
        r_x = Ring(S, st, "xC", [128, D], F32, 2)
        r_y = Ring(S, st, "yC", [128, D], BF16, 2)
        r_yT = Ring(S, st, "yTC", [128, 8, 128], BF16, 2)
        r_h1 = Ring(S, st, "h1C", [128, D], F32, 1)
        r_o = Ring(S, st, "oC", [128, D], F32, 1)
        for s in range(NSEQ):
            for i in range(NT):
                r0 = s * T + i * 128
                xt = r_x.next()
                S.dma("sp", xt.v(), dv(C.x[r0:r0 + 128, :]))
                y = r_y.next()
                S.dma("sp", y[:, 0:512], dv(C.ya[r0:r0 + 128, :]))
                S.dma("sp", y[:, 512:1024], dv(C.yb[r0:r0 + 128, :]))
                pT = tp.next()
                for k in range(8):
                    S.tr(pT[:, k * 128:(k + 1) * 128], y[:, k * 128:(k + 1) * 128], cblk(cb, C_ID))
                yT = r_yT.next()
                S.copy("act", yT.v().rr("p k t -> p (k t)"), pT.v())
                h1 = r_h1.next()
                for n in range(2):
                    ps = pp.next()
                    for k in range(8):
                        S.mm(ps.v(), yT[:, k, :], w_out[:, k, n * 512:(n + 1) * 512], start=(k == 0), stop=(k == 7))
                    S.tt("dve", h1[:, n * 512:(n + 1) * 512], ps.v(), xt[:, n * 512:(n + 1) * 512], ALU.add)
                o = r_o.next()
                mlp_tile(C, R, h1, gbf, w_up, w_down, o)
                S.dma("sp", dv(C.h2[r0:r0 + 128, :]), o.v())
        S.barrier()


import math


def bcv(v, shape):
    return V(v.ap.unsqueeze(2).broadcast_to(list(shape)), v.bufs)


def phase_D(C):
    S = C.S
    T, NSEQ = C.T, C.NSEQ
    NT = T // 128
    cf, cb = C.cf, C.cb
    A = C.aps
    ID = cblk(cb, C_ID)
    with contextlib.ExitStack() as st:
        w_r = S.sbuf(st, "w_r", [128, 8, D], BF16)
        w_k = S.sbuf(st, "w_k", [128, 8, D], BF16)
        w_v = S.sbuf(st, "w_v", [128, 8, D], BF16)
        load_w(S, w_r, A["rwkv_w_rkv"][0, 0], 8, D)
        load_w(S, w_k, A["rwkv_w_rkv"][0, 1], 8, D)
        load_w(S, w_v, A["rwkv_w_rkv"][0, 2], 8, D)
        w1 = S.sbuf(st, "w1", [128, 8, 64], BF16)
        a1 = S.sbuf(st, "a1", [128, 8, 64], BF16)
        g1 = S.sbuf(st, "g1", [128, 8, 128], BF16)
        load_w(S, w1, A["rwkv_w1"][0], 8, 64)
        load_w(S, a1, A["rwkv_a1"][0], 8, 64)
        load_w(S, g1, A["rwkv_g1"][0], 8, 128)
        w2 = S.sbuf(st, "w2", [128, D], BF16)
        a2 = S.sbuf(st, "a2", [128, D], BF16)
        g2 = S.sbuf(st, "g2", [128, D], BF16)
        S.dma("pool", w2[0:64, :], dv(A["rwkv_w2"][0]))
        S.dma("pool", a2[0:64, :], dv(A["rwkv_a2"][0]))
        S.dma("pool", g2.v(), dv(A["rwkv_g2"][0]))
        bc_ = {}
        for nm, ap in (("gmix", C.norm_mix_g[1:2, :]), ("w0b", A["rwkv_w0"][0:1, :]), ("a0b", A["rwkv_a0"][0:1, :]),
                       ("k_kb", A["rwkv_k_k"][0:1, :]), ("k_ab", A["rwkv_k_a"][0:1, :]),
                       ("r_kb", A["rwkv_r_k"].rearrange("a h n -> a (h n)"))):
            t_ = S.sbuf(st, nm, [128, D], F32)
            bload(S, t_.v(), ap, D)
            bc_[nm] = t_
        gmix, w0b, a0b, k_kb, k_ab, r_kb = (bc_[n] for n in ("gmix", "w0b", "a0b", "k_kb", "k_ab", "r_kb"))
        tp = Ring(S, st, "tpD", [128, 1024], BF16, 2, psum=True)
        pp = Ring(S, st, "ppD", [128, 512], F32, 2, psum=True)
        gp = Ring(S, st, "gpD", [128, 512], F32, 4, psum=True)
        mu_rows = S.sbuf(st, "mu_rows", [128, D], F32)
        S.dma("sp", mu_rows[0:6, :], dv(A["rwkv_mu"][0]))
        mu_kc = S.sbuf(st, "mu_kc", [128, 8, 6], F32)
        mps = gp.next()
        for k in range(8):
            S.tr(mps[:, k * 6:(k + 1) * 6], mu_rows[0:6, k * 128:(k + 1) * 128], cblk(cf, C_ID)[0:6, 0:6])
        S.copy("dve", mu_kc.v().rr("p k m -> p (k m)"), mps[:, 0:48])

        r_x = Ring(S, st, "xD", [128, D], F32, 2)
        r_junk = Ring(S, st, "junkD", [128, D], BF16, 1)
        r_st = Ring(S, st, "stD", [128, 16], F32, 6)
        r_f = Ring(S, st, "fD", [128, D], F32, 10)
        r_e = Ring(S, st, "eD", [128, 512], F32, 4)
        r_b = Ring(S, st, "bD", [128, D], BF16, 8)
        r_mix = Ring(S, st, "mixD", [128, 8, 128], BF16, 6)
        r_o = Ring(S, st, "oD", [128, D], BF16, 8)
        r_s = Ring(S, st, "sD", [128, 128], BF16, 4)
        NEG = -math.exp(-0.5)

        for s in range(NSEQ):
            for i in range(NT):
                r0 = s * T + i * 128
                xc = r_x.next()
                S.dma("sp", xc.v(), dv(C.out[r0:r0 + 128, :]))
                xp = r_x.next()
                if i == 0:
                    S.memset("pool", xp[0:1, :], 0.0)
                    S.dma("sp", xp[1:128, :], dv(C.out[r0:r0 + 127, :]))
                else:
                    S.dma("sp", xp.v(), dv(C.out[r0 - 1:r0 + 127, :]))
                hc = r_f.next()
                hp = r_f.next()
                for xt_, ht_ in ((xc, hc), (xp, hp)):
                    junk = r_junk.next()
                    sv = r_st.next()
                    S.act(junk.v(), xt_.v(), AF.Square, accum_out=sv[:, 0:1])
                    rms_rstd(S, C, sv[:, 0:1], sv[:, 1:2], D, EPS)
                    S.stt(ht_.v(), xt_.v(), sv[:, 1:2], gmix.v(), ALU.mult, ALU.mult)
                hcb = r_b.next()
                xxb = r_b.next()
                S.copy("act", hcb.v(), hc.v())
                S.tt("pool", xxb.v(), hp.v(), hc.v(), ALU.subtract)
                pH = tp.next()
                pX = tp.next()
                for k in range(8):
                    S.tr(pH[:, k * 128:(k + 1) * 128], hcb[:, k * 128:(k + 1) * 128], ID)
                for k in range(8):
                    S.tr(pX[:, k * 128:(k + 1) * 128], xxb[:, k * 128:(k + 1) * 128], ID)
                hcT = r_b.next()
                xxT = r_b.next()
                S.copy("act", hcT.v(), pH.v())
                S.copy("dve", xxT.v(), pX.v())
                mixT = []
                for m in range(6):
                    mt = r_mix.next()
                    e = "pool" if m % 2 == 0 else "dve"
                    S.tt(e, mt.v(), xxT.v().rr("p (k t) -> p k t", k=8), bcv(mu_kc[:, :, m], [128, 8, 128]), ALU.mult)
                    S.tt(e, mt.v(), mt.v(), hcT.v().rr("p (k t) -> p k t", k=8), ALU.add)
                    mixT.append(mt)
                xr, xw, xk, xv, xa, xg = mixT

                def proj(w, xT, dst):
                    for n in range(2):
                        ps = pp.next()
                        for k in range(8):
                            S.mm(ps.v(), xT[:, k, :], w[:, k, n * 512:(n + 1) * 512], start=(k == 0), stop=(k == 7))
                        S.copy("act", dst[:, n * 512:(n + 1) * 512], ps.v())
                rf = r_f.next()
                proj(w_r, xr, rf)
                kf = r_f.next()
                proj(w_k, xk, kf)
                vb = r_b.next()
                proj(w_v, xv, vb)
                S.dma("sp", dv(C.sV[r0:r0 + 128, :]), vb.v())

                def lora(w_a, xT, w_b, nj, func, bias_t, dst, final_func):
                    lp = gp.next()
                    for k in range(8):
                        S.mm(lp[0:nj, 0:128], w_a[:, k, :], xT[:, k, :], start=(k == 0), stop=(k == 7))
                    th = r_s.next()
                    S.act(th[0:nj, :], lp[0:nj, 0:128], func)
                    for n in range(2):
                        ps = pp.next()
                        S.mm(ps.v(), th[0:nj, :], w_b[0:nj, n * 512:(n + 1) * 512])
                        if bias_t is not None:
                            S.tt("dve", dst[:, n * 512:(n + 1) * 512], ps.v(), bias_t[:, n * 512:(n + 1) * 512], ALU.add)
                        else:
                            S.copy("act", dst[:, n * 512:(n + 1) * 512], ps.v())
                    if final_func is not None:
                        S.act(dst.v(), dst.v(), final_func)
                lw = r_f.next()
                lora(w1, xw, w2, 64, AF.Tanh, w0b, lw, AF.Sigmoid)
                S.ts("pool", lw.v(), lw.v(), NEG, ALU.mult, 0.0, ALU.add)
                af = r_f.next()
                lora(a1, xa, a2, 64, AF.Copy, a0b, af, AF.Sigmoid)
                gb_ = r_b.next()
                lora(g1, xg, g2, 128, AF.Sigmoid, None, gb_, None)
                S.dma("sp", dv(C.sG[r0:r0 + 128, :]), gb_.v())

                kk = r_f.next()
                S.tt("dve", kk.v(), kf.v(), k_kb.v(), ALU.mult)
                sq = r_f.next()
                S.tt("pool", sq.v(), kk.v(), kk.v(), ALU.mult)
                sv = r_st.next()
                S.reduce(sv.v(), sq.v().rr("p (h d) -> p h d", h=16))
                S.ts("dve", sv.v(), sv.v(), 1e-24, ALU.max)
                S.act(sv.v(), sv.v(), AF.Sqrt)
                S.recip(sv.v(), sv.v())
                S.tt("dve", kk.v().rr("p (h d) -> p h d", h=16), kk.v().rr("p (h d) -> p h d", h=16),
                     bcv(sv.v(), [128, 16, 64]), ALU.mult)
                S.stt(sq.v(), af.v(), -1.0, k_ab.v(), ALU.add, ALU.mult)
                S.stt(kf.v(), sq.v(), 1.0, kf.v(), ALU.add, ALU.mult)
                t2 = r_f.next()
                S.tt("pool", t2.v(), rf.v(), kf.v(), ALU.mult)
                S.tt("pool", t2.v(), t2.v(), r_kb.v(), ALU.mult)
                rkv = r_st.next()
                S.reduce(rkv.v(), t2.v().rr("p (h d) -> p h d", h=16))
                S.dma("sp", dv(C.rk[r0:r0 + 128, :]), rkv.v())
                S.tt("dve", t2.v(), kk.v(), af.v(), ALU.mult)
                Rt = r_o.next()
                Kt = r_o.next()
                Bt = r_o.next()
                At = r_o.next()
                for n in range(2):
                    hs = slice(n * 512, (n + 1) * 512)
                    cw = gp.next()
                    cwx = gp.next()
                    S.mm(cw.v(), cblk(cf, C_LE), lw[:, hs])
                    S.mm(cwx.v(), cblk(cf, C_LT), lw[:, hs])
                    ep = r_e.next()
                    S.act(ep.v(), cw.v(), AF.Exp)
                    en = r_e.next()
                    S.act(en.v(), cw.v(), AF.Exp, scale=-1.0)
                    epx = r_e.next()
                    S.act(epx.v(), cwx.v(), AF.Exp)
                    S.tt("dve", Rt[:, hs], rf[:, hs], ep.v(), ALU.mult)
                    S.tt("pool", Kt[:, hs], kf[:, hs], en.v(), ALU.mult)
                    S.tt("dve", Bt[:, hs], t2[:, hs], en.v(), ALU.mult)
                    S.stt(At[:, hs], kk[:, hs], -1.0, epx.v(), ALU.mult, ALU.mult)
                S.dma("sp", dv(C.sR[r0:r0 + 128, :]), Rt.v())
                S.dma("sp", dv(C.sK[r0:r0 + 128, :]), Kt.v())
                S.dma("sp", dv(C.sB[r0:r0 + 128, :]), Bt.v())
                S.dma("sp", dv(C.sA[r0:r0 + 128, :]), At.v())
                pcp = gp.next()
                for p in range(8):
                    S.mm(pcp[:, p * 2:(p + 1) * 2], lw[:, p * 128:(p + 1) * 128], cblk(cf, C_ONES)[:, 0:2])
                pct = r_st.next()
                S.act(pct.v(), pcp[:, 0:16], AF.Exp)
                S.dma("sp", dv(C.pcs[s * NT + i]), pct.v())
        S.barrier()


def phase_E(C):
    S = C.S
    T, NSEQ = C.T, C.NSEQ
    NT = T // 128
    cf, cb = C.cf, C.cb
    A = C.aps
    ID = cblk(cb, C_ID)
    with contextlib.ExitStack() as st:
        w_o = S.sbuf(st, "w_o", [128, 8, D], BF16)
        load_w(S, w_o, A["rwkv_w_o"][0], 8, D)
        lnxg = S.sbuf(st, "lnxg", [128, D], F32)
        lnxb = S.sbuf(st, "lnxb", [128, D], F32)
        bload(S, lnxg.v(), A["rwkv_lnx_g"][0:1, :], D)
        bload(S, lnxb.v(), A["rwkv_lnx_b"][0:1, :], D)
        mask4 = S.sbuf(st, "mask4", [128, 4, 128], F32)
        gt4 = S.sbuf(st, "gt4", [128, 4, 128], F32)
        id4 = S.sbuf(st, "id4", [128, 4, 128], BF16)
        for j in range(4):
            S.copy("pool", mask4[:, j, :], cblk(cf, C_LT if j % 2 == 0 else C_LE))
            S.copy("pool", gt4[:, j, :], cblk(cf, C_GT))
            S.copy("pool", id4[:, j, :], cblk(cb, C_ID))
        BD = cblk(cf, C_BD)
        Mf = S.sbuf(st, "Mf", [128, 8, 128], F32)
        Mb = S.sbuf(st, "Mb", [128, 8, 128], BF16)
        Apad = S.sbuf(st, "Apad", [128, 16, 128], BF16)
        S.memset("pool", Apad.v(), 0.0)
        AMall = S.sbuf(st, "AMall", [128, 16, 4, 128], BF16)
        r_in = Ring(S, st, "inE", [128, D], BF16, 12)
        r_x = Ring(S, st, "xE", [128, D], F32, 2)
        r_sm = Ring(S, st, "smE", [128, 16], F32, 8)
        r_art = Ring(S, st, "artE", [128, 8, 2, 128], BF16, 1)
        r_t = Ring(S, st, "tE", [128, 8, 128], BF16, 2)
        r_pq = Ring(S, st, "pqE", [128, 4, 128], BF16, 16)
        r_tt = Ring(S, st, "ttE", [128, 4, 128], BF16, 10)
        r_big = Ring(S, st, "bigE", [128, D], BF16, 5)
        r_u = Ring(S, st, "uE", [128, 128], BF16, 4)
        r_tmp = Ring(S, st, "tmpE", [128, 128], F32, 4)
        r_f = Ring(S, st, "fE", [128, D], F32, 3)
        r_h = Ring(S, st, "hE", [128, D], F32, 2)
        tp = Ring(S, st, "tpE", [128, 1024], BF16, 2, psum=True)
        gp = Ring(S, st, "gpE", [128, 512], F32, 6, psum=True)

        for s in range(NSEQ):
            S.memset("pool", Mf.v(), 0.0)
            S.memset("pool", Mb.v(), 0.0)
            for i in range(NT):
                r0 = s * T + i * 128
                ins = {}
                for nm, src in (("R", C.sR), ("K", C.sK), ("B", C.sB), ("A", C.sA), ("V", C.sV), ("G", C.sG)):
                    t_ = r_in.next()
                    S.dma("sp", t_.v(), dv(src[r0:r0 + 128, :]))
                    ins[nm] = t_
                Rt, Kt, Bt, At, Vb, Gb = (ins[n] for n in "RKBAVG")
                xc = r_x.next()
                S.dma("sp", xc.v(), dv(C.out[r0:r0 + 128, :]))
                rk = r_sm.next()
                S.dma("sp", rk.v(), dv(C.rk[r0:r0 + 128, :]))
                pc = r_sm.next()
                S.dma("sp", pc.v(), dv(C.pcs[s * NT + i]))
                ART = r_art.next()
                BtT = r_t.next()
                KtT = r_t.next()
                for src, dstv, eng in ((At, ART[:, :, 0, :], "act"), (Rt, ART[:, :, 1, :], "dve"),
                                       (Bt, BtT.v(), "act"), (Kt, KtT.v(), "dve")):
                    pt_ = tp.next()
                    for p in range(8):
                        S.tr(pt_[:, p * 128:(p + 1) * 128], src[:, p * 128:(p + 1) * 128], ID)
                    S.copy(eng, dstv, pt_.v().rr("q (p t) -> q p t", p=8))
                Apv = Apad.v().rr("t (p hh) c -> t p hh c", hh=2)
                Atv = At.v().rr("t (p hh k) -> t p hh k", hh=2, k=64)
                S.copy("pool", Apv[:, :, 0, 0:64], Atv[:, :, 0, :])
                S.copy("pool", Apv[:, :, 1, 64:128], Atv[:, :, 1, :])
                P = [None] * 4
                Q = [None] * 4
                TT = [None] * 4
                for g in range(4):
                    npsl = [gp.next(), gp.next()]
                    for hl in range(4):
                        h = 4 * g + hl
                        p, pb = h // 2, (h % 2) * 64
                        aps = gp.next()
                        art2 = ART[pb:pb + 64, p, :, :].rr("k a t -> k (a t)")
                        S.mm(aps[:, 0:256], BtT[pb:pb + 64, p, :], art2)
                        S.mm(aps[:, 256:512], KtT[pb:pb + 64, p, :], art2)
                        S.tt("dve", AMall[:, h, :, :].rr("q a t -> q (a t)"), aps.v(),
                             mask4.v().rr("q a t -> q (a t)"), ALU.mult)
                        S.mm(npsl[h % 2][:, (hl // 2) * 128:(hl // 2 + 1) * 128], ART[pb:pb + 64, p, 0, :], BtT[pb:pb + 64, p, :])
                    P[g] = r_pq.next()
                    Pv = P[g].v().rr("q (a two) t -> q a two t", two=2)
                    for par in range(2):
                        S.tt("dve", Pv[:, :, par, :], npsl[par][:, 0:256].rr("q (a t) -> q a t", a=2), gt4[:, 0:2, :], ALU.mult)
                    Q[g] = AMall[:, 4 * g:4 * g + 4, 0, :]
                    TT[g] = r_tt.next()
                    S.tt("pool", TT[g].v(), Q[g], id4.v(), ALU.add)
                for j in range(1, 7):
                    for g in range(4):
                        pps = gp.next()
                        for hl in range(4):
                            S.mm(pps[:, hl * 128:(hl + 1) * 128], Q[g][:, hl, :], P[g][:, hl, :])
                        if j < 6:
                            qps = gp.next()
                            for hl in range(4):
                                S.mm(qps[:, hl * 128:(hl + 1) * 128], P[g][:, hl, :], Q[g][:, hl, :])
                        Pn = r_pq.next()
                        S.copy("act", Pn.v().rr("q a t -> q (a t)"), pps.v())
                        if j < 6:
                            Qn = r_pq.next()
                            S.copy("act", Qn.v().rr("q a t -> q (a t)"), qps.v())
                        tps = gp.next()
                        for hl in range(4):
                            S.mm(tps[:, hl * 128:(hl + 1) * 128], Pn[:, hl, :], TT[g][:, hl, :])
                        TTn = r_tt.next()
                        S.tt("dve", TTn.v().rr("q a t -> q (a t)"), tps.v(), TT[g].v().rr("q a t -> q (a t)"), ALU.add)
                        P[g] = Pn
                        if j < 6:
                            Q[g] = Qn.v()
                        TT[g] = TTn
                AVb = r_big.next()
                Wb = r_big.next()
                for half in range(2):
                    avp = gp.next()
                    for hl in range(8):
                        h = half * 8 + hl
                        S.mm(avp[:, hl * 64:(hl + 1) * 64], AMall[:, h, 2, :], Vb[:, h * 64:(h + 1) * 64])
                    S.copy("act", AVb[:, half * 512:(half + 1) * 512], avp.v())
                for half in range(2):
                    wp = gp.next()
                    for hl in range(8):
                        h = half * 8 + hl
                        S.mm(wp[:, hl * 64:(hl + 1) * 64], TT[h // 4][:, h % 4, :], AVb[:, h * 64:(h + 1) * 64])
                    S.copy("act", Wb[:, half * 512:(half + 1) * 512], wp.v())
                AhT = r_t.next()
                for half in range(2):
                    ahp = gp.next()
                    for pl in range(4):
                        p = half * 4 + pl
                        h0, h1 = 2 * p, 2 * p + 1
                        S.mm(ahp[:, pl * 128:(pl + 1) * 128], Apad[:, h0, :], TT[h0 // 4][:, h0 % 4, :], start=True, stop=False)
                        S.mm(ahp[:, pl * 128:(pl + 1) * 128], Apad[:, h1, :], TT[h1 // 4][:, h1 % 4, :], start=False, stop=True)
                    S.copy("dve", AhT[:, half * 4:(half + 1) * 4, :].rr("q a t -> q (a t)"), ahp.v())
                Yf = r_f.next()
                for p in range(8):
                    pcs_ = slice(p * 128, (p + 1) * 128)
                    ups = gp.next()
                    S.mm(ups[:, 0:128], AhT[:, p, :], Mb[:, p, :])
                    Ub = r_u.next()
                    S.tt("dve", Ub.v(), ups[:, 0:128], Wb[:, pcs_], ALU.add)
                    yps = gp.next()
                    S.mm(yps[:, 0:128], ART[:, p, 1, :], Mb[:, p, :], start=True, stop=False)
                    for hh in range(2):
                        h = 2 * p + hh
                        S.mm(yps[:, hh * 64:(hh + 1) * 64], AMall[:, h, 1, :], Ub[:, hh * 64:(hh + 1) * 64], start=False, stop=False)
                        S.mm(yps[:, hh * 64:(hh + 1) * 64], AMall[:, h, 3, :], Vb[:, h * 64:(h + 1) * 64], start=False, stop=(hh == 1))
                    S.copy("act", Yf[:, pcs_], yps[:, 0:128])
                    mps = gp.next()
                    S.mm(mps[:, 0:128], Bt[:, pcs_], Ub.v(), start=True, stop=False)
                    S.mm(mps[:, 0:128], Kt[:, pcs_], Vb[:, pcs_], start=False, stop=True)
                    tmp = r_tmp.next()
                    S.stt(tmp.v(), mps[:, 0:128], pc[:, 2 * p:2 * p + 1], BD, ALU.mult, ALU.mult)
                    S.stt(Mf[:, p, :], Mf[:, p, :], pc[:, 2 * p:2 * p + 1], tmp.v(), ALU.mult, ALU.add)
                    S.copy("act", Mb[:, p, :], Mf[:, p, :])
                Y3 = Yf.v().rr("q (h d) -> q h d", h=16)
                sm = r_sm.next()
                S.reduce(sm.v(), Y3)
                S.ts("dve", sm.v(), sm.v(), 1.0 / 64, ALU.mult)
                S.tt("dve", Y3, Y3, bcv(sm.v(), [128, 16, 64]), ALU.subtract)
                sq = r_f.next()
                S.tt("pool", sq.v(), Yf.v(), Yf.v(), ALU.mult)
                sm2 = r_sm.next()
                S.reduce(sm2.v(), sq.v().rr("q (h d) -> q h d", h=16))
                rms_rstd(S, C, sm2.v(), sm2.v(), 64, GN_EPS)
                S.tt("dve", Y3, Y3, bcv(sm2.v(), [128, 16, 64]), ALU.mult)
                S.tt("pool", Yf.v(), Yf.v(), lnxg.v(), ALU.mult)
                S.tt("pool", Yf.v(), Yf.v(), lnxb.v(), ALU.add)
                S.tt("dve", sq.v().rr("q (h d) -> q h d", h=16), Vb.v().rr("q (h d) -> q h d", h=16),
                     bcv(rk.v(), [128, 16, 64]), ALU.mult)
                S.tt("pool", Yf.v(), Yf.v(), sq.v(), ALU.add)
                yfin = r_big.next()
                S.tt("dve", yfin.v(), Yf.v(), Gb.v(), ALU.mult)
                pT = tp.next()
                for k in range(8):
                    S.tr(pT[:, k * 128:(k + 1) * 128], yfin[:, k * 128:(k + 1) * 128], ID)
                yT = r_t.next()
                S.copy("act", yT.v().rr("q k t -> q (k t)"), pT.v())
                h3 = r_h.next()
                for n in range(2):
                    ps = gp.next()
                    for k in range(8):
                        S.mm(ps.v(), yT[:, k, :], w_o[:, k, n * 512:(n + 1) * 512], start=(k == 0), stop=(k == 7))
                    S.tt("dve", h3[:, n * 512:(n + 1) * 512], ps.v(), xc[:, n * 512:(n + 1) * 512], ALU.add)
                S.dma("sp", dv(C.out[r0:r0 + 128, :]), h3.v())
        S.barrier()


def phase_F(C):
    S = C.S
    T, NSEQ = C.T, C.NSEQ
    NT = T // 128
    with contextlib.ExitStack() as st:
        w_up = S.sbuf(st, "w_upF", [128, 8, DFF], BF16)
        load_w(S, w_up, C.mlp_w_up[1], 8, DFF)
        w_down = S.sbuf(st, "w_downF", [128, 32, D], BF16)
        load_w(S, w_down, C.mlp_w_down[1], 32, D)
        gbf = S.sbuf(st, "gbfF", [128, D], F32)
        bload(S, gbf.v(), C.norm_ffn_g[1:2, :], D)
        tp = Ring(S, st, "tpF", [128, 1024], BF16, 2, psum=True)
        pp = Ring(S, st, "ppF", [128, 512], F32, 4, psum=True)
        R = MlpRings(S, st, tp, pp, "F")
        r_x = Ring(S, st, "xF", [128, D], F32, 3)
        r_o = Ring(S, st, "oF", [128, D], F32, 2)
        for s in range(NSEQ):
            for i in range(NT):
                r0 = s * T + i * 128
                xt = r_x.next()
                S.dma("sp", xt.v(), dv(C.out[r0:r0 + 128, :]))
                o = r_o.next()
                mlp_tile(C, R, xt, gbf, w_up, w_down, o)
                S.dma("sp", dv(C.out[r0:r0 + 128, :]), o.v())
        S.barrier()

INPUT_SPECS = [
    ("norm_mix_g", [2, D]), ("norm_ffn_g", [2, D]), ("ab_w_in", [1, D, ABIN]),
    ("hgrn_lower_bounds", [3, 512]), ("hgrn_norm_g", [1, 512]), ("fox_forget_bias", [1, 8]),
    ("fox_q_norm_g", [1, 64]), ("fox_k_norm_g", [1, 64]), ("ab_w_out", [1, D, D]),
    ("rwkv_mu", [1, 6, D]), ("rwkv_w_rkv", [1, 3, D, D]), ("rwkv_w0", [1, D]),
    ("rwkv_w1", [1, D, 64]), ("rwkv_w2", [1, 64, D]), ("rwkv_a0", [1, D]),
    ("rwkv_a1", [1, D, 64]), ("rwkv_a2", [1, 64, D]), ("rwkv_g1", [1, D, 128]),
    ("rwkv_g2", [1, 128, D]), ("rwkv_k_k", [1, D]), ("rwkv_k_a", [1, D]),
    ("rwkv_r_k", [1, 16, 64]), ("rwkv_lnx_g", [1, D]), ("rwkv_lnx_b", [1, D]),
    ("rwkv_w_o", [1, D, D]), ("mlp_w_up", [2, D, DFF]), ("mlp_w_down", [2, DFF, D]),
]


def build(T, NSEQ, upto="F"):
    nc = bass.Bass("TRN2", target_bir_lowering=False)
    C = Ctx()
    C.nc = nc
    C.T, C.NSEQ = T, NSEQ
    C.cut = 0
    NTOK = T * NSEQ
    C.x = nc.dram_tensor("x", [NTOK, D], F32, kind="ExternalInput").ap()
    aps = {}
    for name, shp in INPUT_SPECS:
        aps[name] = nc.dram_tensor(name, shp, F32, kind="ExternalInput").ap()
    C.norm_mix_g = aps["norm_mix_g"]
    C.norm_ffn_g = aps["norm_ffn_g"]
    C.ab_w_in = aps["ab_w_in"]
    C.hgrn_lb = aps["hgrn_lower_bounds"]
    C.hgrn_norm_g = aps["hgrn_norm_g"]
    C.fox_fb = aps["fox_forget_bias"]
    C.fox_q_g = aps["fox_q_norm_g"]
    C.fox_k_g = aps["fox_k_norm_g"]
    C.ab_w_out = aps["ab_w_out"]
    C.mlp_w_up = aps["mlp_w_up"]
    C.mlp_w_down = aps["mlp_w_down"]
    C.aps = aps
    cst = nc.dram_tensor("cst", [128, NCONST * 128], F32, kind="ExternalInput").ap()
    C.out = nc.dram_tensor("out", [NTOK, D], F32, kind="ExternalOutput").ap()

    def scratch(name, shp, dt):
        return nc.dram_tensor(name, shp, dt, kind="ExternalOutput").ap()
    sR = scratch("sR", [NTOK, D], BF16)
    sK = scratch("sK", [NTOK, D], BF16)
    sB = scratch("sB", [NTOK, D], BF16)
    C.sR, C.sK, C.sB = sR, sK, sB
    C.sA = scratch("sA", [NTOK, D], BF16)
    C.sV = scratch("sV", [NTOK, D], BF16)
    C.sG = scratch("sG", [NTOK, D], BF16)
    C.rk = scratch("rk", [NTOK, 16], F32)
    C.pcs = scratch("pcs", [NTOK // 128, 128, 16], F32)
    sRv = sR.rearrange("(x r) d -> x (r d)", x=2).rearrange("x (s a p t) -> x s a p t", s=NSEQ, a=4, p=128)
    C.fq = sRv[0]
    C.fk = sRv[1]
    C.fv = sK[:, 0:512]
    C.fg = sK[:, 512:1024]
    C.fc = scratch("fc", [NTOK, 8], F32)
    C.ya = sB[:, 0:512]
    C.yb = sB[:, 512:1024]
    C.h2 = C.out
    with contextlib.ExitStack() as st:
        S = Sched(nc, st)
        C.S = S
        cf = S.sbuf(st, "cstf", [128, NCONST * 128], F32)
        cbt = S.sbuf(st, "cstb", [128, NCONST * 128], BF16)
        S.dma("sp", cf.v(), dv(cst))
        S.copy("dve", cbt.v(), cf.v())
        C.cf, C.cb = cf, cbt
        phase_A(C)
        if upto != "A":
            phase_B(C)
            phase_C(C)
        if upto not in ("A", "C"):
            phase_D(C)
            if upto != "D":
                phase_E(C)
        if upto == "F":
            phase_F(C)
        S.final_wait("sp")
        C.ninst = S.ninst
    return nc, C


_CACHE = {}


def kernel(**inputs):
    x = np.ascontiguousarray(inputs["x"], dtype=np.float32)
    B, T, _ = x.shape
    NSEQ = B // NCORES
    key = (T, NSEQ)
    if key not in _CACHE:
        _CACHE[key] = build(T, NSEQ)[0]
    nc = _CACHE[key]
    cst = make_consts()
    shared = {name: np.ascontiguousarray(inputs[name], dtype=np.float32) for name, _ in INPUT_SPECS}
    shared["cst"] = cst
    in_maps = []
    for c in range(NCORES):
        m = dict(shared)
        m["x"] = x[c * NSEQ:(c + 1) * NSEQ].reshape(NSEQ * T, D)
        in_maps.append(m)
    res = run_bass_kernel_spmd(nc, in_maps, core_ids=list(range(NCORES)))
    outs = [np.asarray(r["out"]).reshape(NSEQ, T, D) for r in res.results]
    return np.concatenate(outs, axis=0).astype(np.float32)
```

```python
import numpy as np
import concourse.bass as bass
import concourse.mybir as mybir

F32 = mybir.dt.float32
BF16 = mybir.dt.bfloat16
AF = mybir.ActivationFunctionType
ALU = mybir.AluOpType
AX = mybir.AxisListType


class Buf:
    __slots__ = ("name", "lw", "rd")

    def __init__(self, name):
        self.name = name
        self.lw = None
        self.rd = {}


class V:
    __slots__ = ("ap", "bufs")

    def __init__(self, ap, bufs):
        self.ap = ap
        self.bufs = bufs

    def __getitem__(self, idx):
        return V(self.ap[idx], self.bufs)

    def rr(self, pat, **kw):
        return V(self.ap.rearrange(pat, **kw), self.bufs)

    def bc(self, shape):
        return V(self.ap.broadcast_to(shape), self.bufs)


class Tile:
    def __init__(self, t, name):
        self.t = t
        self.buf = Buf(name)

    def __getitem__(self, idx):
        return V(self.t[idx], (self.buf,))

    def v(self):
        return V(self.t[:], (self.buf,))

    def part(self, name):
        p = Tile.__new__(Tile)
        p.t = self.t
        p.buf = Buf(name)
        return p


class Sched:
    ENG = ("pe", "act", "dve", "pool", "sp")

    def __init__(self, nc, stack, n_dma_sems=24):
        self.nc = nc
        self.stack = stack
        self.eng = {"pe": nc.tensor, "act": nc.scalar, "dve": nc.vector,
                    "pool": nc.gpsimd, "sp": nc.sync}
        self.sem = {}
        self.cnt = {}
        for e in self.ENG:
            self.sem[e] = stack.enter_context(nc.semaphore("s_" + e))
            self.cnt[e] = 0
        self.dsem = [stack.enter_context(nc.semaphore("d%d" % i)) for i in range(n_dma_sems)]
        self.dcnt = [0] * n_dma_sems
        self.dnext = 0
        self.waited = {e: {} for e in self.ENG}
        self.ninst = 0

    def _semh(self, key):
        if isinstance(key, str):
            return self.sem[key]
        return self.dsem[key]

    def _wait(self, e, key, val):
        w = self.waited[e]
        if w.get(key, 0) >= val:
            return
        self.eng[e].wait_ge(self._semh(key), val)
        w[key] = val
        self.ninst += 1

    def _deps(self, e, reads, writes, skip_same_pe=True):
        need = {}

        def add(dep):
            if dep is None:
                return
            k, v = dep
            if need.get(k, 0) < v:
                need[k] = v
        for b in reads:
            add(b.lw)
        for b in writes:
            add(b.lw)
            for k, v in b.rd.items():
                add((k, v))
        for k, v in need.items():
            if e == "pe" and k == "pe":
                continue
            self._wait(e, k, v)

    def _mark(self, key, val, reads, writes):
        for b in reads:
            if b.rd.get(key, 0) < val:
                b.rd[key] = val
        for b in writes:
            b.lw = (key, val)
            b.rd = {}

    def op(self, e, fn, outs, ins):
        reads = [b for v in ins if v is not None for b in v.bufs]
        writes = [b for v in outs if v is not None for b in v.bufs]
        self._deps(e, reads, writes)
        self.cnt[e] += 1
        inst = fn()
        inst.then_inc(self.sem[e], 1)
        self._mark(e, self.cnt[e], reads, writes)
        self.ninst += 1
        return inst

    def dma(self, q, out, in_, **kw):
        reads = list(in_.bufs)
        writes = list(out.bufs)
        i = self.dnext
        self.dnext = (self.dnext + 1) % len(self.dsem)
        if self.dcnt[i] > 0:
            self._wait(q, i, self.dcnt[i])
        self._deps(q, reads, writes)
        self.dcnt[i] += 16
        self.eng[q].dma_start(out=out.ap, in_=in_.ap, **kw).then_inc(self.dsem[i], 16)
        self._mark(i, self.dcnt[i], reads, writes)
        self.ninst += 1

    def barrier(self):
        for e in self.ENG:
            for e2 in self.ENG:
                if e2 != e and self.cnt[e2] > 0:
                    self._wait(e, e2, self.cnt[e2])
            for i in range(len(self.dsem)):
                if self.dcnt[i] > 0:
                    self._wait(e, i, self.dcnt[i])

    def final_wait(self, e="sp"):
        for i in range(len(self.dsem)):
            if self.dcnt[i] > 0:
                self._wait(e, i, self.dcnt[i])
        for e2 in self.ENG:
            if e2 != e and self.cnt[e2] > 0:
                self._wait(e, e2, self.cnt[e2])

    def sbuf(self, stack, name, shape, dt):
        t = stack.enter_context(self.nc.sbuf_tensor(name, list(shape), dt))
        return Tile(t, name)

    def psum(self, stack, name, shape, dt):
        t = stack.enter_context(self.nc.psum_tensor(name, list(shape), dt))
        return Tile(t, name)

    def _pe_mode(self, lhsT):
        def rnd(n):
            return 32 if n <= 32 else (64 if n <= 64 else 128)
        shp = lhsT.ap.shape
        k = rnd(shp[0])
        m = 1
        for d in shp[1:]:
            m *= d
        mode = (k, rnd(m))
        if getattr(self, "_last_pe_mode", None) not in (None, mode):
            self.nc.tensor.drain()
            self.ninst += 1
        self._last_pe_mode = mode

    def mm(self, out, lhsT, rhs, start=True, stop=True, **kw):
        self._pe_mode(lhsT)
        return self.op("pe", lambda: self.nc.tensor.matmul(out.ap, lhsT.ap, rhs.ap, start=start, stop=stop, **kw),
                       [out], [lhsT, rhs])

    def tr(self, out, in_, ident):
        self._pe_mode(in_)
        return self.op("pe", lambda: self.nc.tensor.transpose(out.ap, in_.ap, ident.ap), [out], [in_, ident])

    def act(self, out, in_, func, bias=None, scale=None, accum_out=None):
        kw = {}
        ins = [in_]
        if bias is not None:
            if isinstance(bias, V):
                kw["bias"] = bias.ap
                ins.append(bias)
            else:
                kw["bias"] = bias
        if scale is not None:
            if isinstance(scale, V):
                kw["scale"] = scale.ap
                ins.append(scale)
            else:
                kw["scale"] = scale
        outs = [out]
        if accum_out is not None:
            kw["accum_out"] = accum_out.ap
            outs.append(accum_out)
        return self.op("act", lambda: self.nc.scalar.activation(out.ap, in_.ap, func, **kw), outs, ins)

    def _ve(self, e):
        return self.nc.vector if e == "dve" else self.nc.gpsimd

    def tt(self, e, out, in0, in1, op):
        return self.op(e, lambda: self._ve(e).tensor_tensor(out.ap, in0.ap, in1.ap, op), [out], [in0, in1])

    def ts(self, e, out, in0, s1, op0, s2=None, op1=None, accum_out=None):
        ins = [in0]
        a1 = s1
        if isinstance(s1, V):
            ins.append(s1)
            a1 = s1.ap
        a2 = s2
        if isinstance(s2, V):
            ins.append(s2)
            a2 = s2.ap
        kw = {}
        outs = [out]
        if op1 is not None:
            kw["op1"] = op1
        if accum_out is not None:
            kw["accum_out"] = accum_out.ap
            outs.append(accum_out)
        return self.op(e, lambda: self._ve(e).tensor_scalar(out.ap, in0.ap, a1, a2, op0, **kw), outs, ins)

    def stt(self, out, in0, scalar, in1, op0, op1):
        ins = [in0, in1]
        sc = scalar
        if isinstance(scalar, V):
            ins.append(scalar)
            sc = scalar.ap
        return self.op("dve", lambda: self.nc.vector.scalar_tensor_tensor(out.ap, in0.ap, sc, in1.ap, op0, op1),
                       [out], ins)

    def copy(self, e, out, in_):
        if e == "act":
            return self.op("act", lambda: self.nc.scalar.copy(out.ap, in_.ap), [out], [in_])
        return self.op(e, lambda: self._ve(e).tensor_copy(out.ap, in_.ap), [out], [in_])

    def memset(self, e, out, val):
        return self.op(e, lambda: self._ve(e).memset(out.ap, val), [out], [])

    def reduce(self, out, in_, op=None, axis=None):
        op = op or ALU.add
        axis = axis or AX.X
        return self.op("dve", lambda: self.nc.vector.tensor_reduce(out.ap, in_.ap, axis, op), [out], [in_])

    def recip(self, out, in_):
        return self.op("dve", lambda: self.nc.vector.reciprocal(out.ap, in_.ap), [out], [in_])


import contextlib
from concourse.bass_utils import run_bass_kernel_spmd

D = 1024
DFF = 4096
ABIN = 4104
EPS = 1e-6
GN_EPS = 64e-5
NCORES = 8


class Ring:
    def __init__(self, S, st, name, shape, dt, n, psum=False):
        mk = S.psum if psum else S.sbuf
        self.tiles = [mk(st, "%s%d" % (name, i), shape, dt) for i in range(n)]
        self.i = 0

    def next(self):
        t = self.tiles[self.i % len(self.tiles)]
        self.i += 1
        return t


def dv(ap):
    return V(ap, (Buf("d"),))


class Ctx:
    pass


C_ID, C_LE, C_LT, C_GT, C_LE64, C_BD, C_ONES, C_SEL0, C_SEL1 = range(9)
NCONST = 9


def make_consts():
    p = np.arange(128)[:, None]
    f = np.arange(128)[None, :]
    blocks = [
        (p == f), (p <= f), (p < f), (p > f),
        (p <= f) & (p // 64 == f // 64), (p // 64 == f // 64),
        np.ones((128, 128), bool), (p // 64 == 0) & (f >= 0), (p // 64 == 1) & (f >= 0),
    ]
    return np.concatenate([b.astype(np.float32) for b in blocks], axis=1)


def load_w(S, dst, src_ap, K, N, q="pool"):
    src = src_ap.rearrange("(k p) n -> p k n", p=128)
    for k in range(K):
        for c0 in range(0, N, 1024):
            c1 = min(N, c0 + 1024)
            S.dma(q, dst[:, k, c0:c1], dv(src[:, k, c0:c1]))


def bload(S, dstv, src_row_ap, n):
    S.dma("sp", dstv, dv(src_row_ap.broadcast_to([128, n])))


def cblk(t, i):
    return t[:, i * 128:(i + 1) * 128]


def rms_rstd(S, C, ss, rstd, n, eps):
    S.ts("dve", rstd, ss, 1.0 / n, ALU.mult, eps, ALU.add)
    S.act(rstd, rstd, AF.Ln)
    S.act(rstd, rstd, AF.Exp, scale=-0.5)


def drive(a, b):
    gens = [g for g in (a, b) if g is not None]
    while gens:
        for g in list(gens):
            try:
                next(g)
            except StopIteration:
                gens.remove(g)


def run_pipelined(tiles, front, back):
    prev = None
    for t in tiles:
        X = {}
        drive(prev, front(t, X))
        prev = back(t, X)
    drive(prev, None)


def bcv(v, shape):
    return V(v.ap.unsqueeze(2).broadcast_to(list(shape)), v.bufs)


def phase_A(C):
    S = C.S
    T, NSEQ = C.T, C.NSEQ
    NT = T // 128
    cf, cb = C.cf, C.cb
    with contextlib.ExitStack() as st:
        w_in = S.sbuf(st, "w_in", [128, 8, ABIN], BF16)
        load_w(S, w_in, C.ab_w_in[0], 8, ABIN)
        gb = S.sbuf(st, "gbA", [128, D], F32)
        bload(S, gb.v(), C.norm_mix_g[0:1, :], D)
        G = S.sbuf(st, "G", [128, 3, 512], F32)
        for l in range(3):
            bload(S, G[:, l, :], C.hgrn_lb[l:l + 1, :], 512)
        lbb = S.sbuf(st, "lbb", [128, 512], F32)
        oml = S.sbuf(st, "oml", [128, 512], F32)
        S.act(G.v(), G.v(), AF.Exp)
        S.tt("dve", lbb.v(), G[:, 0, :], G[:, 1, :], ALU.add)
        S.tt("dve", lbb.v(), lbb.v(), G[:, 2, :], ALU.add)
        S.recip(lbb.v(), lbb.v())
        S.tt("dve", lbb.v(), lbb.v(), G[:, 0, :], ALU.mult)
        S.ts("dve", oml.v(), lbb.v(), -1.0, ALU.mult, 1.0, ALU.add)
        hng = S.sbuf(st, "hng", [128, 512], F32)
        bload(S, hng.v(), C.hgrn_norm_g[0:1, :], 512)
        fqg = S.sbuf(st, "fqg", [128, 8, 64], F32)
        fkg = S.sbuf(st, "fkg", [128, 8, 64], F32)
        for h in range(8):
            bload(S, fqg[:, h, :], C.fox_q_g[0:1, :], 64)
            bload(S, fkg[:, h, :], C.fox_k_g[0:1, :], 64)
        fbb = S.sbuf(st, "fbb", [128, 8], F32)
        bload(S, fbb.v(), C.fox_fb[0:1, :], 8)
        m64x4 = S.sbuf(st, "m64x4", [128, 4, 128], F32)
        for h in range(4):
            S.copy("pool", m64x4[:, h, :], cblk(cf, C_LE64))
        Sf = S.sbuf(st, "Sf", [128, 512], F32)
        SbA = S.sbuf(st, "SbA", [128, 512], BF16)
        SbB = S.sbuf(st, "SbB", [128, 512], BF16)
        ctot = S.sbuf(st, "ctot", [128, 8], F32)
        r_x = Ring(S, st, "xA", [128, D], F32, 2)
        r_junk = Ring(S, st, "junkA", [128, D], BF16, 1)
        r_stF = Ring(S, st, "stAF", [128, 8], F32, 6)
        r_stB = Ring(S, st, "stAB", [128, 8], F32, 2)
        r_hn = Ring(S, st, "hnA", [128, D], BF16, 2)
        r_hnT = Ring(S, st, "hnTA", [128, 8, 128], BF16, 2)
        r_qs = Ring(S, st, "qsA", [128, 512], F32, 2)
        r_gl = Ring(S, st, "glA", [128, 512], F32, 2)
        r_kf = Ring(S, st, "kfA", [128, 512], F32, 2)
        r_sg = Ring(S, st, "sgA", [128, 512], F32, 2)
        r_vt = Ring(S, st, "vtA", [128, 512], BF16, 2)
        r_fF = Ring(S, st, "fAF", [128, 512], F32, 4)
        r_fB = Ring(S, st, "fAB", [128, 512], F32, 6)
        r_bF = Ring(S, st, "bAF", [128, 512], BF16, 4)
        r_bB = Ring(S, st, "bAB", [128, 512], BF16, 4)
        r_qtT = Ring(S, st, "qtT", [128, 4, 128], BF16, 2)
        r_ktT = Ring(S, st, "ktT", [128, 4, 128], BF16, 2)
        r_q0 = Ring(S, st, "qtT0", [128, 4, 128], BF16, 2)
        r_q1 = Ring(S, st, "qtT1", [128, 4, 128], BF16, 2)
        for t in r_q0.tiles + r_q1.tiles:
            S.memset("pool", t.v(), 0.0)
        r_fT = Ring(S, st, "fT", [128, 4, 128], BF16, 4)
        tpF = Ring(S, st, "tpAF", [128, 1024], BF16, 1, psum=True)
        ppF = Ring(S, st, "ppAF", [128, 512], F32, 2, psum=True)
        gpF = Ring(S, st, "gpAF", [128, 512], F32, 1, psum=True)
        tpB = Ring(S, st, "tpAB", [128, 1024], BF16, 2, psum=True)
        gpB = Ring(S, st, "gpAB", [128, 512], F32, 2, psum=True)
        ID = cblk(cb, C_ID)

        def front(tile, X):
            s, i = tile
            r0 = s * T + i * 128
            if i == 0:
                S.memset("pool", ctot.v(), 0.0)
            xt = r_x.next()
            S.dma("sp", xt.v(), dv(C.x[r0:r0 + 128, :]))
            junk = r_junk.next()
            sv = r_stF.next()
            S.act(junk.v(), xt.v(), AF.Square, accum_out=sv[:, 0:1])
            rms_rstd(S, C, sv[:, 0:1], sv[:, 1:2], D, EPS)
            hn = r_hn.next()
            S.stt(hn.v(), xt.v(), sv[:, 1:2], gb.v(), ALU.mult, ALU.mult)
            yield
            pT = tpF.next()
            for k in range(8):
                S.tr(pT[:, k * 128:(k + 1) * 128], hn[:, k * 128:(k + 1) * 128], ID)
            hnT = r_hnT.next()
            S.copy("act", hnT.v().rr("p k t -> p (k t)"), pT.v())
            yield

            def proj(og, n=512):
                ps = ppF.next()
                for k in range(8):
                    S.mm(ps[:, 0:n], hnT[:, k, :], w_in[:, k, og * 512:og * 512 + n], start=(k == 0), stop=(k == 7))
                return ps
            ps = proj(0)
            qs = r_qs.next()
            S.act(qs.v(), ps.v(), AF.Sigmoid)
            S.tt("dve", qs.v(), ps.v(), qs.v(), ALU.mult)
            yield
            ps = proj(1)
            f = r_fF.next()
            S.act(f.v(), ps.v(), AF.Sigmoid)
            S.tt("dve", f.v(), f.v(), oml.v(), ALU.mult)
            S.tt("dve", f.v(), f.v(), lbb.v(), ALU.add)
            yield
            ps = proj(3)
            sgate = r_sg.next()
            S.act(sgate.v(), ps.v(), AF.Sigmoid)
            S.tt("dve", sgate.v(), ps.v(), sgate.v(), ALU.mult)
            yield
            ps = proj(7)
            fgt = r_bF.next()
            S.act(fgt.v(), ps.v(), AF.Sigmoid)
            S.dma("sp", dv(C.fg[r0:r0 + 128, :]), fgt.v())
            ps = proj(8, 8)
            lf = r_stF.next()
            S.tt("dve", lf.v(), ps[:, 0:8], fbb.v(), ALU.add)
            S.act(lf.v(), lf.v(), AF.Sigmoid)
            S.act(lf.v(), lf.v(), AF.Ln)
            gl = r_gl.next()
            S.act(gl.v(), f.v(), AF.Ln)
            kf = r_kf.next()
            S.ts("pool", kf.v(), f.v(), -1.0, ALU.mult, 1.0, ALU.add)
            yield
            cps = gpF.next()
            S.mm(cps[:, 0:8], cblk(cf, C_LE), lf.v())
            S.mm(cps[:, 8:16], cblk(cf, C_ONES), lf.v())
            cs = r_stF.next()
            S.tt("dve", cs.v(), cps[:, 0:8], ctot.v(), ALU.add)
            S.dma("sp", dv(C.fc[r0:r0 + 128, :]), cs.v())
            S.tt("dve", ctot.v(), ctot.v(), cps[:, 8:16], ALU.add)
            ps = proj(2)
            vt = r_vt.next()
            S.copy("act", vt.v(), ps.v())
            X.update(qs=qs, gl=gl, kf=kf, sgate=sgate, vt=vt, r0=r0, first=(i == 0))
            yield

            for og, gtile, dst_dram in ((4, fqg, C.fq), (5, fkg, C.fk)):
                ps = proj(og)
                bq = r_fF.next()
                S.copy("act", bq.v(), ps.v())
                sq = r_fF.next()
                S.tt("pool", sq.v(), bq.v(), bq.v(), ALU.mult)
                sv2 = r_stF.next()
                S.reduce(sv2.v(), sq.v().rr("p (h d) -> p h d", h=8))
                rms_rstd(S, C, sv2.v(), sv2.v(), 64, EPS)
                S.tt("dve", bq.v().rr("p (h d) -> p h d", h=8), bq.v().rr("p (h d) -> p h d", h=8),
                     bcv(sv2.v(), [128, 8, 64]), ALU.mult)
                qn = r_bF.next()
                S.tt("pool", qn.v(), bq.v(), gtile.v().rr("p h d -> p (h d)"), ALU.mult)
                yield
                pq = tpF.next()
                for hp in range(4):
                    S.tr(pq[:, hp * 128:(hp + 1) * 128], qn[:, hp * 128:(hp + 1) * 128], ID)
                qT = r_fT.next()
                S.copy("dve", qT.v().rr("p a t -> p (a t)"), pq[:, 0:512])
                S.dma("sp", dv(dst_dram[s, :, :, i * 128:(i + 1) * 128].rearrange("a p t -> p a t")), qT.v())
                yield
            ps = proj(6)
            fvt = r_bF.next()
            S.copy("act", fvt.v(), ps.v())
            S.dma("sp", dv(C.fv[r0:r0 + 128, :]), fvt.v())

        def back(tile, X):
            qs, gl, kf, sgate, vt, r0 = (X[k] for k in ("qs", "gl", "kf", "sgate", "vt", "r0"))
            if X["first"]:
                S.memset("pool", Sf.v(), 0.0)
                S.memset("pool", SbA.v(), 0.0)
            bps = gpB.next()
            S.mm(bps.v(), cblk(cf, C_LE64), gl.v())
            eb = r_fB.next()
            S.act(eb.v(), bps.v(), AF.Exp)
            enb = r_fB.next()
            S.act(enb.v(), bps.v(), AF.Exp, scale=-1.0)
            qt = r_bB.next()
            S.tt("dve", qt.v(), qs.v(), eb.v(), ALU.mult)
            kt = r_bB.next()
            S.tt("pool", kt.v(), kf.v(), enb.v(), ALU.mult)
            yield
            ebl = []
            for c in range(2):
                eps_ = gpB.next()
                for h in range(4):
                    S.mm(eps_[:, h * 128:(h + 1) * 128], gl[:, h * 128:(h + 1) * 128], cblk(cf, C_SEL0 + c))
                e_ = r_fB.next()
                S.act(e_.v(), eps_.v(), AF.Exp)
                ebl.append(e_)
            yield
            pqq = tpB.next()
            pkk = tpB.next()
            for h in range(4):
                S.tr(pqq[:, h * 128:(h + 1) * 128], qt[:, h * 128:(h + 1) * 128], ID)
            for h in range(4):
                S.tr(pkk[:, h * 128:(h + 1) * 128], kt[:, h * 128:(h + 1) * 128], ID)
            qtT = r_qtT.next()
            ktT = r_ktT.next()
            q0 = r_q0.next()
            q1 = r_q1.next()
            S.copy("act", qtT.v().rr("p h t -> p (h t)"), pqq[:, 0:512])
            S.copy("dve", ktT.v().rr("p h t -> p (h t)"), pkk[:, 0:512])
            pq3 = pqq[:, 0:512].rr("p (h t) -> p h t", h=4)
            S.copy("act", q0[:, :, 0:64], pq3[:, :, 0:64])
            S.copy("act", q1[:, :, 64:128], pq3[:, :, 64:128])
            yield
            scp = gpB.next()
            for h in range(4):
                S.mm(scp[:, h * 128:(h + 1) * 128], ktT[:, h, :], qtT[:, h, :])
            scT = r_bB.next()
            S.tt("dve", scT.v(), scp.v(), m64x4.v().rr("p h t -> p (h t)"), ALU.mult)
            yield

            def state_update(c, dstb):
                ups = gpB.next()
                for h in range(4):
                    S.mm(ups[:, h * 128:(h + 1) * 128], kt[c * 64:(c + 1) * 64, h * 128:(h + 1) * 128],
                         vt[c * 64:(c + 1) * 64, h * 128:(h + 1) * 128])
                S.tt("dve", Sf.v(), Sf.v(), ups.v(), ALU.add)
                S.tt("pool", Sf.v(), Sf.v(), ebl[c].v(), ALU.mult)
                S.copy("act", dstb.v(), Sf.v())
            state_update(0, SbB)
            yield
            ops = gpB.next()
            for h in range(4):
                hs = slice(h * 128, (h + 1) * 128)
                S.mm(ops[:, hs], scT[:, hs], vt[:, hs], start=True, stop=False)
                S.mm(ops[:, hs], q0[:, h, :], SbA[:, hs], start=False, stop=False)
                S.mm(ops[:, hs], q1[:, h, :], SbB[:, hs], start=False, stop=True)
            yield
            state_update(1, SbA)
            osb = r_fB.next()
            S.copy("act", osb.v(), ops.v())
            sq = r_fB.next()
            S.tt("pool", sq.v(), osb.v(), osb.v(), ALU.mult)
            yield
            sv3 = r_stB.next()
            S.reduce(sv3[:, 0:4], sq.v().rr("p (h d) -> p h d", h=4))
            rms_rstd(S, C, sv3[:, 0:4], sv3[:, 0:4], 128, EPS)
            S.tt("dve", osb.v().rr("p (h d) -> p h d", h=4), osb.v().rr("p (h d) -> p h d", h=4),
                 bcv(sv3[:, 0:4], [128, 4, 128]), ALU.mult)
            S.tt("pool", osb.v(), osb.v(), hng.v(), ALU.mult)
            yat = r_bB.next()
            S.tt("dve", yat.v(), osb.v(), sgate.v(), ALU.mult)
            S.dma("sp", dv(C.ya[r0:r0 + 128, :]), yat.v())

        tiles = [(s, i) for s in range(NSEQ) for i in range(NT)]
        run_pipelined(tiles, front, back)
        S.barrier()


def phase_B(C):
    S = C.S
    T, NSEQ = C.T, C.NSEQ
    NT = T // 128
    QG = min(4, NT)
    NG = NT // QG
    cf, cb = C.cf, C.cb
    with contextlib.ExitStack() as st:
        r_kT = Ring(S, st, "kTB", [128, 2, T], BF16, 2)
        for t in r_kT.tiles:
            S.memset("pool", t[64:128, 0, :], 0.0)
            S.memset("pool", t[0:64, 1, :], 0.0)
        r_qT = Ring(S, st, "qTB", [128, T], BF16, 2)
        r_v = Ring(S, st, "vB", [128, NT, 2, 65], BF16, 2)
        for t in r_v.tiles:
            S.memset("pool", t.v(), 1.0)
        r_g = Ring(S, st, "gB", [128, NT, 128], BF16, 2)
        r_y = Ring(S, st, "yB", [128, NT, 128], BF16, 2)
        cT = S.sbuf(st, "cT", [128, NT, 8], F32)
        r_cref = Ring(S, st, "cref", [128, 8], F32, 2)
        r_bias = Ring(S, st, "biasB", [128, NT], F32, 3)
        r_p = Ring(S, st, "pB", [128, QG * 128], BF16, 6)
        r_rc = Ring(S, st, "rcB", [128, QG], F32, 2)
        lemask = S.sbuf(st, "lemaskB", [128, 128], BF16)
        S.copy("pool", lemask.v(), cblk(cf, C_LE))
        stp = Ring(S, st, "stB", [128, 512], F32, 4, psum=True)
        accp = Ring(S, st, "accB", [128, 512], F32, 2, psum=True)
        for s in range(NSEQ):
            S.dma("sp", cT.v(), dv(C.fc[s * T:(s + 1) * T, :].rearrange("(b p) h -> p b h", p=128)))
            for hp in range(4):
                kT = r_kT.next()
                qT = r_qT.next()
                vt = r_v.next()
                gt = r_g.next()
                yt = r_y.next()
                S.dma("sp", kT[0:64, 0, :], dv(C.fk[s, hp, 0:64, :]))
                S.dma("sp", kT[64:128, 1, :], dv(C.fk[s, hp, 64:128, :]))
                S.dma("sp", qT.v(), dv(C.fq[s, hp, :, :]))
                for hh_ in range(2):
                    c0_ = hp * 128 + hh_ * 64
                    S.dma("sp", vt[:, :, hh_, 0:64],
                          dv(C.fv[s * T:(s + 1) * T, c0_:c0_ + 64].rearrange("(b p) d -> p b d", p=128)))
                S.dma("sp", gt.v(),
                      dv(C.fg[s * T:(s + 1) * T, hp * 128:(hp + 1) * 128].rearrange("(b p) d -> p b d", p=128)))
                items = []
                for qg in range(NG):
                    for hh in range(2):
                        for j in range(qg * QG + QG):
                            items.append((qg, hh, j))
                LOOK = 2
                stq = {}

                def emit_st(idx):
                    qg_, hh_, j_ = items[idx]
                    i0_ = qg_ * QG
                    pb_ = hh_ * 64
                    ilo_ = max(j_, i0_)
                    ncol_ = (i0_ + QG - ilo_) * 128
                    sp_ = stp.next()
                    S.mm(sp_[:, 0:ncol_], kT[:, hh_, j_ * 128:(j_ + 1) * 128],
                         qT[:, ilo_ * 128:(i0_ + QG) * 128])
                    stq[idx] = (sp_, ncol_, ilo_)
                for idx in range(min(LOOK, len(items))):
                    emit_st(idx)
                cref = None
                for idx, (qg, hh, j) in enumerate(items):
                    if idx + LOOK < len(items):
                        emit_st(idx + LOOK)
                    i0 = qg * QG
                    nj = i0 + QG
                    h = hp * 2 + hh
                    pb = hh * 64
                    if j == 0:
                        if hh == 0:
                            tok_ref = s * T + i0 * 128 + (QG * 128) // 2 - 1
                            cref = r_cref.next()
                            bload(S, cref.v(), C.fc[tok_ref:tok_ref + 1, :], 8)
                        bias = r_bias.next()
                        S.ts("dve", bias[:, 0:nj], cT[:, 0:nj, h], cref[:, h:h + 1], ALU.subtract, -1.0, ALU.mult)
                        acc = accp.next()[:, 0:QG * 65].rr("p (a d) -> p a d", a=QG)
                        first = True
                    sp_, ncol, ilo = stq.pop(idx)
                    pt = r_p.next()
                    S.act(pt[:, 0:ncol], sp_[:, 0:ncol], AF.Exp, bias=bias[:, j:j + 1], scale=0.125)
                    if j >= i0:
                        S.tt("dve", pt[:, 0:128], pt[:, 0:128], lemask.v(), ALU.mult)
                    for qi in range(ilo, i0 + QG):
                        S.mm(acc[:, qi - i0, :], pt[:, (qi - ilo) * 128:(qi - ilo + 1) * 128], vt[:, j, hh, :],
                             start=first, stop=(j == qi), skip_group_check=True)
                        first = False
                    if j == nj - 1:
                        rc = r_rc.next()
                        S.recip(rc.v(), acc[:, :, 64])
                        for qi in range(QG):
                            S.stt(yt[:, i0 + qi, pb:pb + 64], acc[:, qi, 0:64], rc[:, qi:qi + 1],
                                  gt[:, i0 + qi, pb:pb + 64], ALU.mult, ALU.mult)
                S.dma("sp", dv(C.yb[s * T:(s + 1) * T, hp * 128:(hp + 1) * 128].rearrange("(b p) d -> p b d", p=128)),
                      yt.v())
        S.barrier()


def mlp_tile(C, R, h, gbf, w_up, w_down, out):
    S = C.S
    cb = C.cb
    junk = R.junk.next()
    sv = R.st.next()
    S.act(junk.v(), h.v(), AF.Square, accum_out=sv[:, 0:1])
    rms_rstd(S, C, sv[:, 0:1], sv[:, 1:2], D, EPS)
    hn = R.hn.next()
    S.stt(hn.v(), h.v(), sv[:, 1:2], gbf.v(), ALU.mult, ALU.mult)
    pT = R.tp.next()
    for k in range(8):
        S.tr(pT[:, k * 128:(k + 1) * 128], hn[:, k * 128:(k + 1) * 128], cblk(cb, C_ID))
    hnT = R.hnT.next()
    S.copy("act", hnT.v().rr("p k t -> p (k t)"), pT.v())
    hid = R.hid.next()
    for fg in range(8):
        ps = R.pp.next()
        for fc in range(4):
            fcol = (fg * 4 + fc) * 128
            for k in range(8):
                S.mm(ps[:, fc * 128:(fc + 1) * 128], w_up[:, k, fcol:fcol + 128], hnT[:, k, :],
                     start=(k == 0), stop=(k == 7))
        rl = R.rl.next()
        S.act(rl.v(), ps.v(), AF.Relu)
        S.tt("pool" if fg % 2 == 0 else "dve", hid[:, fg * 4:(fg + 1) * 4, :].rr("p a t -> p (a t)"), rl.v(), rl.v(), ALU.mult)
    for n in range(2):
        ps = R.pp.next()
        for kf in range(32):
            S.mm(ps.v(), hid[:, kf, :], w_down[:, kf, n * 512:(n + 1) * 512], start=(kf == 0), stop=(kf == 31))
        S.tt("dve", out[:, n * 512:(n + 1) * 512], ps.v(), h[:, n * 512:(n + 1) * 512], ALU.add)


class MlpRings:
    def __init__(self, S, st, tp, pp, sfx=""):
        self.junk = Ring(S, st, "junkM" + sfx, [128, D], BF16, 1)
        self.st = Ring(S, st, "stM" + sfx, [128, 8], F32, 4)
        self.hn = Ring(S, st, "hnM" + sfx, [128, D], BF16, 2)
        self.hnT = Ring(S, st, "hnTM" + sfx, [128, 8, 128], BF16, 2)
        self.hid = Ring(S, st, "hidM" + sfx, [128, 32, 128], BF16, 1)
        self.rl = Ring(S, st, "rlM" + sfx, [128, 512], BF16, 3)
        self.tp = tp
        self.pp = pp


def phase_C(C):
    S = C.S
    T, NSEQ = C.T, C.NSEQ
    NT = T // 128
    cb = C.cb
    with contextlib.ExitStack() as st:
        w_out = S.sbuf(st, "w_out", [128, 8, D], BF16)
        load_w(S, w_out, C.ab_w_out[0], 8, D)
        w_up = S.sbuf(st, "w_up", [128, 8, DFF], BF16)
        load_w(S, w_up, C.mlp_w_up[0], 8, DFF)
        w_down = S.sbuf(st, "w_down", [128, 32, D], BF16)
        load_w(S, w_down, C.mlp_w_down[0], 32, D)
        gbf = S.sbuf(st, "gbfC", [128, D], F32)
        bload(S, gbf.v(), C.norm_ffn_g[0:1, :], D)
        tp = Ring(S, st, "tpC", [128, 1024], BF16, 2, psum=True)
        pp = Ring(S, st, "ppC", [128, 512], F32, 4, psum=True)
        R = MlpRings(S, st, tp, pp)
        r_x = Ring(S, st, "xC", [128, D], F32, 2)
        r_y = Ring(S, st, "yC", [128, D], BF16, 2)
        r_yT = Ring(S, st, "yTC", [128, 8, 128], BF16, 2)
        r_h1 = Ring(S, st, "h1C", [128, D], F32, 1)
        r_o = Ring(S, st, "oC", [128, D], F32, 1)
        for s in range(NSEQ):
            for i in range(NT):
                r0 = s * T + i * 128
                xt = r_x.next()
                S.dma("sp", xt.v(), dv(C.x[r0:r0 + 128, :]))
                y = r_y.next()
                S.dma("sp", y[:, 0:512], dv(C.ya[r0:r0 + 128, :]))
                S.dma("sp", y[:, 512:1024], dv(C.yb[r0:r0 + 128, :]))
                pT = tp.next()
                for k in range(8):
                    S.tr(pT[:, k * 128:(k + 1) * 128], y[:, k * 128:(k + 1) * 128], cblk(cb, C_ID))
                yT = r_yT.next()
                S.copy("act", yT.v().rr("p k t -> p (k t)"), pT.v())
                h1 = r_h1.next()
                for n in range(2):
                    ps = pp.next()
                    for k in range(8):
                        S.mm(ps.v(), yT[:, k, :], w_out[:, k, n * 512:(n + 1) * 512], start=(k == 0), stop=(k == 7))
                    S.tt("dve", h1[:, n * 512:(n + 1) * 512], ps.v(), xt[:, n * 512:(n + 1) * 512], ALU.add)
                o = r_o.next()
                mlp_tile(C, R, h1, gbf, w_up, w_down, o)
                S.dma("sp", dv(C.h2[r0:r0 + 128, :]), o.v())
        S.barrier()


import math


def phase_D(C):
    S = C.S
    T, NSEQ = C.T, C.NSEQ
    NT = T // 128
    cf, cb = C.cf, C.cb
    A = C.aps
    ID = cblk(cb, C_ID)
    with contextlib.ExitStack() as st:
        w_r = S.sbuf(st, "w_r", [128, 8, D], BF16)
        w_k = S.sbuf(st, "w_k", [128, 8, D], BF16)
        w_v = S.sbuf(st, "w_v", [128, 8, D], BF16)
        load_w(S, w_r, A["rwkv_w_rkv"][0, 0], 8, D)
        load_w(S, w_k, A["rwkv_w_rkv"][0, 1], 8, D)
        load_w(S, w_v, A["rwkv_w_rkv"][0, 2], 8, D)
        w1 = S.sbuf(st, "w1", [128, 8, 64], BF16)
        a1 = S.sbuf(st, "a1", [128, 8, 64], BF16)
        g1 = S.sbuf(st, "g1", [128, 8, 128], BF16)
        load_w(S, w1, A["rwkv_w1"][0], 8, 64)
        load_w(S, a1, A["rwkv_a1"][0], 8, 64)
        load_w(S, g1, A["rwkv_g1"][0], 8, 128)
        w2 = S.sbuf(st, "w2", [128, D], BF16)
        a2 = S.sbuf(st, "a2", [128, D], BF16)
        g2 = S.sbuf(st, "g2", [128, D], BF16)
        S.dma("pool", w2[0:64, :], dv(A["rwkv_w2"][0]))
        S.dma("pool", a2[0:64, :], dv(A["rwkv_a2"][0]))
        S.dma("pool", g2.v(), dv(A["rwkv_g2"][0]))
        bc_ = {}
        for nm, ap in (("gmix", C.norm_mix_g[1:2, :]), ("w0b", A["rwkv_w0"][0:1, :]), ("a0b", A["rwkv_a0"][0:1, :]),
                       ("k_kb", A["rwkv_k_k"][0:1, :]), ("k_ab", A["rwkv_k_a"][0:1, :]),
                       ("r_kb", A["rwkv_r_k"].rearrange("a h n -> a (h n)"))):
            t_ = S.sbuf(st, nm, [128, D], F32)
            bload(S, t_.v(), ap, D)
            bc_[nm] = t_
        gmix, w0b, a0b, k_kb, k_ab, r_kb = (bc_[n] for n in ("gmix", "w0b", "a0b", "k_kb", "k_ab", "r_kb"))
        tp = Ring(S, st, "tpD", [128, 1024], BF16, 2, psum=True)
        pp = Ring(S, st, "ppD", [128, 512], F32, 2, psum=True)
        gp = Ring(S, st, "gpD", [128, 512], F32, 4, psum=True)
        mu_rows = S.sbuf(st, "mu_rows", [128, D], F32)
        S.dma("sp", mu_rows[0:6, :], dv(A["rwkv_mu"][0]))
        mu_kc = S.sbuf(st, "mu_kc", [128, 8, 6], F32)
        mps = gp.next()
        for k in range(8):
            S.tr(mps[:, k * 6:(k + 1) * 6], mu_rows[0:6, k * 128:(k + 1) * 128], cblk(cf, C_ID)[0:6, 0:6])
        S.copy("dve", mu_kc.v().rr("p k m -> p (k m)"), mps[:, 0:48])

        r_x = Ring(S, st, "xD", [128, D], F32, 2)
        r_junk = Ring(S, st, "junkD", [128, D], BF16, 1)
        r_st = Ring(S, st, "stD", [128, 16], F32, 6)
        r_f = Ring(S, st, "fD", [128, D], F32, 10)
        r_e = Ring(S, st, "eD", [128, 512], F32, 4)
        r_b = Ring(S, st, "bD", [128, D], BF16, 8)
        r_mix = Ring(S, st, "mixD", [128, 8, 128], BF16, 6)
        r_o = Ring(S, st, "oD", [128, D], BF16, 8)
        r_s = Ring(S, st, "sD", [128, 128], BF16, 4)
        NEG = -math.exp(-0.5)

        for s in range(NSEQ):
            for i in range(NT):
                r0 = s * T + i * 128
                xc = r_x.next()
                S.dma("sp", xc.v(), dv(C.out[r0:r0 + 128, :]))
                xp = r_x.next()
                if i == 0:
                    S.memset("pool", xp[0:1, :], 0.0)
                    S.dma("sp", xp[1:128, :], dv(C.out[r0:r0 + 127, :]))
                else:
                    S.dma("sp", xp.v(), dv(C.out[r0 - 1:r0 + 127, :]))
                hc = r_f.next()
                hp = r_f.next()
                for xt_, ht_ in ((xc, hc), (xp, hp)):
                    junk = r_junk.next()
                    sv = r_st.next()
                    S.act(junk.v(), xt_.v(), AF.Square, accum_out=sv[:, 0:1])
                    rms_rstd(S, C, sv[:, 0:1], sv[:, 1:2], D, EPS)
                    S.stt(ht_.v(), xt_.v(), sv[:, 1:2], gmix.v(), ALU.mult, ALU.mult)
                hcb = r_b.next()
                xxb = r_b.next()
                S.copy("act", hcb.v(), hc.v())
                S.tt("pool", xxb.v(), hp.v(), hc.v(), ALU.subtract)
                pH = tp.next()
                pX = tp.next()
                for k in range(8):
                    S.tr(pH[:, k * 128:(k + 1) * 128], hcb[:, k * 128:(k + 1) * 128], ID)
                for k in range(8):
                    S.tr(pX[:, k * 128:(k + 1) * 128], xxb[:, k * 128:(k + 1) * 128], ID)
                hcT = r_b.next()
                xxT = r_b.next()
                S.copy("act", hcT.v(), pH.v())
                S.copy("dve", xxT.v(), pX.v())
                mixT = []
                for m in range(6):
                    mt = r_mix.next()
                    e = "pool" if m % 2 == 0 else "dve"
                    S.tt(e, mt.v(), xxT.v().rr("p (k t) -> p k t", k=8), bcv(mu_kc[:, :, m], [128, 8, 128]), ALU.mult)
                    S.tt(e, mt.v(), mt.v(), hcT.v().rr("p (k t) -> p k t", k=8), ALU.add)
                    mixT.append(mt)
                xr, xw, xk, xv, xa, xg = mixT

                def proj(w, xT, dst):
                    for n in range(2):
                        ps = pp.next()
                        for k in range(8):
                            S.mm(ps.v(), xT[:, k, :], w[:, k, n * 512:(n + 1) * 512], start=(k == 0), stop=(k == 7))
                        S.copy("act", dst[:, n * 512:(n + 1) * 512], ps.v())
                rf = r_f.next()
                proj(w_r, xr, rf)
                kf = r_f.next()
                proj(w_k, xk, kf)
                vb = r_b.next()
                proj(w_v, xv, vb)
                S.dma("sp", dv(C.sV[r0:r0 + 128, :]), vb.v())

                def lora(w_a, xT, w_b, nj, func, bias_t, dst, final_func):
                    lp = gp.next()
                    for k in range(8):
                        S.mm(lp[0:nj, 0:128], w_a[:, k, :], xT[:, k, :], start=(k == 0), stop=(k == 7))
                    th = r_s.next()
                    S.act(th[0:nj, :], lp[0:nj, 0:128], func)
                    for n in range(2):
                        ps = pp.next()
                        S.mm(ps.v(), th[0:nj, :], w_b[0:nj, n * 512:(n + 1) * 512])
                        if bias_t is not None:
                            S.tt("dve", dst[:, n * 512:(n + 1) * 512], ps.v(), bias_t[:, n * 512:(n + 1) * 512], ALU.add)
                        else:
                            S.copy("act", dst[:, n * 512:(n + 1) * 512], ps.v())
                    if final_func is not None:
                        S.act(dst.v(), dst.v(), final_func)
                lw = r_f.next()
                lora(w1, xw, w2, 64, AF.Tanh, w0b, lw, AF.Sigmoid)
                S.ts("pool", lw.v(), lw.v(), NEG, ALU.mult, 0.0, ALU.add)
                af = r_f.next()
                lora(a1, xa, a2, 64, AF.Copy, a0b, af, AF.Sigmoid)
                gb_ = r_b.next()
                lora(g1, xg, g2, 128, AF.Sigmoid, None, gb_, None)
                S.dma("sp", dv(C.sG[r0:r0 + 128, :]), gb_.v())

                kk = r_f.next()
                S.tt("dve", kk.v(), kf.v(), k_kb.v(), ALU.mult)
                sq = r_f.next()
                S.tt("pool", sq.v(), kk.v(), kk.v(), ALU.mult)
                sv = r_st.next()
                S.reduce(sv.v(), sq.v().rr("p (h d) -> p h d", h=16))
                S.ts("dve", sv.v(), sv.v(), 1e-24, ALU.max)
                S.act(sv.v(), sv.v(), AF.Ln)
                S.act(sv.v(), sv.v(), AF.Exp, scale=-0.5)
                S.tt("dve", kk.v().rr("p (h d) -> p h d", h=16), kk.v().rr("p (h d) -> p h d", h=16),
                     bcv(sv.v(), [128, 16, 64]), ALU.mult)
                S.stt(sq.v(), af.v(), -1.0, k_ab.v(), ALU.add, ALU.mult)
                S.stt(kf.v(), sq.v(), 1.0, kf.v(), ALU.add, ALU.mult)
                t2 = r_f.next()
                S.tt("pool", t2.v(), rf.v(), kf.v(), ALU.mult)
                S.tt("pool", t2.v(), t2.v(), r_kb.v(), ALU.mult)
                rkv = r_st.next()
                S.reduce(rkv.v(), t2.v().rr("p (h d) -> p h d", h=16))
                S.dma("sp", dv(C.rk[r0:r0 + 128, :]), rkv.v())
                S.tt("dve", t2.v(), kk.v(), af.v(), ALU.mult)
                Rt = r_o.next()
                Kt = r_o.next()
                Bt = r_o.next()
                At = r_o.next()
                for n in range(2):
                    hs = slice(n * 512, (n + 1) * 512)
                    cw = gp.next()
                    cwx = gp.next()
                    S.mm(cw.v(), cblk(cf, C_LE), lw[:, hs])
                    S.mm(cwx.v(), cblk(cf, C_LT), lw[:, hs])
                    ep = r_e.next()
                    S.act(ep.v(), cw.v(), AF.Exp)
                    en = r_e.next()
                    S.act(en.v(), cw.v(), AF.Exp, scale=-1.0)
                    epx = r_e.next()
                    S.act(epx.v(), cwx.v(), AF.Exp)
                    S.tt("dve", Rt[:, hs], rf[:, hs], ep.v(), ALU.mult)
                    S.tt("pool", Kt[:, hs], kf[:, hs], en.v(), ALU.mult)
                    S.tt("dve", Bt[:, hs], t2[:, hs], en.v(), ALU.mult)
                    S.stt(At[:, hs], kk[:, hs], -1.0, epx.v(), ALU.mult, ALU.mult)
                S.dma("sp", dv(C.sR[r0:r0 + 128, :]), Rt.v())
                S.dma("sp", dv(C.sK[r0:r0 + 128, :]), Kt.v())
                S.dma("sp", dv(C.sB[r0:r0 + 128, :]), Bt.v())
                S.dma("sp", dv(C.sA[r0:r0 + 128, :]), At.v())
                pcp = gp.next()
                for p in range(8):
                    S.mm(pcp[:, p * 2:(p + 1) * 2], lw[:, p * 128:(p + 1) * 128], cblk(cf, C_ONES)[:, 0:2])
                pct = r_st.next()
                S.act(pct.v(), pcp[:, 0:16], AF.Exp)
                S.dma("sp", dv(C.pcs[s * NT + i]), pct.v())
        S.barrier()


def phase_E(C):
    S = C.S
    T, NSEQ = C.T, C.NSEQ
    NT = T // 128
    cf, cb = C.cf, C.cb
    A = C.aps
    ID = cblk(cb, C_ID)
    with contextlib.ExitStack() as st:
        w_o = S.sbuf(st, "w_o", [128, 8, D], BF16)
        load_w(S, w_o, A["rwkv_w_o"][0], 8, D)
        lnxg = S.sbuf(st, "lnxg", [128, D], F32)
        lnxb = S.sbuf(st, "lnxb", [128, D], F32)
        bload(S, lnxg.v(), A["rwkv_lnx_g"][0:1, :], D)
        bload(S, lnxb.v(), A["rwkv_lnx_b"][0:1, :], D)
        lt2 = S.sbuf(st, "lt2", [128, 2, 128], F32)
        le2 = S.sbuf(st, "le2", [128, 2, 128], F32)
        gt2 = S.sbuf(st, "gt2", [128, 2, 128], F32)
        id4 = S.sbuf(st, "id4", [128, 4, 128], BF16)
        for j in range(2):
            S.copy("pool", lt2[:, j, :], cblk(cf, C_LT))
            S.copy("pool", le2[:, j, :], cblk(cf, C_LE))
            S.copy("pool", gt2[:, j, :], cblk(cf, C_GT))
        for j in range(4):
            S.copy("pool", id4[:, j, :], cblk(cb, C_ID))
        BD = cblk(cf, C_BD)
        Mf = S.sbuf(st, "Mf", [128, 8, 128], F32)
        Mb = S.sbuf(st, "Mb", [128, 8, 128], BF16)
        Apad = S.sbuf(st, "Apad", [128, 16, 128], BF16)
        S.memset("pool", Apad.v(), 0.0)
        AMf = S.sbuf(st, "AMf", [128, 16, 2, 128], BF16)
        r_amb = Ring(S, st, "AMb", [128, 16, 2, 128], BF16, 2)
        r_in = Ring(S, st, "inE", [128, D], BF16, 12)
        r_x = Ring(S, st, "xE", [128, D], F32, 2)
        r_rp = Ring(S, st, "rpE", [128, 16], F32, 4)
        r_sm = Ring(S, st, "smE", [128, 16], F32, 4)
        r_art = Ring(S, st, "artE", [128, 8, 2, 128], BF16, 2)
        r_bk = Ring(S, st, "bkE", [128, 8, 128], BF16, 2)
        r_aht = Ring(S, st, "ahtE", [128, 8, 128], BF16, 2)
        r_yT = Ring(S, st, "yTE", [128, 8, 128], BF16, 1)
        r_pq = Ring(S, st, "pqE", [128, 4, 128], BF16, 16)
        r_tt = Ring(S, st, "ttE", [128, 4, 128], BF16, 10)
        r_av = Ring(S, st, "avE", [128, D], BF16, 1)
        r_w = Ring(S, st, "wE", [128, D], BF16, 2)
        r_yfin = Ring(S, st, "yfinE", [128, D], BF16, 1)
        r_u = Ring(S, st, "uE", [128, 128], BF16, 4)
        r_tmp = Ring(S, st, "tmpE", [128, 128], F32, 4)
        r_f = Ring(S, st, "fE", [128, D], F32, 3)
        r_h = Ring(S, st, "hE", [128, D], F32, 2)
        tpF = Ring(S, st, "tpEF", [128, 1024], BF16, 1, psum=True)
        tpB = Ring(S, st, "tpEB", [128, 1024], BF16, 1, psum=True)
        gpF = Ring(S, st, "gpEF", [128, 512], F32, 4, psum=True)
        gpB = Ring(S, st, "gpEB", [128, 512], F32, 2, psum=True)

        def front(tile, X):
            s, i = tile
            r0 = s * T + i * 128
            ins = {}
            for nm, src in (("R", C.sR), ("K", C.sK), ("B", C.sB), ("A", C.sA), ("V", C.sV), ("G", C.sG)):
                t_ = r_in.next()
                S.dma("sp", t_.v(), dv(src[r0:r0 + 128, :]))
                ins[nm] = t_
            Rt, Kt, Bt, At, Vb, Gb = (ins[n] for n in "RKBAVG")
            xc = r_x.next()
            S.dma("sp", xc.v(), dv(C.out[r0:r0 + 128, :]))
            rk = r_rp.next()
            S.dma("sp", rk.v(), dv(C.rk[r0:r0 + 128, :]))
            pc = r_rp.next()
            S.dma("sp", pc.v(), dv(C.pcs[s * NT + i]))
            X.update(Kt=Kt, Bt=Bt, Vb=Vb, Gb=Gb, xc=xc, rk=rk, pc=pc, r0=r0, first=(i == 0))
            yield
            ART = r_art.next()
            BtT = r_bk.next()
            KtT = r_bk.next()
            for src, dstv, eng in ((At, ART[:, :, 0, :], "act"), (Rt, ART[:, :, 1, :], "dve"),
                                   (Bt, BtT.v(), "act"), (Kt, KtT.v(), "dve")):
                pt_ = tpF.next()
                for p in range(8):
                    S.tr(pt_[:, p * 128:(p + 1) * 128], src[:, p * 128:(p + 1) * 128], ID)
                S.copy(eng, dstv, pt_.v().rr("q (p t) -> q p t", p=8))
                yield
            Apv = Apad.v().rr("t (p hh) c -> t p hh c", hh=2)
            Atv = At.v().rr("t (p hh k) -> t p hh k", hh=2, k=64)
            S.copy("pool", Apv[:, :, 0, 0:64], Atv[:, :, 0, :])
            S.copy("pool", Apv[:, :, 1, 64:128], Atv[:, :, 1, :])
            AMb = r_amb.next()
            X.update(ART=ART, AMb=AMb)
            P = [None] * 4
            Q = [None] * 4
            TT = [None] * 4
            for g in range(4):
                for hl in range(4):
                    h = 4 * g + hl
                    p, pb = h // 2, (h % 2) * 64
                    aps = gpF.next()
                    art2 = ART[pb:pb + 64, p, :, :].rr("k a t -> k (a t)")
                    S.mm(aps[:, 0:256], BtT[pb:pb + 64, p, :], art2)
                    S.mm(aps[:, 256:512], KtT[pb:pb + 64, p, :], art2)
                    a4 = aps.v().rr("q (b a t) -> q b a t", b=2, a=2)
                    S.tt("dve", AMf[:, h, :, :], a4[:, :, 0, :], lt2.v(), ALU.mult)
                    S.tt("dve", AMb[:, h, :, :], a4[:, :, 1, :], le2.v(), ALU.mult)
                    if hl % 2 == 1:
                        yield
                npsl = [gpF.next(), gpF.next()]
                for hl in range(4):
                    h = 4 * g + hl
                    p, pb = h // 2, (h % 2) * 64
                    S.mm(npsl[h % 2][:, (hl // 2) * 128:(hl // 2 + 1) * 128], ART[pb:pb + 64, p, 0, :], BtT[pb:pb + 64, p, :])
                P[g] = r_pq.next()
                Pv = P[g].v().rr("q (a two) t -> q a two t", two=2)
                for par in range(2):
                    S.tt("dve", Pv[:, :, par, :], npsl[par][:, 0:256].rr("q (a t) -> q a t", a=2), gt2.v(), ALU.mult)
                Q[g] = AMf[:, 4 * g:4 * g + 4, 0, :]
                TT[g] = r_tt.next()
                S.tt("pool", TT[g].v(), Q[g], id4.v(), ALU.add)
                yield
            for j in range(1, 7):
                for g in range(4):
                    pps = gpF.next()
                    for hl in range(4):
                        S.mm(pps[:, hl * 128:(hl + 1) * 128], Q[g][:, hl, :], P[g][:, hl, :])
                    if j < 6:
                        qps = gpF.next()
                        for hl in range(4):
                            S.mm(qps[:, hl * 128:(hl + 1) * 128], P[g][:, hl, :], Q[g][:, hl, :])
                    Pn = r_pq.next()
                    S.copy("act", Pn.v().rr("q a t -> q (a t)"), pps.v())
                    if j < 6:
                        Qn = r_pq.next()
                        S.copy("act", Qn.v().rr("q a t -> q (a t)"), qps.v())
                    tps = gpF.next()
                    for hl in range(4):
                        S.mm(tps[:, hl * 128:(hl + 1) * 128], Pn[:, hl, :], TT[g][:, hl, :])
                    TTn = r_tt.next()
                    S.tt("dve", TTn.v().rr("q a t -> q (a t)"), tps.v(), TT[g].v().rr("q a t -> q (a t)"), ALU.add)
                    P[g] = Pn
                    if j < 6:
                        Q[g] = Qn.v()
                    TT[g] = TTn
                    yield
            AVb = r_av.next()
            Wb = r_w.next()
            for half in range(2):
                avp = gpF.next()
                for hl in range(8):
                    h = half * 8 + hl
                    S.mm(avp[:, hl * 64:(hl + 1) * 64], AMf[:, h, 1, :], Vb[:, h * 64:(h + 1) * 64])
                S.copy("act", AVb[:, half * 512:(half + 1) * 512], avp.v())
                yield
            for half in range(2):
                wp = gpF.next()
                for hl in range(8):
                    h = half * 8 + hl
                    S.mm(wp[:, hl * 64:(hl + 1) * 64], TT[h // 4][:, h % 4, :], AVb[:, h * 64:(h + 1) * 64])
                S.copy("act", Wb[:, half * 512:(half + 1) * 512], wp.v())
                yield
            AhT = r_aht.next()
            for half in range(2):
                ahp = gpF.next()
                for pl in range(4):
                    p = half * 4 + pl
                    h0, h1 = 2 * p, 2 * p + 1
                    S.mm(ahp[:, pl * 128:(pl + 1) * 128], Apad[:, h0, :], TT[h0 // 4][:, h0 % 4, :], start=True, stop=False)
                    S.mm(ahp[:, pl * 128:(pl + 1) * 128], Apad[:, h1, :], TT[h1 // 4][:, h1 % 4, :], start=False, stop=True)
                S.copy("dve", AhT[:, half * 4:(half + 1) * 4, :].rr("q a t -> q (a t)"), ahp.v())
                yield
            X.update(Wb=Wb, AhT=AhT)

        def back(tile, X):
            Kt, Bt, Vb, Gb, xc, rk, pc, r0 = (X[k] for k in ("Kt", "Bt", "Vb", "Gb", "xc", "rk", "pc", "r0"))
            ART, AMb, Wb, AhT = X["ART"], X["AMb"], X["Wb"], X["AhT"]
            if X["first"]:
                S.memset("pool", Mf.v(), 0.0)
                S.memset("pool", Mb.v(), 0.0)
            Yf = r_f.next()
            for p in range(8):
                pcs_ = slice(p * 128, (p + 1) * 128)
                ups = gpB.next()
                S.mm(ups[:, 0:128], AhT[:, p, :], Mb[:, p, :])
                Ub = r_u.next()
                S.tt("dve", Ub.v(), ups[:, 0:128], Wb[:, pcs_], ALU.add)
                yield
                yps = gpB.next()
                S.mm(yps[:, 0:128], ART[:, p, 1, :], Mb[:, p, :], start=True, stop=False)
                for hh in range(2):
                    h = 2 * p + hh
                    S.mm(yps[:, hh * 64:(hh + 1) * 64], AMb[:, h, 0, :], Ub[:, hh * 64:(hh + 1) * 64], start=False, stop=False)
                    S.mm(yps[:, hh * 64:(hh + 1) * 64], AMb[:, h, 1, :], Vb[:, h * 64:(h + 1) * 64], start=False, stop=(hh == 1))
                S.copy("act", Yf[:, pcs_], yps[:, 0:128])
                mps = gpB.next()
                S.mm(mps[:, 0:128], Bt[:, pcs_], Ub.v(), start=True, stop=False)
                S.mm(mps[:, 0:128], Kt[:, pcs_], Vb[:, pcs_], start=False, stop=True)
                tmp = r_tmp.next()
                S.stt(tmp.v(), mps[:, 0:128], pc[:, 2 * p:2 * p + 1], BD, ALU.mult, ALU.mult)
                S.stt(Mf[:, p, :], Mf[:, p, :], pc[:, 2 * p:2 * p + 1], tmp.v(), ALU.mult, ALU.add)
                S.copy("act", Mb[:, p, :], Mf[:, p, :])
                yield
            Y3 = Yf.v().rr("q (h d) -> q h d", h=16)
            sm = r_sm.next()
            S.reduce(sm.v(), Y3)
            S.ts("dve", sm.v(), sm.v(), 1.0 / 64, ALU.mult)
            S.tt("dve", Y3, Y3, bcv(sm.v(), [128, 16, 64]), ALU.subtract)
            sq = r_f.next()
            S.tt("pool", sq.v(), Yf.v(), Yf.v(), ALU.mult)
            yield
            sm2 = r_sm.next()
            S.reduce(sm2.v(), sq.v().rr("q (h d) -> q h d", h=16))
            rms_rstd(S, C, sm2.v(), sm2.v(), 64, GN_EPS)
            S.tt("dve", Y3, Y3, bcv(sm2.v(), [128, 16, 64]), ALU.mult)
            S.tt("pool", Yf.v(), Yf.v(), lnxg.v(), ALU.mult)
            S.tt("pool", Yf.v(), Yf.v(), lnxb.v(), ALU.add)
            yield
            S.tt("dve", sq.v().rr("q (h d) -> q h d", h=16), Vb.v().rr("q (h d) -> q h d", h=16),
                 bcv(rk.v(), [128, 16, 64]), ALU.mult)
            S.tt("pool", Yf.v(), Yf.v(), sq.v(), ALU.add)
            yfin = r_yfin.next()
            S.tt("dve", yfin.v(), Yf.v(), Gb.v(), ALU.mult)
            yield
            pT = tpB.next()
            for k in range(8):
                S.tr(pT[:, k * 128:(k + 1) * 128], yfin[:, k * 128:(k + 1) * 128], ID)
            yT = r_yT.next()
            S.copy("act", yT.v().rr("q k t -> q (k t)"), pT.v())
            yield
            h3 = r_h.next()
            for n in range(2):
                ps = gpB.next()
                for k in range(8):
                    S.mm(ps.v(), yT[:, k, :], w_o[:, k, n * 512:(n + 1) * 512], start=(k == 0), stop=(k == 7))
                S.tt("dve", h3[:, n * 512:(n + 1) * 512], ps.v(), xc[:, n * 512:(n + 1) * 512], ALU.add)
                yield
            S.dma("sp", dv(C.out[r0:r0 + 128, :]), h3.v())

        tiles = [(s, i) for s in range(NSEQ) for i in range(NT)]
        run_pipelined(tiles, front, back)
        S.barrier()


def phase_F(C):
    S = C.S
    T, NSEQ = C.T, C.NSEQ
    NT = T // 128
    with contextlib.ExitStack() as st:
        w_up = S.sbuf(st, "w_upF", [128, 8, DFF], BF16)
        load_w(S, w_up, C.mlp_w_up[1], 8, DFF)
        w_down = S.sbuf(st, "w_downF", [128, 32, D], BF16)
        load_w(S, w_down, C.mlp_w_down[1], 32, D)
        gbf = S.sbuf(st, "gbfF", [128, D], F32)
        bload(S, gbf.v(), C.norm_ffn_g[1:2, :], D)
        tp = Ring(S, st, "tpF", [128, 1024], BF16, 2, psum=True)
        pp = Ring(S, st, "ppF", [128, 512], F32, 4, psum=True)
        R = MlpRings(S, st, tp, pp, "F")
        r_x = Ring(S, st, "xF", [128, D], F32, 3)
        r_o = Ring(S, st, "oF", [128, D], F32, 2)
        for s in range(NSEQ):
            for i in range(NT):
                r0 = s * T + i * 128
                xt = r_x.next()
                S.dma("sp", xt.v(), dv(C.out[r0:r0 + 128, :]))
                o = r_o.next()
                mlp_tile(C, R, xt, gbf, w_up, w_down, o)
                S.dma("sp", dv(C.out[r0:r0 + 128, :]), o.v())
        S.barrier()

INPUT_SPECS = [
    ("norm_mix_g", [2, D]), ("norm_ffn_g", [2, D]), ("ab_w_in", [1, D, ABIN]),
    ("hgrn_lower_bounds", [3, 512]), ("hgrn_norm_g", [1, 512]), ("fox_forget_bias", [1, 8]),
    ("fox_q_norm_g", [1, 64]), ("fox_k_norm_g", [1, 64]), ("ab_w_out", [1, D, D]),
    ("rwkv_mu", [1, 6, D]), ("rwkv_w_rkv", [1, 3, D, D]), ("rwkv_w0", [1, D]),
    ("rwkv_w1", [1, D, 64]), ("rwkv_w2", [1, 64, D]), ("rwkv_a0", [1, D]),
    ("rwkv_a1", [1, D, 64]), ("rwkv_a2", [1, 64, D]), ("rwkv_g1", [1, D, 128]),
    ("rwkv_g2", [1, 128, D]), ("rwkv_k_k", [1, D]), ("rwkv_k_a", [1, D]),
    ("rwkv_r_k", [1, 16, 64]), ("rwkv_lnx_g", [1, D]), ("rwkv_lnx_b", [1, D]),
    ("rwkv_w_o", [1, D, D]), ("mlp_w_up", [2, D, DFF]), ("mlp_w_down", [2, DFF, D]),
]


def build(T, NSEQ, upto="F"):
    nc = bass.Bass("TRN2", target_bir_lowering=False)
    C = Ctx()
    C.nc = nc
    C.T, C.NSEQ = T, NSEQ
    C.cut = 0
    NTOK = T * NSEQ
    C.x = nc.dram_tensor("x", [NTOK, D], F32, kind="ExternalInput").ap()
    aps = {}
    for name, shp in INPUT_SPECS:
        aps[name] = nc.dram_tensor(name, shp, F32, kind="ExternalInput").ap()
    C.norm_mix_g = aps["norm_mix_g"]
    C.norm_ffn_g = aps["norm_ffn_g"]
    C.ab_w_in = aps["ab_w_in"]
    C.hgrn_lb = aps["hgrn_lower_bounds"]
    C.hgrn_norm_g = aps["hgrn_norm_g"]
    C.fox_fb = aps["fox_forget_bias"]
    C.fox_q_g = aps["fox_q_norm_g"]
    C.fox_k_g = aps["fox_k_norm_g"]
    C.ab_w_out = aps["ab_w_out"]
    C.mlp_w_up = aps["mlp_w_up"]
    C.mlp_w_down = aps["mlp_w_down"]
    C.aps = aps
    cst = nc.dram_tensor("cst", [128, NCONST * 128], F32, kind="ExternalInput").ap()
    C.out = nc.dram_tensor("out", [NTOK, D], F32, kind="ExternalOutput").ap()

    def scratch(name, shp, dt):
        return nc.dram_tensor(name, shp, dt, kind="ExternalOutput").ap()
    sR = scratch("sR", [NTOK, D], BF16)
    sK = scratch("sK", [NTOK, D], BF16)
    sB = scratch("sB", [NTOK, D], BF16)
    C.sR, C.sK, C.sB = sR, sK, sB
    C.sA = scratch("sA", [NTOK, D], BF16)
    C.sV = scratch("sV", [NTOK, D], BF16)
    C.sG = scratch("sG", [NTOK, D], BF16)
    C.rk = scratch("rk", [NTOK, 16], F32)
    C.pcs = scratch("pcs", [NTOK // 128, 128, 16], F32)
    sRv = sR.rearrange("(x r) d -> x (r d)", x=2).rearrange("x (s a p t) -> x s a p t", s=NSEQ, a=4, p=128)
    C.fq = sRv[0]
    C.fk = sRv[1]
    C.fv = sK[:, 0:512]
    C.fg = sK[:, 512:1024]
    C.fc = scratch("fc", [NTOK, 8], F32)
    C.ya = sB[:, 0:512]
    C.yb = sB[:, 512:1024]
    C.h2 = C.out
    with contextlib.ExitStack() as st:
        S = Sched(nc, st)
        C.S = S
        cf = S.sbuf(st, "cstf", [128, NCONST * 128], F32)
        cbt = S.sbuf(st, "cstb", [128, NCONST * 128], BF16)
        S.dma("sp", cf.v(), dv(cst))
        S.copy("dve", cbt.v(), cf.v())
        C.cf, C.cb = cf, cbt
        phase_A(C)
        if upto != "A":
            phase_B(C)
            phase_C(C)
        if upto not in ("A", "C"):
            phase_D(C)
            if upto != "D":
                phase_E(C)
        if upto == "F":
            phase_F(C)
        S.final_wait("sp")
        C.ninst = S.ninst
    return nc, C


_CACHE = {}


def kernel(**inputs):
    x = np.ascontiguousarray(inputs["x"], dtype=np.float32)
    B, T, _ = x.shape
    NSEQ = B // NCORES
    key = (T, NSEQ)
    if key not in _CACHE:
        _CACHE[key] = build(T, NSEQ)[0]
    nc = _CACHE[key]
    cst = make_consts()
    shared = {name: np.ascontiguousarray(inputs[name], dtype=np.float32) for name, _ in INPUT_SPECS}
    shared["cst"] = cst
    in_maps = []
    for c in range(NCORES):
        m = dict(shared)
        m["x"] = x[c * NSEQ:(c + 1) * NSEQ].reshape(NSEQ * T, D)
        in_maps.append(m)
    res = run_bass_kernel_spmd(nc, in_maps, core_ids=list(range(NCORES)))
    outs = [np.asarray(r["out"]).reshape(NSEQ, T, D) for r in res.results]
    return np.concatenate(outs, axis=0).astype(np.float32)
```

```python
import numpy as np
import concourse.bass as bass
import concourse.mybir as mybir

F32 = mybir.dt.float32
BF16 = mybir.dt.bfloat16
AF = mybir.ActivationFunctionType
ALU = mybir.AluOpType
AX = mybir.AxisListType


class Buf:
    __slots__ = ("name", "lw", "rd")

    def __init__(self, name):
        self.name = name
        self.lw = None
        self.rd = {}


class V:
    __slots__ = ("ap", "bufs")

    def __init__(self, ap, bufs):
        self.ap = ap
        self.bufs = bufs

    def __getitem__(self, idx):
        return V(self.ap[idx], self.bufs)

    def rr(self, pat, **kw):
        return V(self.ap.rearrange(pat, **kw), self.bufs)

    def bc(self, shape):
        return V(self.ap.broadcast_to(shape), self.bufs)


class Tile:
    def __init__(self, t, name):
        self.t = t
        self.buf = Buf(name)

    def __getitem__(self, idx):
        return V(self.t[idx], (self.buf,))

    def v(self):
        return V(self.t[:], (self.buf,))

    def part(self, name):
        p = Tile.__new__(Tile)
        p.t = self.t
        p.buf = Buf(name)
        return p


class Sched:
    ENG = ("pe", "act", "dve", "pool", "sp")

    def __init__(self, nc, stack, n_dma_sems=24):
        self.nc = nc
        self.stack = stack
        self.eng = {"pe": nc.tensor, "act": nc.scalar, "dve": nc.vector,
                    "pool": nc.gpsimd, "sp": nc.sync}
        self.sem = {}
        self.cnt = {}
        for e in self.ENG:
            self.sem[e] = stack.enter_context(nc.semaphore("s_" + e))
            self.cnt[e] = 0
        self.dsem = [stack.enter_context(nc.semaphore("d%d" % i)) for i in range(n_dma_sems)]
        self.dcnt = [0] * n_dma_sems
        self.dnext = 0
        self.waited = {e: {} for e in self.ENG}
        self.ninst = 0

    def _semh(self, key):
        if isinstance(key, str):
            return self.sem[key]
        return self.dsem[key]

    def _wait(self, e, key, val):
        w = self.waited[e]
        if w.get(key, 0) >= val:
            return
        self.eng[e].wait_ge(self._semh(key), val)
        w[key] = val
        self.ninst += 1

    def _deps(self, e, reads, writes, skip_same_pe=True):
        need = {}

        def add(dep):
            if dep is None:
                return
            k, v = dep
            if need.get(k, 0) < v:
                need[k] = v
        for b in reads:
            add(b.lw)
        for b in writes:
            add(b.lw)
            for k, v in b.rd.items():
                add((k, v))
        for k, v in need.items():
            if e == "pe" and k == "pe":
                continue
            self._wait(e, k, v)

    def _mark(self, key, val, reads, writes):
        for b in reads:
            if b.rd.get(key, 0) < val:
                b.rd[key] = val
        for b in writes:
            b.lw = (key, val)
            b.rd = {}

    def op(self, e, fn, outs, ins):
        reads = [b for v in ins if v is not None for b in v.bufs]
        writes = [b for v in outs if v is not None for b in v.bufs]
        self._deps(e, reads, writes)
        self.cnt[e] += 1
        inst = fn()
        inst.then_inc(self.sem[e], 1)
        self._mark(e, self.cnt[e], reads, writes)
        self.ninst += 1
        return inst

    def dma(self, q, out, in_, **kw):
        reads = list(in_.bufs)
        writes = list(out.bufs)
        i = self.dnext
        self.dnext = (self.dnext + 1) % len(self.dsem)
        if self.dcnt[i] > 0:
            self._wait(q, i, self.dcnt[i])
        self._deps(q, reads, writes)
        self.dcnt[i] += 16
        self.eng[q].dma_start(out=out.ap, in_=in_.ap, **kw).then_inc(self.dsem[i], 16)
        self._mark(i, self.dcnt[i], reads, writes)
        self.ninst += 1

    def barrier(self):
        for e in self.ENG:
            for e2 in self.ENG:
                if e2 != e and self.cnt[e2] > 0:
                    self._wait(e, e2, self.cnt[e2])
            for i in range(len(self.dsem)):
                if self.dcnt[i] > 0:
                    self._wait(e, i, self.dcnt[i])

    def final_wait(self, e="sp"):
        for i in range(len(self.dsem)):
            if self.dcnt[i] > 0:
                self._wait(e, i, self.dcnt[i])
        for e2 in self.ENG:
            if e2 != e and self.cnt[e2] > 0:
                self._wait(e, e2, self.cnt[e2])

    def sbuf(self, stack, name, shape, dt):
        t = stack.enter_context(self.nc.sbuf_tensor(name, list(shape), dt))
        return Tile(t, name)

    def psum(self, stack, name, shape, dt):
        t = stack.enter_context(self.nc.psum_tensor(name, list(shape), dt))
        return Tile(t, name)

    def _pe_mode(self, lhsT):
        def rnd(n):
            return 32 if n <= 32 else (64 if n <= 64 else 128)
        shp = lhsT.ap.shape
        k = rnd(shp[0])
        m = 1
        for d in shp[1:]:
            m *= d
        mode = (k, rnd(m))
        if getattr(self, "_last_pe_mode", None) not in (None, mode):
            self.nc.tensor.drain()
            self.ninst += 1
        self._last_pe_mode = mode

    def mm(self, out, lhsT, rhs, start=True, stop=True, **kw):
        self._pe_mode(lhsT)
        return self.op("pe", lambda: self.nc.tensor.matmul(out.ap, lhsT.ap, rhs.ap, start=start, stop=stop, **kw),
                       [out], [lhsT, rhs])

    def tr(self, out, in_, ident):
        self._pe_mode(in_)
        return self.op("pe", lambda: self.nc.tensor.transpose(out.ap, in_.ap, ident.ap), [out], [in_, ident])

    def act(self, out, in_, func, bias=None, scale=None, accum_out=None):
        kw = {}
        ins = [in_]
        if bias is not None:
            if isinstance(bias, V):
                kw["bias"] = bias.ap
                ins.append(bias)
            else:
                kw["bias"] = bias
        if scale is not None:
            if isinstance(scale, V):
                kw["scale"] = scale.ap
                ins.append(scale)
            else:
                kw["scale"] = scale
        outs = [out]
        if accum_out is not None:
            kw["accum_out"] = accum_out.ap
            outs.append(accum_out)
        return self.op("act", lambda: self.nc.scalar.activation(out.ap, in_.ap, func, **kw), outs, ins)

    def _ve(self, e):
        return self.nc.vector if e == "dve" else self.nc.gpsimd

    def tt(self, e, out, in0, in1, op):
        return self.op(e, lambda: self._ve(e).tensor_tensor(out.ap, in0.ap, in1.ap, op), [out], [in0, in1])

    def ts(self, e, out, in0, s1, op0, s2=None, op1=None, accum_out=None):
        ins = [in0]
        a1 = s1
        if isinstance(s1, V):
            ins.append(s1)
            a1 = s1.ap
        a2 = s2
        if isinstance(s2, V):
            ins.append(s2)
            a2 = s2.ap
        kw = {}
        outs = [out]
        if op1 is not None:
            kw["op1"] = op1
        if accum_out is not None:
            kw["accum_out"] = accum_out.ap
            outs.append(accum_out)
        return self.op(e, lambda: self._ve(e).tensor_scalar(out.ap, in0.ap, a1, a2, op0, **kw), outs, ins)

    def stt(self, out, in0, scalar, in1, op0, op1):
        ins = [in0, in1]
        sc = scalar
        if isinstance(scalar, V):
            ins.append(scalar)
            sc = scalar.ap
        return self.op("dve", lambda: self.nc.vector.scalar_tensor_tensor(out.ap, in0.ap, sc, in1.ap, op0, op1),
                       [out], ins)

    def copy(self, e, out, in_):
        if e == "act":
            return self.op("act", lambda: self.nc.scalar.copy(out.ap, in_.ap), [out], [in_])
        return self.op(e, lambda: self._ve(e).tensor_copy(out.ap, in_.ap), [out], [in_])

    def memset(self, e, out, val):
        return self.op(e, lambda: self._ve(e).memset(out.ap, val), [out], [])

    def reduce(self, out, in_, op=None, axis=None):
        op = op or ALU.add
        axis = axis or AX.X
        return self.op("dve", lambda: self.nc.vector.tensor_reduce(out.ap, in_.ap, axis, op), [out], [in_])

    def recip(self, out, in_):
        return self.op("dve", lambda: self.nc.vector.reciprocal(out.ap, in_.ap), [out], [in_])


import contextlib
from concourse.bass_utils import run_bass_kernel_spmd

D = 1024
DFF = 4096
ABIN = 4104
EPS = 1e-6
GN_EPS = 64e-5
NCORES = 8


class Ring:
    def __init__(self, S, st, name, shape, dt, n, psum=False):
        mk = S.psum if psum else S.sbuf
        self.tiles = [mk(st, "%s%d" % (name, i), shape, dt) for i in range(n)]
        self.i = 0

    def next(self):
        t = self.tiles[self.i % len(self.tiles)]
        self.i += 1
        return t


def dv(ap):
    return V(ap, (Buf("d"),))


class Ctx:
    pass


C_ID, C_LE, C_LT, C_GT, C_LE64, C_BD, C_ONES, C_SEL0, C_SEL1 = range(9)
NCONST = 9


def make_consts():
    p = np.arange(128)[:, None]
    f = np.arange(128)[None, :]
    blocks = [
        (p == f), (p <= f), (p < f), (p > f),
        (p <= f) & (p // 64 == f // 64), (p // 64 == f // 64),
        np.ones((128, 128), bool), (p // 64 == 0) & (f >= 0), (p // 64 == 1) & (f >= 0),
    ]
    return np.concatenate([b.astype(np.float32) for b in blocks], axis=1)


def load_w(S, dst, src_ap, K, N, q="pool"):
    src = src_ap.rearrange("(k p) n -> p k n", p=128)
    for k in range(K):
        for c0 in range(0, N, 1024):
            c1 = min(N, c0 + 1024)
            S.dma(q, dst[:, k, c0:c1], dv(src[:, k, c0:c1]))


def bload(S, dstv, src_row_ap, n):
    S.dma("sp", dstv, dv(src_row_ap.broadcast_to([128, n])))


def cblk(t, i):
    return t[:, i * 128:(i + 1) * 128]


def rms_rstd(S, C, ss, rstd, n, eps):
    S.ts("dve", rstd, ss, 1.0 / n, ALU.mult, eps, ALU.add)
    S.act(rstd, rstd, AF.Ln)
    S.act(rstd, rstd, AF.Exp, scale=-0.5)


def drive(a, b):
    gens = [g for g in (a, b) if g is not None]
    while gens:
        for g in list(gens):
            try:
                next(g)
            except StopIteration:
                gens.remove(g)


def run_pipelined(tiles, front, back):
    prev = None
    for t in tiles:
        X = {}
        drive(prev, front(t, X))
        prev = back(t, X)
    drive(prev, None)


def bcv(v, shape):
    return V(v.ap.unsqueeze(2).broadcast_to(list(shape)), v.bufs)


def phase_A(C):
    S = C.S
    T, NSEQ = C.T, C.NSEQ
    NT = T // 128
    cf, cb = C.cf, C.cb
    with contextlib.ExitStack() as st:
        w_in = S.sbuf(st, "w_in", [128, 8, ABIN], BF16)
        load_w(S, w_in, C.ab_w_in[0], 8, ABIN)
        gb = S.sbuf(st, "gbA", [128, D], F32)
        bload(S, gb.v(), C.norm_mix_g[0:1, :], D)
        G = S.sbuf(st, "G", [128, 3, 512], F32)
        for l in range(3):
            bload(S, G[:, l, :], C.hgrn_lb[l:l + 1, :], 512)
        lbb = S.sbuf(st, "lbb", [128, 512], F32)
        oml = S.sbuf(st, "oml", [128, 512], F32)
        S.act(G.v(), G.v(), AF.Exp)
        S.tt("dve", lbb.v(), G[:, 0, :], G[:, 1, :], ALU.add)
        S.tt("dve", lbb.v(), lbb.v(), G[:, 2, :], ALU.add)
        S.recip(lbb.v(), lbb.v())
        S.tt("dve", lbb.v(), lbb.v(), G[:, 0, :], ALU.mult)
        S.ts("dve", oml.v(), lbb.v(), -1.0, ALU.mult, 1.0, ALU.add)
        hng = S.sbuf(st, "hng", [128, 512], F32)
        bload(S, hng.v(), C.hgrn_norm_g[0:1, :], 512)
        fqg = S.sbuf(st, "fqg", [128, 8, 64], F32)
        fkg = S.sbuf(st, "fkg", [128, 8, 64], F32)
        for h in range(8):
            bload(S, fqg[:, h, :], C.fox_q_g[0:1, :], 64)
            bload(S, fkg[:, h, :], C.fox_k_g[0:1, :], 64)
        fbb = S.sbuf(st, "fbb", [128, 8], F32)
        bload(S, fbb.v(), C.fox_fb[0:1, :], 8)
        m64x4 = S.sbuf(st, "m64x4", [128, 4, 128], F32)
        for h in range(4):
            S.copy("pool", m64x4[:, h, :], cblk(cf, C_LE64))
        Sf = S.sbuf(st, "Sf", [128, 512], F32)
        SbA = S.sbuf(st, "SbA", [128, 512], BF16)
        SbB = S.sbuf(st, "SbB", [128, 512], BF16)
        ctot = S.sbuf(st, "ctot", [128, 8], F32)
        r_x = Ring(S, st, "xA", [128, D], F32, 2)
        r_junk = Ring(S, st, "junkA", [128, D], BF16, 1)
        r_stF = Ring(S, st, "stAF", [128, 8], F32, 6)
        r_stB = Ring(S, st, "stAB", [128, 8], F32, 2)
        r_hn = Ring(S, st, "hnA", [128, D], BF16, 2)
        r_hnT = Ring(S, st, "hnTA", [128, 8, 128], BF16, 2)
        r_qs = Ring(S, st, "qsA", [128, 512], F32, 2)
        r_gl = Ring(S, st, "glA", [128, 512], F32, 2)
        r_kf = Ring(S, st, "kfA", [128, 512], F32, 2)
        r_sg = Ring(S, st, "sgA", [128, 512], F32, 2)
        r_vt = Ring(S, st, "vtA", [128, 512], BF16, 2)
        r_fF = Ring(S, st, "fAF", [128, 512], F32, 4)
        r_fB = Ring(S, st, "fAB", [128, 512], F32, 6)
        r_bF = Ring(S, st, "bAF", [128, 512], BF16, 4)
        r_bB = Ring(S, st, "bAB", [128, 512], BF16, 4)
        r_qtT = Ring(S, st, "qtT", [128, 4, 128], BF16, 2)
        r_ktT = Ring(S, st, "ktT", [128, 4, 128], BF16, 2)
        r_q0 = Ring(S, st, "qtT0", [128, 4, 128], BF16, 2)
        r_q1 = Ring(S, st, "qtT1", [128, 4, 128], BF16, 2)
        for t in r_q0.tiles + r_q1.tiles:
            S.memset("pool", t.v(), 0.0)
        r_fT = Ring(S, st, "fT", [128, 4, 128], BF16, 4)
        tpF = Ring(S, st, "tpAF", [128, 1024], BF16, 1, psum=True)
        ppF = Ring(S, st, "ppAF", [128, 512], F32, 2, psum=True)
        gpF = Ring(S, st, "gpAF", [128, 512], F32, 1, psum=True)
        tpB = Ring(S, st, "tpAB", [128, 1024], BF16, 2, psum=True)
        gpB = Ring(S, st, "gpAB", [128, 512], F32, 2, psum=True)
        ID = cblk(cb, C_ID)

        def front(tile, X):
            s, i = tile
            r0 = s * T + i * 128
            if i == 0:
                S.memset("pool", ctot.v(), 0.0)
            xt = r_x.next()
            S.dma("sp", xt.v(), dv(C.x[r0:r0 + 128, :]))
            junk = r_junk.next()
            sv = r_stF.next()
            S.act(junk.v(), xt.v(), AF.Square, accum_out=sv[:, 0:1])
            rms_rstd(S, C, sv[:, 0:1], sv[:, 1:2], D, EPS)
            hn = r_hn.next()
            S.stt(hn.v(), xt.v(), sv[:, 1:2], gb.v(), ALU.mult, ALU.mult)
            yield
            pT = tpF.next()
            for k in range(8):
                S.tr(pT[:, k * 128:(k + 1) * 128], hn[:, k * 128:(k + 1) * 128], ID)
            hnT = r_hnT.next()
            S.copy("act", hnT.v().rr("p k t -> p (k t)"), pT.v())
            yield

            def proj(og, n=512):
                ps = ppF.next()
                for k in range(8):
                    S.mm(ps[:, 0:n], hnT[:, k, :], w_in[:, k, og * 512:og * 512 + n], start=(k == 0), stop=(k == 7))
                return ps
            ps = proj(0)
            qs = r_qs.next()
            S.act(qs.v(), ps.v(), AF.Sigmoid)
            S.tt("dve", qs.v(), ps.v(), qs.v(), ALU.mult)
            yield
            ps = proj(1)
            f = r_fF.next()
            S.act(f.v(), ps.v(), AF.Sigmoid)
            S.tt("dve", f.v(), f.v(), oml.v(), ALU.mult)
            S.tt("dve", f.v(), f.v(), lbb.v(), ALU.add)
            yield
            ps = proj(3)
            sgate = r_sg.next()
            S.act(sgate.v(), ps.v(), AF.Sigmoid)
            S.tt("dve", sgate.v(), ps.v(), sgate.v(), ALU.mult)
            yield
            ps = proj(7)
            fgt = r_bF.next()
            S.act(fgt.v(), ps.v(), AF.Sigmoid)
            S.dma("sp", dv(C.fg[r0:r0 + 128, :]), fgt.v())
            ps = proj(8, 8)
            lf = r_stF.next()
            S.tt("dve", lf.v(), ps[:, 0:8], fbb.v(), ALU.add)
            S.act(lf.v(), lf.v(), AF.Sigmoid)
            S.act(lf.v(), lf.v(), AF.Ln)
            gl = r_gl.next()
            S.act(gl.v(), f.v(), AF.Ln)
            kf = r_kf.next()
            S.ts("pool", kf.v(), f.v(), -1.0, ALU.mult, 1.0, ALU.add)
            yield
            cps = gpF.next()
            S.mm(cps[:, 0:8], cblk(cf, C_LE), lf.v())
            S.mm(cps[:, 8:16], cblk(cf, C_ONES), lf.v())
            cs = r_stF.next()
            S.tt("dve", cs.v(), cps[:, 0:8], ctot.v(), ALU.add)
            S.dma("sp", dv(C.fc[r0:r0 + 128, :]), cs.v())
            S.tt("dve", ctot.v(), ctot.v(), cps[:, 8:16], ALU.add)
            ps = proj(2)
            vt = r_vt.next()
            S.copy("act", vt.v(), ps.v())
            X.update(qs=qs, gl=gl, kf=kf, sgate=sgate, vt=vt, r0=r0, first=(i == 0))
            yield

            for og, gtile, dst_dram in ((4, fqg, C.fq), (5, fkg, C.fk)):
                ps = proj(og)
                bq = r_fF.next()
                S.copy("act", bq.v(), ps.v())
                sq = r_fF.next()
                S.tt("pool", sq.v(), bq.v(), bq.v(), ALU.mult)
                sv2 = r_stF.next()
                S.reduce(sv2.v(), sq.v().rr("p (h d) -> p h d", h=8))
                rms_rstd(S, C, sv2.v(), sv2.v(), 64, EPS)
                S.tt("dve", bq.v().rr("p (h d) -> p h d", h=8), bq.v().rr("p (h d) -> p h d", h=8),
                     bcv(sv2.v(), [128, 8, 64]), ALU.mult)
                qn = r_bF.next()
                S.tt("pool", qn.v(), bq.v(), gtile.v().rr("p h d -> p (h d)"), ALU.mult)
                yield
                pq = tpF.next()
                for hp in range(4):
                    S.tr(pq[:, hp * 128:(hp + 1) * 128], qn[:, hp * 128:(hp + 1) * 128], ID)
                qT = r_fT.next()
                S.copy("dve", qT.v().rr("p a t -> p (a t)"), pq[:, 0:512])
                S.dma("sp", dv(dst_dram[s, :, :, i * 128:(i + 1) * 128].rearrange("a p t -> p a t")), qT.v())
                yield
            ps = proj(6)
            fvt = r_bF.next()
            S.copy("act", fvt.v(), ps.v())
            S.dma("sp", dv(C.fv[r0:r0 + 128, :]), fvt.v())

        def back(tile, X):
            qs, gl, kf, sgate, vt, r0 = (X[k] for k in ("qs", "gl", "kf", "sgate", "vt", "r0"))
            if X["first"]:
                S.memset("pool", Sf.v(), 0.0)
                S.memset("pool", SbA.v(), 0.0)
            bps = gpB.next()
            S.mm(bps.v(), cblk(cf, C_LE64), gl.v())
            eb = r_fB.next()
            S.act(eb.v(), bps.v(), AF.Exp)
            enb = r_fB.next()
            S.act(enb.v(), bps.v(), AF.Exp, scale=-1.0)
            qt = r_bB.next()
            S.tt("dve", qt.v(), qs.v(), eb.v(), ALU.mult)
            kt = r_bB.next()
            S.tt("pool", kt.v(), kf.v(), enb.v(), ALU.mult)
            yield
            ebl = []
            for c in range(2):
                eps_ = gpB.next()
                for h in range(4):
                    S.mm(eps_[:, h * 128:(h + 1) * 128], gl[:, h * 128:(h + 1) * 128], cblk(cf, C_SEL0 + c))
                e_ = r_fB.next()
                S.act(e_.v(), eps_.v(), AF.Exp)
                ebl.append(e_)
            yield
            pqq = tpB.next()
            pkk = tpB.next()
            for h in range(4):
                S.tr(pqq[:, h * 128:(h + 1) * 128], qt[:, h * 128:(h + 1) * 128], ID)
            for h in range(4):
                S.tr(pkk[:, h * 128:(h + 1) * 128], kt[:, h * 128:(h + 1) * 128], ID)
            qtT = r_qtT.next()
            ktT = r_ktT.next()
            q0 = r_q0.next()
            q1 = r_q1.next()
            S.copy("act", qtT.v().rr("p h t -> p (h t)"), pqq[:, 0:512])
            S.copy("dve", ktT.v().rr("p h t -> p (h t)"), pkk[:, 0:512])
            pq3 = pqq[:, 0:512].rr("p (h t) -> p h t", h=4)
            S.copy("act", q0[:, :, 0:64], pq3[:, :, 0:64])
            S.copy("act", q1[:, :, 64:128], pq3[:, :, 64:128])
            yield
            scp = gpB.next()
            for h in range(4):
                S.mm(scp[:, h * 128:(h + 1) * 128], ktT[:, h, :], qtT[:, h, :])
            scT = r_bB.next()
            S.tt("dve", scT.v(), scp.v(), m64x4.v().rr("p h t -> p (h t)"), ALU.mult)
            yield

            def state_update(c, dstb):
                ups = gpB.next()
                for h in range(4):
                    S.mm(ups[:, h * 128:(h + 1) * 128], kt[c * 64:(c + 1) * 64, h * 128:(h + 1) * 128],
                         vt[c * 64:(c + 1) * 64, h * 128:(h + 1) * 128])
                S.tt("dve", Sf.v(), Sf.v(), ups.v(), ALU.add)
                S.tt("pool", Sf.v(), Sf.v(), ebl[c].v(), ALU.mult)
                S.copy("act", dstb.v(), Sf.v())
            state_update(0, SbB)
            yield
            ops = gpB.next()
            for h in range(4):
                hs = slice(h * 128, (h + 1) * 128)
                S.mm(ops[:, hs], scT[:, hs], vt[:, hs], start=True, stop=False)
                S.mm(ops[:, hs], q0[:, h, :], SbA[:, hs], start=False, stop=False)
                S.mm(ops[:, hs], q1[:, h, :], SbB[:, hs], start=False, stop=True)
            yield
            state_update(1, SbA)
            osb = r_fB.next()
            S.copy("act", osb.v(), ops.v())
            sq = r_fB.next()
            S.tt("pool", sq.v(), osb.v(), osb.v(), ALU.mult)
            yield
            sv3 = r_stB.next()
            S.reduce(sv3[:, 0:4], sq.v().rr("p (h d) -> p h d", h=4))
            rms_rstd(S, C, sv3[:, 0:4], sv3[:, 0:4], 128, EPS)
            S.tt("dve", osb.v().rr("p (h d) -> p h d", h=4), osb.v().rr("p (h d) -> p h d", h=4),
                 bcv(sv3[:, 0:4], [128, 4, 128]), ALU.mult)
            S.tt("pool", osb.v(), osb.v(), hng.v(), ALU.mult)
            yat = r_bB.next()
            S.tt("dve", yat.v(), osb.v(), sgate.v(), ALU.mult)
            S.dma("sp", dv(C.ya[r0:r0 + 128, :]), yat.v())

        tiles = [(s, i) for s in range(NSEQ) for i in range(NT)]
        run_pipelined(tiles, front, back)
        S.barrier()


def phase_B(C):
    S = C.S
    T, NSEQ = C.T, C.NSEQ
    NT = T // 128
    QG = min(4, NT)
    NG = NT // QG
    cf, cb = C.cf, C.cb
    with contextlib.ExitStack() as st:
        r_kT = Ring(S, st, "kTB", [128, 2, T], BF16, 2)
        for t in r_kT.tiles:
            S.memset("pool", t[64:128, 0, :], 0.0)
            S.memset("pool", t[0:64, 1, :], 0.0)
        r_qT = Ring(S, st, "qTB", [128, T], BF16, 2)
        r_v = Ring(S, st, "vB", [128, NT, 2, 65], BF16, 2)
        for t in r_v.tiles:
            S.memset("pool", t.v(), 1.0)
        r_g = Ring(S, st, "gB", [128, NT, 128], BF16, 2)
        r_y = Ring(S, st, "yB", [128, NT, 128], BF16, 2)
        cT = S.sbuf(st, "cT", [128, NT, 8], F32)
        r_cref = Ring(S, st, "cref", [128, 8], F32, 2)
        r_bias = Ring(S, st, "biasB", [128, NT], F32, 3)
        r_p = Ring(S, st, "pB", [128, QG * 128], BF16, 6)
        r_rc = Ring(S, st, "rcB", [128, QG], F32, 2)
        lemask = S.sbuf(st, "lemaskB", [128, 128], BF16)
        S.copy("pool", lemask.v(), cblk(cf, C_LE))
        stp = Ring(S, st, "stB", [128, 512], F32, 4, psum=True)
        accp = Ring(S, st, "accB", [128, 512], F32, 2, psum=True)
        for s in range(NSEQ):
            S.dma("sp", cT.v(), dv(C.fc[s * T:(s + 1) * T, :].rearrange("(b p) h -> p b h", p=128)))
            for hp in range(4):
                kT = r_kT.next()
                qT = r_qT.next()
                vt = r_v.next()
                gt = r_g.next()
                yt = r_y.next()
                S.dma("sp", kT[0:64, 0, :], dv(C.fk[s, hp, 0:64, :]))
                S.dma("sp", kT[64:128, 1, :], dv(C.fk[s, hp, 64:128, :]))
                S.dma("sp", qT.v(), dv(C.fq[s, hp, :, :]))
                for hh_ in range(2):
                    c0_ = hp * 128 + hh_ * 64
                    S.dma("sp", vt[:, :, hh_, 0:64],
                          dv(C.fv[s * T:(s + 1) * T, c0_:c0_ + 64].rearrange("(b p) d -> p b d", p=128)))
                S.dma("sp", gt.v(),
                      dv(C.fg[s * T:(s + 1) * T, hp * 128:(hp + 1) * 128].rearrange("(b p) d -> p b d", p=128)))
                items = []
                for qg in range(NG):
                    for hh in range(2):
                        for j in range(qg * QG + QG):
                            items.append((qg, hh, j))
                LOOK = 2
                stq = {}

                def emit_st(idx):
                    qg_, hh_, j_ = items[idx]
                    i0_ = qg_ * QG
                    pb_ = hh_ * 64
                    ilo_ = max(j_, i0_)
                    ncol_ = (i0_ + QG - ilo_) * 128
                    sp_ = stp.next()
                    S.mm(sp_[:, 0:ncol_], kT[:, hh_, j_ * 128:(j_ + 1) * 128],
                         qT[:, ilo_ * 128:(i0_ + QG) * 128])
                    stq[idx] = (sp_, ncol_, ilo_)
                for idx in range(min(LOOK, len(items))):
                    emit_st(idx)
                cref = None
                for idx, (qg, hh, j) in enumerate(items):
                    if idx + LOOK < len(items):
                        emit_st(idx + LOOK)
                    i0 = qg * QG
                    nj = i0 + QG
                    h = hp * 2 + hh
                    pb = hh * 64
                    if j == 0:
                        if hh == 0:
                            tok_ref = s * T + i0 * 128 + (QG * 128) // 2 - 1
                            cref = r_cref.next()
                            bload(S, cref.v(), C.fc[tok_ref:tok_ref + 1, :], 8)
                        bias = r_bias.next()
                        S.ts("dve", bias[:, 0:nj], cT[:, 0:nj, h], cref[:, h:h + 1], ALU.subtract, -1.0, ALU.mult)
                        acc = accp.next()[:, 0:QG * 65].rr("p (a d) -> p a d", a=QG)
                        first = True
                    sp_, ncol, ilo = stq.pop(idx)
                    pt = r_p.next()
                    S.act(pt[:, 0:ncol], sp_[:, 0:ncol], AF.Exp, bias=bias[:, j:j + 1], scale=0.125)
                    if j >= i0:
                        S.tt("dve", pt[:, 0:128], pt[:, 0:128], lemask.v(), ALU.mult)
                    for qi in range(ilo, i0 + QG):
                        S.mm(acc[:, qi - i0, :], pt[:, (qi - ilo) * 128:(qi - ilo + 1) * 128], vt[:, j, hh, :],
                             start=first, stop=(j == qi), skip_group_check=True)
                        first = False
                    if j == nj - 1:
                        rc = r_rc.next()
                        S.recip(rc.v(), acc[:, :, 64])
                        for qi in range(QG):
                            S.stt(yt[:, i0 + qi, pb:pb + 64], acc[:, qi, 0:64], rc[:, qi:qi + 1],
                                  gt[:, i0 + qi, pb:pb + 64], ALU.mult, ALU.mult)
                S.dma("sp", dv(C.yb[s * T:(s + 1) * T, hp * 128:(hp + 1) * 128].rearrange("(b p) d -> p b d", p=128)),
                      yt.v())
        S.barrier()


class MlpRings:
    def __init__(self, S, st, sfx, n_h):
        self.st = Ring(S, st, "stM" + sfx, [128, 8], F32, 4)
        self.hn = Ring(S, st, "hnM" + sfx, [128, D], BF16, 1)
        self.hnT = Ring(S, st, "hnTM" + sfx, [128, 8, 128], BF16, 1)
        self.hid = Ring(S, st, "hidM" + sfx, [128, 32, 128], BF16, 2)
        self.rl = Ring(S, st, "rlM" + sfx, [128, 512], BF16, 3)
        self.h = Ring(S, st, "hM" + sfx, [128, D], F32, n_h)
        self.o = Ring(S, st, "oM" + sfx, [128, D], F32, 1)
        self.tp = Ring(S, st, "tpM" + sfx, [128, 1024], BF16, 2, psum=True)
        self.ppF = Ring(S, st, "ppMF" + sfx, [128, 512], F32, 4, psum=True)
        self.ppB = Ring(S, st, "ppMB" + sfx, [128, 512], F32, 2, psum=True)


def mlp_front(C, R, h, gbf, w_up, hid):
    S = C.S
    ID = cblk(C.cb, C_ID)
    sv = R.st.next()
    hn = R.hn.next()
    S.act(hn.v(), h.v(), AF.Square, accum_out=sv[:, 0:1])
    rms_rstd(S, C, sv[:, 0:1], sv[:, 1:2], D, EPS)
    S.stt(hn.v(), h.v(), sv[:, 1:2], gbf.v(), ALU.mult, ALU.mult)
    yield
    pT = R.tp.next()
    for k in range(8):
        S.tr(pT[:, k * 128:(k + 1) * 128], hn[:, k * 128:(k + 1) * 128], ID)
    hnT = R.hnT.next()
    S.copy("act", hnT.v().rr("p k t -> p (k t)"), pT.v())
    yield
    for fg in range(8):
        ps = R.ppF.next()
        for fc in range(4):
            fcol = (fg * 4 + fc) * 128
            for k in range(8):
                S.mm(ps[:, fc * 128:(fc + 1) * 128], w_up[:, k, fcol:fcol + 128], hnT[:, k, :],
                     start=(k == 0), stop=(k == 7))
            if fc % 2 == 1 and fc < 3:
                yield
        rl = R.rl.next()
        S.act(rl.v(), ps.v(), AF.Relu)
        S.tt("pool" if fg % 2 == 0 else "dve", hid[:, fg * 4:(fg + 1) * 4, :].rr("p a t -> p (a t)"), rl.v(), rl.v(), ALU.mult)
        yield


def mlp_back(C, R, h, w_down, hid, r0):
    S = C.S
    o = R.o.next()
    for n in range(2):
        ps = R.ppB.next()
        for kf in range(32):
            S.mm(ps.v(), hid[:, kf, :], w_down[:, kf, n * 512:(n + 1) * 512], start=(kf == 0), stop=(kf == 31))
            if kf % 8 == 7 and kf < 31:
                yield
        S.tt("dve", o[:, n * 512:(n + 1) * 512], ps.v(), h[:, n * 512:(n + 1) * 512], ALU.add)
        yield
    S.dma("sp", dv(C.out[r0:r0 + 128, :]), o.v())


def phase_C(C):
    S = C.S
    T, NSEQ = C.T, C.NSEQ
    NT = T // 128
    cb = C.cb
    with contextlib.ExitStack() as st:
        w_out = S.sbuf(st, "w_out", [128, 8, D], BF16)
        load_w(S, w_out, C.ab_w_out[0], 8, D)
        w_up = S.sbuf(st, "w_up", [128, 8, DFF], BF16)
        load_w(S, w_up, C.mlp_w_up[0], 8, DFF)
        w_down = S.sbuf(st, "w_down", [128, 32, D], BF16)
        load_w(S, w_down, C.mlp_w_down[0], 32, D)
        gbf = S.sbuf(st, "gbfC", [128, D], F32)
        bload(S, gbf.v(), C.norm_ffn_g[0:1, :], D)
        R = MlpRings(S, st, "C", 2)
        r_x = Ring(S, st, "xC", [128, D], F32, 1)
        r_y = Ring(S, st, "yC", [128, D], BF16, 1)
        r_yT = Ring(S, st, "yTC", [128, 8, 128], BF16, 1)
        ID = cblk(cb, C_ID)

        def front(tile, X):
            s, i = tile
            r0 = s * T + i * 128
            xt = r_x.next()
            S.dma("sp", xt.v(), dv(C.x[r0:r0 + 128, :]))
            y = r_y.next()
            S.dma("sp", y.v(), dv(C.sB[r0:r0 + 128, :]))
            pT = R.tp.next()
            for k in range(8):
                S.tr(pT[:, k * 128:(k + 1) * 128], y[:, k * 128:(k + 1) * 128], ID)
            yT = r_yT.next()
            S.copy("act", yT.v().rr("p k t -> p (k t)"), pT.v())
            yield
            h1 = R.h.next()
            for n in range(2):
                ps = R.ppF.next()
                for k in range(8):
                    S.mm(ps.v(), yT[:, k, :], w_out[:, k, n * 512:(n + 1) * 512], start=(k == 0), stop=(k == 7))
                S.tt("dve", h1[:, n * 512:(n + 1) * 512], ps.v(), xt[:, n * 512:(n + 1) * 512], ALU.add)
                yield
            hid = R.hid.next()
            X.update(h=h1, hid=hid, r0=r0)
            yield from mlp_front(C, R, h1, gbf, w_up, hid)

        def back(tile, X):
            yield from mlp_back(C, R, X["h"], w_down, X["hid"], X["r0"])

        tiles = [(s, i) for s in range(NSEQ) for i in range(NT)]
        run_pipelined(tiles, front, back)
        S.barrier()


import math


def phase_D(C):
    S = C.S
    T, NSEQ = C.T, C.NSEQ
    NT = T // 128
    cf, cb = C.cf, C.cb
    A = C.aps
    ID = cblk(cb, C_ID)
    with contextlib.ExitStack() as st:
        w_r = S.sbuf(st, "w_r", [128, 8, D], BF16)
        w_k = S.sbuf(st, "w_k", [128, 8, D], BF16)
        w_v = S.sbuf(st, "w_v", [128, 8, D], BF16)
        load_w(S, w_r, A["rwkv_w_rkv"][0, 0], 8, D)
        load_w(S, w_k, A["rwkv_w_rkv"][0, 1], 8, D)
        load_w(S, w_v, A["rwkv_w_rkv"][0, 2], 8, D)
        w1 = S.sbuf(st, "w1", [128, 8, 64], BF16)
        a1 = S.sbuf(st, "a1", [128, 8, 64], BF16)
        g1 = S.sbuf(st, "g1", [128, 8, 128], BF16)
        load_w(S, w1, A["rwkv_w1"][0], 8, 64)
        load_w(S, a1, A["rwkv_a1"][0], 8, 64)
        load_w(S, g1, A["rwkv_g1"][0], 8, 128)
        w2 = S.sbuf(st, "w2", [128, D], BF16)
        a2 = S.sbuf(st, "a2", [128, D], BF16)
        g2 = S.sbuf(st, "g2", [128, D], BF16)
        S.dma("pool", w2[0:64, :], dv(A["rwkv_w2"][0]))
        S.dma("pool", a2[0:64, :], dv(A["rwkv_a2"][0]))
        S.dma("pool", g2.v(), dv(A["rwkv_g2"][0]))
        bc_ = {}
        for nm, ap in (("gmix", C.norm_mix_g[1:2, :]), ("w0b", A["rwkv_w0"][0:1, :]), ("a0b", A["rwkv_a0"][0:1, :]),
                       ("k_kb", A["rwkv_k_k"][0:1, :]), ("k_ab", A["rwkv_k_a"][0:1, :]),
                       ("r_kb", A["rwkv_r_k"].rearrange("a h n -> a (h n)"))):
            t_ = S.sbuf(st, nm, [128, D], F32)
            bload(S, t_.v(), ap, D)
            bc_[nm] = t_
        gmix, w0b, a0b, k_kb, k_ab, r_kb = (bc_[n] for n in ("gmix", "w0b", "a0b", "k_kb", "k_ab", "r_kb"))
        tpF = Ring(S, st, "tpD", [128, 1024], BF16, 2, psum=True)
        ppF = Ring(S, st, "ppD", [128, 512], F32, 2, psum=True)
        gpF = Ring(S, st, "gpDF", [128, 512], F32, 1, psum=True)
        gpB = Ring(S, st, "gpDB", [128, 512], F32, 3, psum=True)
        mu_rows = S.sbuf(st, "mu_rows", [128, D], F32)
        S.dma("sp", mu_rows[0:6, :], dv(A["rwkv_mu"][0]))
        mu_kc = S.sbuf(st, "mu_kc", [128, 8, 6], F32)
        mps = gpB.next()
        for k in range(8):
            S.tr(mps[:, k * 6:(k + 1) * 6], mu_rows[0:6, k * 128:(k + 1) * 128], cblk(cf, C_ID)[0:6, 0:6])
        S.copy("dve", mu_kc.v().rr("p k m -> p (k m)"), mps[:, 0:48])

        r_x = Ring(S, st, "xD", [128, D], F32, 2)
        r_junk = Ring(S, st, "junkD", [128, D], BF16, 1)
        r_stF = Ring(S, st, "stDF", [128, 16], F32, 4)
        r_stB = Ring(S, st, "stDB", [128, 16], F32, 6)
        r_fF = Ring(S, st, "fDF", [128, D], F32, 2)
        r_fB = Ring(S, st, "fDB", [128, D], F32, 3)
        r_rf = Ring(S, st, "rfD", [128, D], F32, 2)
        r_kf = Ring(S, st, "kfD", [128, D], F32, 2)
        r_lw = Ring(S, st, "lwD", [128, D], F32, 2)
        r_af = Ring(S, st, "afD", [128, D], F32, 2)
        r_e = Ring(S, st, "eD", [128, 512], F32, 4)
        r_b = Ring(S, st, "bD", [128, D], BF16, 6)
        r_mix = Ring(S, st, "mixD", [128, 8, 128], BF16, 6)
        r_o = Ring(S, st, "oD", [128, D], BF16, 8)
        r_s = Ring(S, st, "sD", [128, 128], BF16, 4)
        NEG = -math.exp(-0.5)

        def front(tile, X):
            s, i = tile
            r0 = s * T + i * 128
            xc = r_x.next()
            S.dma("sp", xc.v(), dv(C.out[r0:r0 + 128, :]))
            xp = r_x.next()
            if i == 0:
                S.memset("pool", xp[0:1, :], 0.0)
                S.dma("sp", xp[1:128, :], dv(C.out[r0:r0 + 127, :]))
            else:
                S.dma("sp", xp.v(), dv(C.out[r0 - 1:r0 + 127, :]))
            hc = r_fF.next()
            hp = r_fF.next()
            for xt_, ht_ in ((xc, hc), (xp, hp)):
                junk = r_junk.next()
                sv = r_stF.next()
                S.act(junk.v(), xt_.v(), AF.Square, accum_out=sv[:, 0:1])
                rms_rstd(S, C, sv[:, 0:1], sv[:, 1:2], D, EPS)
                S.stt(ht_.v(), xt_.v(), sv[:, 1:2], gmix.v(), ALU.mult, ALU.mult)
                yield
            hcb = r_b.next()
            xxb = r_b.next()
            S.copy("act", hcb.v(), hc.v())
            S.tt("pool", xxb.v(), hp.v(), hc.v(), ALU.subtract)
            pH = tpF.next()
            pX = tpF.next()
            for k in range(8):
                S.tr(pH[:, k * 128:(k + 1) * 128], hcb[:, k * 128:(k + 1) * 128], ID)
            for k in range(8):
                S.tr(pX[:, k * 128:(k + 1) * 128], xxb[:, k * 128:(k + 1) * 128], ID)
            hcT = r_b.next()
            xxT = r_b.next()
            S.copy("act", hcT.v(), pH.v())
            S.copy("dve", xxT.v(), pX.v())
            yield
            mixT = []
            for m in range(6):
                mt = r_mix.next()
                e = "pool" if m % 2 == 0 else "dve"
                S.tt(e, mt.v(), xxT.v().rr("p (k t) -> p k t", k=8), bcv(mu_kc[:, :, m], [128, 8, 128]), ALU.mult)
                S.tt(e, mt.v(), mt.v(), hcT.v().rr("p (k t) -> p k t", k=8), ALU.add)
                mixT.append(mt)
                if m % 2 == 1:
                    yield
            xr, xw, xk, xv, xa, xg = mixT

            def proj(w, xT, dst):
                for n in range(2):
                    ps = ppF.next()
                    for k in range(8):
                        S.mm(ps.v(), xT[:, k, :], w[:, k, n * 512:(n + 1) * 512], start=(k == 0), stop=(k == 7))
                    S.copy("act", dst[:, n * 512:(n + 1) * 512], ps.v())
            rf = r_rf.next()
            proj(w_r, xr, rf)
            yield
            kf = r_kf.next()
            proj(w_k, xk, kf)
            yield
            vb = r_b.next()
            proj(w_v, xv, vb)
            S.dma("sp", dv(C.sV[r0:r0 + 128, :]), vb.v())
            yield

            def lora(w_a, xT, w_b, nj, func, bias_t, dst, final_func):
                lp = gpF.next()
                for k in range(8):
                    S.mm(lp[0:nj, 0:128], w_a[:, k, :], xT[:, k, :], start=(k == 0), stop=(k == 7))
                th = r_s.next()
                S.act(th[0:nj, :], lp[0:nj, 0:128], func)
                for n in range(2):
                    ps = ppF.next()
                    S.mm(ps.v(), th[0:nj, :], w_b[0:nj, n * 512:(n + 1) * 512])
                    if bias_t is not None:
                        S.tt("dve", dst[:, n * 512:(n + 1) * 512], ps.v(), bias_t[:, n * 512:(n + 1) * 512], ALU.add)
                    else:
                        S.copy("act", dst[:, n * 512:(n + 1) * 512], ps.v())
                if final_func is not None:
                    S.act(dst.v(), dst.v(), final_func)
            lw = r_lw.next()
            lora(w1, xw, w2, 64, AF.Tanh, w0b, lw, AF.Sigmoid)
            S.ts("pool", lw.v(), lw.v(), NEG, ALU.mult, 0.0, ALU.add)
            yield
            af = r_af.next()
            lora(a1, xa, a2, 64, AF.Copy, a0b, af, AF.Sigmoid)
            yield
            gb_ = r_b.next()
            lora(g1, xg, g2, 128, AF.Sigmoid, None, gb_, None)
            S.dma("sp", dv(C.sG[r0:r0 + 128, :]), gb_.v())
            X.update(rf=rf, kf=kf, lw=lw, af=af, r0=r0, ti=s * NT + i)

        def back(tile, X):
            rf, kf, lw, af, r0 = (X[k] for k in ("rf", "kf", "lw", "af", "r0"))
            kk = r_fB.next()
            S.tt("dve", kk.v(), kf.v(), k_kb.v(), ALU.mult)
            sq = r_fB.next()
            S.tt("pool", sq.v(), kk.v(), kk.v(), ALU.mult)
            yield
            sv = r_stB.next()
            S.reduce(sv.v(), sq.v().rr("p (h d) -> p h d", h=16))
            S.ts("dve", sv.v(), sv.v(), 1e-24, ALU.max)
            S.act(sv.v(), sv.v(), AF.Ln)
            S.act(sv.v(), sv.v(), AF.Exp, scale=-0.5)
            S.tt("dve", kk.v().rr("p (h d) -> p h d", h=16), kk.v().rr("p (h d) -> p h d", h=16),
                 bcv(sv.v(), [128, 16, 64]), ALU.mult)
            yield
            S.stt(sq.v(), af.v(), -1.0, k_ab.v(), ALU.add, ALU.mult)
            S.stt(kf.v(), sq.v(), 1.0, kf.v(), ALU.add, ALU.mult)
            yield
            t2 = r_fB.next()
            S.tt("pool", t2.v(), rf.v(), kf.v(), ALU.mult)
            S.tt("pool", t2.v(), t2.v(), r_kb.v(), ALU.mult)
            rkv = r_stB.next()
            S.reduce(rkv.v(), t2.v().rr("p (h d) -> p h d", h=16))
            S.dma("sp", dv(C.rk[r0:r0 + 128, :]), rkv.v())
            S.tt("dve", t2.v(), kk.v(), af.v(), ALU.mult)
            yield
            Rt = r_o.next()
            Kt = r_o.next()
            Bt = r_o.next()
            At = r_o.next()
            for n in range(2):
                hs = slice(n * 512, (n + 1) * 512)
                cw = gpB.next()
                cwx = gpB.next()
                S.mm(cw.v(), cblk(cf, C_LE), lw[:, hs])
                S.mm(cwx.v(), cblk(cf, C_LT), lw[:, hs])
                ep = r_e.next()
                S.act(ep.v(), cw.v(), AF.Exp)
                en = r_e.next()
                S.act(en.v(), cw.v(), AF.Exp, scale=-1.0)
                epx = r_e.next()
                S.act(epx.v(), cwx.v(), AF.Exp)
                yield
                S.tt("dve", Rt[:, hs], rf[:, hs], ep.v(), ALU.mult)
                S.tt("pool", Kt[:, hs], kf[:, hs], en.v(), ALU.mult)
                S.tt("dve", Bt[:, hs], t2[:, hs], en.v(), ALU.mult)
                S.stt(At[:, hs], kk[:, hs], -1.0, epx.v(), ALU.mult, ALU.mult)
                yield
            S.dma("sp", dv(C.sR[r0:r0 + 128, :]), Rt.v())
            S.dma("sp", dv(C.sK[r0:r0 + 128, :]), Kt.v())
            S.dma("sp", dv(C.sB[r0:r0 + 128, :]), Bt.v())
            S.dma("sp", dv(C.sA[r0:r0 + 128, :]), At.v())
            pcp = gpB.next()
            for p in range(8):
                S.mm(pcp[:, p * 2:(p + 1) * 2], lw[:, p * 128:(p + 1) * 128], cblk(cf, C_ONES)[:, 0:2])
            pct = r_stB.next()
            S.act(pct.v(), pcp[:, 0:16], AF.Exp)
            S.dma("sp", dv(C.pcs[X["ti"]]), pct.v())

        tiles = [(s, i) for s in range(NSEQ) for i in range(NT)]
        run_pipelined(tiles, front, back)
        S.barrier()


def phase_E(C):
    S = C.S
    T, NSEQ = C.T, C.NSEQ
    NT = T // 128
    cf, cb = C.cf, C.cb
    A = C.aps
    ID = cblk(cb, C_ID)
    with contextlib.ExitStack() as st:
        w_o = S.sbuf(st, "w_o", [128, 8, D], BF16)
        load_w(S, w_o, A["rwkv_w_o"][0], 8, D)
        lnxg = S.sbuf(st, "lnxg", [128, D], F32)
        lnxb = S.sbuf(st, "lnxb", [128, D], F32)
        bload(S, lnxg.v(), A["rwkv_lnx_g"][0:1, :], D)
        bload(S, lnxb.v(), A["rwkv_lnx_b"][0:1, :], D)
        lt2 = S.sbuf(st, "lt2", [128, 2, 128], F32)
        le2 = S.sbuf(st, "le2", [128, 2, 128], F32)
        gt2 = S.sbuf(st, "gt2", [128, 2, 128], F32)
        id4 = S.sbuf(st, "id4", [128, 4, 128], BF16)
        for j in range(2):
            S.copy("pool", lt2[:, j, :], cblk(cf, C_LT))
            S.copy("pool", le2[:, j, :], cblk(cf, C_LE))
            S.copy("pool", gt2[:, j, :], cblk(cf, C_GT))
        for j in range(4):
            S.copy("pool", id4[:, j, :], cblk(cb, C_ID))
        BD = cblk(cf, C_BD)
        Mf = S.sbuf(st, "Mf", [128, 8, 128], F32)
        Mb = S.sbuf(st, "Mb", [128, 8, 128], BF16)
        Apad = S.sbuf(st, "Apad", [128, 16, 128], BF16)
        S.memset("pool", Apad.v(), 0.0)
        AMf = S.sbuf(st, "AMf", [128, 16, 2, 128], BF16)
        r_amb = Ring(S, st, "AMb", [128, 16, 2, 128], BF16, 2)
        r_in = Ring(S, st, "inE", [128, D], BF16, 12)
        r_x = Ring(S, st, "xE", [128, D], F32, 2)
        r_rp = Ring(S, st, "rpE", [128, 16], F32, 4)
        r_sm = Ring(S, st, "smE", [128, 16], F32, 4)
        r_art = Ring(S, st, "artE", [128, 8, 2, 128], BF16, 2)
        r_bk = Ring(S, st, "bkE", [128, 8, 128], BF16, 2)
        r_aht = Ring(S, st, "ahtE", [128, 8, 128], BF16, 2)
        r_yT = Ring(S, st, "yTE", [128, 8, 128], BF16, 1)
        r_pq = Ring(S, st, "pqE", [128, 4, 128], BF16, 16)
        r_tt = Ring(S, st, "ttE", [128, 4, 128], BF16, 10)
        r_av = Ring(S, st, "avE", [128, D], BF16, 1)
        r_w = Ring(S, st, "wE", [128, D], BF16, 2)
        r_yfin = Ring(S, st, "yfinE", [128, D], BF16, 1)
        r_u = Ring(S, st, "uE", [128, 128], BF16, 4)
        r_tmp = Ring(S, st, "tmpE", [128, 128], F32, 4)
        r_f = Ring(S, st, "fE", [128, D], F32, 3)
        r_h = Ring(S, st, "hE", [128, D], F32, 2)
        tpF = Ring(S, st, "tpEF", [128, 1024], BF16, 1, psum=True)
        tpB = Ring(S, st, "tpEB", [128, 1024], BF16, 1, psum=True)
        gpF = Ring(S, st, "gpEF", [128, 512], F32, 4, psum=True)
        gpB = Ring(S, st, "gpEB", [128, 512], F32, 2, psum=True)

        def front(tile, X):
            s, i = tile
            r0 = s * T + i * 128
            ins = {}
            for nm, src in (("R", C.sR), ("K", C.sK), ("B", C.sB), ("A", C.sA), ("V", C.sV), ("G", C.sG)):
                t_ = r_in.next()
                S.dma("sp", t_.v(), dv(src[r0:r0 + 128, :]))
                ins[nm] = t_
            Rt, Kt, Bt, At, Vb, Gb = (ins[n] for n in "RKBAVG")
            xc = r_x.next()
            S.dma("sp", xc.v(), dv(C.out[r0:r0 + 128, :]))
            rk = r_rp.next()
            S.dma("sp", rk.v(), dv(C.rk[r0:r0 + 128, :]))
            pc = r_rp.next()
            S.dma("sp", pc.v(), dv(C.pcs[s * NT + i]))
            X.update(Kt=Kt, Bt=Bt, Vb=Vb, Gb=Gb, xc=xc, rk=rk, pc=pc, r0=r0, first=(i == 0))
            yield
            ART = r_art.next()
            BtT = r_bk.next()
            KtT = r_bk.next()
            for src, dstv, eng in ((At, ART[:, :, 0, :], "act"), (Rt, ART[:, :, 1, :], "dve"),
                                   (Bt, BtT.v(), "act"), (Kt, KtT.v(), "dve")):
                pt_ = tpF.next()
                for p in range(8):
                    S.tr(pt_[:, p * 128:(p + 1) * 128], src[:, p * 128:(p + 1) * 128], ID)
                S.copy(eng, dstv, pt_.v().rr("q (p t) -> q p t", p=8))
                yield
            Apv = Apad.v().rr("t (p hh) c -> t p hh c", hh=2)
            Atv = At.v().rr("t (p hh k) -> t p hh k", hh=2, k=64)
            S.copy("pool", Apv[:, :, 0, 0:64], Atv[:, :, 0, :])
            S.copy("pool", Apv[:, :, 1, 64:128], Atv[:, :, 1, :])
            AMb = r_amb.next()
            X.update(ART=ART, AMb=AMb)
            P = [None] * 4
            Q = [None] * 4
            TT = [None] * 4
            for g in range(4):
                for hl in range(4):
                    h = 4 * g + hl
                    p, pb = h // 2, (h % 2) * 64
                    aps = gpF.next()
                    art2 = ART[pb:pb + 64, p, :, :].rr("k a t -> k (a t)")
                    S.mm(aps[:, 0:256], BtT[pb:pb + 64, p, :], art2)
                    S.mm(aps[:, 256:512], KtT[pb:pb + 64, p, :], art2)
                    a4 = aps.v().rr("q (b a t) -> q b a t", b=2, a=2)
                    S.tt("dve", AMf[:, h, :, :], a4[:, :, 0, :], lt2.v(), ALU.mult)
                    S.tt("dve", AMb[:, h, :, :], a4[:, :, 1, :], le2.v(), ALU.mult)
                    if hl % 2 == 1:
                        yield
                npsl = [gpF.next(), gpF.next()]
                for hl in range(4):
                    h = 4 * g + hl
                    p, pb = h // 2, (h % 2) * 64
                    S.mm(npsl[h % 2][:, (hl // 2) * 128:(hl // 2 + 1) * 128], ART[pb:pb + 64, p, 0, :], BtT[pb:pb + 64, p, :])
                P[g] = r_pq.next()
                Pv = P[g].v().rr("q (a two) t -> q a two t", two=2)
                for par in range(2):
                    S.tt("dve", Pv[:, :, par, :], npsl[par][:, 0:256].rr("q (a t) -> q a t", a=2), gt2.v(), ALU.mult)
                Q[g] = AMf[:, 4 * g:4 * g + 4, 0, :]
                TT[g] = r_tt.next()
                S.tt("pool", TT[g].v(), Q[g], id4.v(), ALU.add)
                yield
            for j in range(1, 7):
                for g in range(4):
                    pps = gpF.next()
                    for hl in range(4):
                        S.mm(pps[:, hl * 128:(hl + 1) * 128], Q[g][:, hl, :], P[g][:, hl, :])
                    if j < 6:
                        qps = gpF.next()
                        for hl in range(4):
                            S.mm(qps[:, hl * 128:(hl + 1) * 128], P[g][:, hl, :], Q[g][:, hl, :])
                    Pn = r_pq.next()
                    S.copy("act", Pn.v().rr("q a t -> q (a t)"), pps.v())
                    if j < 6:
                        Qn = r_pq.next()
                        S.copy("act", Qn.v().rr("q a t -> q (a t)"), qps.v())
                    tps = gpF.next()
                    for hl in range(4):
                        S.mm(tps[:, hl * 128:(hl + 1) * 128], Pn[:, hl, :], TT[g][:, hl, :])
                    TTn = r_tt.next()
                    S.tt("dve", TTn.v().rr("q a t -> q (a t)"), tps.v(), TT[g].v().rr("q a t -> q (a t)"), ALU.add)
                    P[g] = Pn
                    if j < 6:
                        Q[g] = Qn.v()
                    TT[g] = TTn
                    yield
            AVb = r_av.next()
            Wb = r_w.next()
            for half in range(2):
                avp = gpF.next()
                for hl in range(8):
                    h = half * 8 + hl
                    S.mm(avp[:, hl * 64:(hl + 1) * 64], AMf[:, h, 1, :], Vb[:, h * 64:(h + 1) * 64])
                S.copy("act", AVb[:, half * 512:(half + 1) * 512], avp.v())
                yield
            for half in range(2):
                wp = gpF.next()
                for hl in range(8):
                    h = half * 8 + hl
                    S.mm(wp[:, hl * 64:(hl + 1) * 64], TT[h // 4][:, h % 4, :], AVb[:, h * 64:(h + 1) * 64])
                S.copy("act", Wb[:, half * 512:(half + 1) * 512], wp.v())
                yield
            AhT = r_aht.next()
            for half in range(2):
                ahp = gpF.next()
                for pl in range(4):
                    p = half * 4 + pl
                    h0, h1 = 2 * p, 2 * p + 1
                    S.mm(ahp[:, pl * 128:(pl + 1) * 128], Apad[:, h0, :], TT[h0 // 4][:, h0 % 4, :], start=True, stop=False)
                    S.mm(ahp[:, pl * 128:(pl + 1) * 128], Apad[:, h1, :], TT[h1 // 4][:, h1 % 4, :], start=False, stop=True)
                S.copy("dve", AhT[:, half * 4:(half + 1) * 4, :].rr("q a t -> q (a t)"), ahp.v())
                yield
            X.update(Wb=Wb, AhT=AhT)

        def back(tile, X):
            Kt, Bt, Vb, Gb, xc, rk, pc, r0 = (X[k] for k in ("Kt", "Bt", "Vb", "Gb", "xc", "rk", "pc", "r0"))
            ART, AMb, Wb, AhT = X["ART"], X["AMb"], X["Wb"], X["AhT"]
            if X["first"]:
                S.memset("pool", Mf.v(), 0.0)
                S.memset("pool", Mb.v(), 0.0)
            Yf = r_f.next()
            for p in range(8):
                pcs_ = slice(p * 128, (p + 1) * 128)
                ups = gpB.next()
                S.mm(ups[:, 0:128], AhT[:, p, :], Mb[:, p, :])
                Ub = r_u.next()
                S.tt("dve", Ub.v(), ups[:, 0:128], Wb[:, pcs_], ALU.add)
                yield
                yps = gpB.next()
                S.mm(yps[:, 0:128], ART[:, p, 1, :], Mb[:, p, :], start=True, stop=False)
                for hh in range(2):
                    h = 2 * p + hh
                    S.mm(yps[:, hh * 64:(hh + 1) * 64], AMb[:, h, 0, :], Ub[:, hh * 64:(hh + 1) * 64], start=False, stop=False)
                    S.mm(yps[:, hh * 64:(hh + 1) * 64], AMb[:, h, 1, :], Vb[:, h * 64:(h + 1) * 64], start=False, stop=(hh == 1))
                S.copy("act", Yf[:, pcs_], yps[:, 0:128])
                mps = gpB.next()
                S.mm(mps[:, 0:128], Bt[:, pcs_], Ub.v(), start=True, stop=False)
                S.mm(mps[:, 0:128], Kt[:, pcs_], Vb[:, pcs_], start=False, stop=True)
                tmp = r_tmp.next()
                S.stt(tmp.v(), mps[:, 0:128], pc[:, 2 * p:2 * p + 1], BD, ALU.mult, ALU.mult)
                S.stt(Mf[:, p, :], Mf[:, p, :], pc[:, 2 * p:2 * p + 1], tmp.v(), ALU.mult, ALU.add)
                S.copy("act", Mb[:, p, :], Mf[:, p, :])
                yield
            Y3 = Yf.v().rr("q (h d) -> q h d", h=16)
            sm = r_sm.next()
            S.reduce(sm.v(), Y3)
            S.ts("dve", sm.v(), sm.v(), 1.0 / 64, ALU.mult)
            S.tt("dve", Y3, Y3, bcv(sm.v(), [128, 16, 64]), ALU.subtract)
            sq = r_f.next()
            S.tt("pool", sq.v(), Yf.v(), Yf.v(), ALU.mult)
            yield
            sm2 = r_sm.next()
            S.reduce(sm2.v(), sq.v().rr("q (h d) -> q h d", h=16))
            rms_rstd(S, C, sm2.v(), sm2.v(), 64, GN_EPS)
            S.tt("dve", Y3, Y3, bcv(sm2.v(), [128, 16, 64]), ALU.mult)
            S.tt("pool", Yf.v(), Yf.v(), lnxg.v(), ALU.mult)
            S.tt("pool", Yf.v(), Yf.v(), lnxb.v(), ALU.add)
            yield
            S.tt("dve", sq.v().rr("q (h d) -> q h d", h=16), Vb.v().rr("q (h d) -> q h d", h=16),
                 bcv(rk.v(), [128, 16, 64]), ALU.mult)
            S.tt("pool", Yf.v(), Yf.v(), sq.v(), ALU.add)
            yfin = r_yfin.next()
            S.tt("dve", yfin.v(), Yf.v(), Gb.v(), ALU.mult)
            yield
            pT = tpB.next()
            for k in range(8):
                S.tr(pT[:, k * 128:(k + 1) * 128], yfin[:, k * 128:(k + 1) * 128], ID)
            yT = r_yT.next()
            S.copy("act", yT.v().rr("q k t -> q (k t)"), pT.v())
            yield
            h3 = r_h.next()
            for n in range(2):
                ps = gpB.next()
                for k in range(8):
                    S.mm(ps.v(), yT[:, k, :], w_o[:, k, n * 512:(n + 1) * 512], start=(k == 0), stop=(k == 7))
                S.tt("dve", h3[:, n * 512:(n + 1) * 512], ps.v(), xc[:, n * 512:(n + 1) * 512], ALU.add)
                yield
            S.dma("sp", dv(C.out[r0:r0 + 128, :]), h3.v())

        tiles = [(s, i) for s in range(NSEQ) for i in range(NT)]
        run_pipelined(tiles, front, back)
        S.barrier()


def phase_F(C):
    S = C.S
    T, NSEQ = C.T, C.NSEQ
    NT = T // 128
    with contextlib.ExitStack() as st:
        w_up = S.sbuf(st, "w_upF", [128, 8, DFF], BF16)
        load_w(S, w_up, C.mlp_w_up[1], 8, DFF)
        w_down = S.sbuf(st, "w_downF", [128, 32, D], BF16)
        load_w(S, w_down, C.mlp_w_down[1], 32, D)
        gbf = S.sbuf(st, "gbfF", [128, D], F32)
        bload(S, gbf.v(), C.norm_ffn_g[1:2, :], D)
        R = MlpRings(S, st, "F", 3)

        def front(tile, X):
            s, i = tile
            r0 = s * T + i * 128
            xt = R.h.next()
            S.dma("sp", xt.v(), dv(C.out[r0:r0 + 128, :]))
            hid = R.hid.next()
            X.update(h=xt, hid=hid, r0=r0)
            yield from mlp_front(C, R, xt, gbf, w_up, hid)

        def back(tile, X):
            yield from mlp_back(C, R, X["h"], w_down, X["hid"], X["r0"])

        tiles = [(s, i) for s in range(NSEQ) for i in range(NT)]
        run_pipelined(tiles, front, back)
        S.barrier()


INPUT_SPECS = [
    ("norm_mix_g", [2, D]), ("norm_ffn_g", [2, D]), ("ab_w_in", [1, D, ABIN]),
    ("hgrn_lower_bounds", [3, 512]), ("hgrn_norm_g", [1, 512]), ("fox_forget_bias", [1, 8]),
    ("fox_q_norm_g", [1, 64]), ("fox_k_norm_g", [1, 64]), ("ab_w_out", [1, D, D]),
    ("rwkv_mu", [1, 6, D]), ("rwkv_w_rkv", [1, 3, D, D]), ("rwkv_w0", [1, D]),
    ("rwkv_w1", [1, D, 64]), ("rwkv_w2", [1, 64, D]), ("rwkv_a0", [1, D]),
    ("rwkv_a1", [1, D, 64]), ("rwkv_a2", [1, 64, D]), ("rwkv_g1", [1, D, 128]),
    ("rwkv_g2", [1, 128, D]), ("rwkv_k_k", [1, D]), ("rwkv_k_a", [1, D]),
    ("rwkv_r_k", [1, 16, 64]), ("rwkv_lnx_g", [1, D]), ("rwkv_lnx_b", [1, D]),
    ("rwkv_w_o", [1, D, D]), ("mlp_w_up", [2, D, DFF]), ("mlp_w_down", [2, DFF, D]),
]


def build(T, NSEQ, upto="F"):
    nc = bass.Bass("TRN2", target_bir_lowering=False)
    C = Ctx()
    C.nc = nc
    C.T, C.NSEQ = T, NSEQ
    C.cut = 0
    NTOK = T * NSEQ
    C.x = nc.dram_tensor("x", [NTOK, D], F32, kind="ExternalInput").ap()
    aps = {}
    for name, shp in INPUT_SPECS:
        aps[name] = nc.dram_tensor(name, shp, F32, kind="ExternalInput").ap()
    C.norm_mix_g = aps["norm_mix_g"]
    C.norm_ffn_g = aps["norm_ffn_g"]
    C.ab_w_in = aps["ab_w_in"]
    C.hgrn_lb = aps["hgrn_lower_bounds"]
    C.hgrn_norm_g = aps["hgrn_norm_g"]
    C.fox_fb = aps["fox_forget_bias"]
    C.fox_q_g = aps["fox_q_norm_g"]
    C.fox_k_g = aps["fox_k_norm_g"]
    C.ab_w_out = aps["ab_w_out"]
    C.mlp_w_up = aps["mlp_w_up"]
    C.mlp_w_down = aps["mlp_w_down"]
    C.aps = aps
    cst = nc.dram_tensor("cst", [128, NCONST * 128], F32, kind="ExternalInput").ap()
    C.out = nc.dram_tensor("out", [NTOK, D], F32, kind="ExternalOutput").ap()

    def scratch(name, shp, dt):
        return nc.dram_tensor(name, shp, dt, kind="ExternalOutput").ap()
    sR = scratch("sR", [NTOK, D], BF16)
    sK = scratch("sK", [NTOK, D], BF16)
    sB = scratch("sB", [NTOK, D], BF16)
    C.sR, C.sK, C.sB = sR, sK, sB
    C.sA = scratch("sA", [NTOK, D], BF16)
    C.sV = scratch("sV", [NTOK, D], BF16)
    C.sG = scratch("sG", [NTOK, D], BF16)
    C.rk = scratch("rk", [NTOK, 16], F32)
    C.pcs = scratch("pcs", [NTOK // 128, 128, 16], F32)
    sRv = sR.rearrange("(x r) d -> x (r d)", x=2).rearrange("x (s a p t) -> x s a p t", s=NSEQ, a=4, p=128)
    C.fq = sRv[0]
    C.fk = sRv[1]
    C.fv = sK[:, 0:512]
    C.fg = sK[:, 512:1024]
    C.fc = scratch("fc", [NTOK, 8], F32)
    C.ya = sB[:, 0:512]
    C.yb = sB[:, 512:1024]
    C.h2 = C.out
    with contextlib.ExitStack() as st:
        S = Sched(nc, st)
        C.S = S
        cf = S.sbuf(st, "cstf", [128, NCONST * 128], F32)
        cbt = S.sbuf(st, "cstb", [128, NCONST * 128], BF16)
        S.dma("sp", cf.v(), dv(cst))
        S.copy("dve", cbt.v(), cf.v())
        C.cf, C.cb = cf, cbt
        phase_A(C)
        if upto != "A":
            phase_B(C)
            phase_C(C)
        if upto not in ("A", "C"):
            phase_D(C)
            if upto != "D":
                phase_E(C)
        if upto == "F":
            phase_F(C)
        S.final_wait("sp")
        C.ninst = S.ninst
    return nc, C


_CACHE = {}


def kernel(**inputs):
    x = np.ascontiguousarray(inputs["x"], dtype=np.float32)
    B, T, _ = x.shape
    NSEQ = B // NCORES
    key = (T, NSEQ)
    if key not in _CACHE:
        _CACHE[key] = build(T, NSEQ)[0]
    nc = _CACHE[key]
    cst = make_consts()
    shared = {name: np.ascontiguousarray(inputs[name], dtype=np.float32) for name, _ in INPUT_SPECS}
    shared["cst"] = cst
    in_maps = []
    for c in range(NCORES):
        m = dict(shared)
        m["x"] = x[c * NSEQ:(c + 1) * NSEQ].reshape(NSEQ * T, D)
        in_maps.append(m)
    res = run_bass_kernel_spmd(nc, in_maps, core_ids=list(range(NCORES)))
    outs = [np.asarray(r["out"]).reshape(NSEQ, T, D) for r in res.results]
    return np.concatenate(outs, axis=0).astype(np.float32)
```

```python
import numpy as np
import concourse.bass as bass
import concourse.mybir as mybir

F32 = mybir.dt.float32
BF16 = mybir.dt.bfloat16
AF = mybir.ActivationFunctionType
ALU = mybir.AluOpType
AX = mybir.AxisListType


class Buf:
    __slots__ = ("name", "lw", "rd")

    def __init__(self, name):
        self.name = name
        self.lw = None
        self.rd = {}


class V:
    __slots__ = ("ap", "bufs")

    def __init__(self, ap, bufs):
        self.ap = ap
        self.bufs = bufs

    def __getitem__(self, idx):
        return V(self.ap[idx], self.bufs)

    def rr(self, pat, **kw):
        return V(self.ap.rearrange(pat, **kw), self.bufs)

    def bc(self, shape):
        return V(self.ap.broadcast_to(shape), self.bufs)


class Tile:
    def __init__(self, t, name):
        self.t = t
        self.buf = Buf(name)

    def __getitem__(self, idx):
        return V(self.t[idx], (self.buf,))

    def v(self):
        return V(self.t[:], (self.buf,))

    def part(self, name):
        p = Tile.__new__(Tile)
        p.t = self.t
        p.buf = Buf(name)
        return p


class Sched:
    ENG = ("pe", "act", "dve", "pool", "sp")

    def __init__(self, nc, stack, n_dma_sems=24):
        self.nc = nc
        self.stack = stack
        self.eng = {"pe": nc.tensor, "act": nc.scalar, "dve": nc.vector,
                    "pool": nc.gpsimd, "sp": nc.sync}
        self.sem = {}
        self.cnt = {}
        for e in self.ENG:
            self.sem[e] = stack.enter_context(nc.semaphore("s_" + e))
            self.cnt[e] = 0
        self.dsem = [stack.enter_context(nc.semaphore("d%d" % i)) for i in range(n_dma_sems)]
        self.dcnt = [0] * n_dma_sems
        self.dnext = 0
        self.waited = {e: {} for e in self.ENG}
        self.ninst = 0

    def _semh(self, key):
        if isinstance(key, str):
            return self.sem[key]
        return self.dsem[key]

    def _wait(self, e, key, val):
        w = self.waited[e]
        if w.get(key, 0) >= val:
            return
        self.eng[e].wait_ge(self._semh(key), val)
        w[key] = val
        self.ninst += 1

    def _deps(self, e, reads, writes, skip_same_pe=True):
        need = {}

        def add(dep):
            if dep is None:
                return
            k, v = dep
            if need.get(k, 0) < v:
                need[k] = v
        for b in reads:
            add(b.lw)
        for b in writes:
            add(b.lw)
            for k, v in b.rd.items():
                add((k, v))
        for k, v in need.items():
            if e == "pe" and k == "pe":
                continue
            self._wait(e, k, v)

    def _mark(self, key, val, reads, writes):
        for b in reads:
            if b.rd.get(key, 0) < val:
                b.rd[key] = val
        for b in writes:
            b.lw = (key, val)
            b.rd = {}

    def op(self, e, fn, outs, ins):
        reads = [b for v in ins if v is not None for b in v.bufs]
        writes = [b for v in outs if v is not None for b in v.bufs]
        self._deps(e, reads, writes)
        self.cnt[e] += 1
        inst = fn()
        inst.then_inc(self.sem[e], 1)
        self._mark(e, self.cnt[e], reads, writes)
        self.ninst += 1
        return inst

    def dma(self, q, out, in_, **kw):
        reads = list(in_.bufs)
        writes = list(out.bufs)
        i = self.dnext
        self.dnext = (self.dnext + 1) % len(self.dsem)
        if self.dcnt[i] > 0:
            self._wait(q, i, self.dcnt[i])
        self._deps(q, reads, writes)
        self.dcnt[i] += 16
        self.eng[q].dma_start(out=out.ap, in_=in_.ap, **kw).then_inc(self.dsem[i], 16)
        self._mark(i, self.dcnt[i], reads, writes)
        self.ninst += 1

    def barrier(self):
        for e in self.ENG:
            for e2 in self.ENG:
                if e2 != e and self.cnt[e2] > 0:
                    self._wait(e, e2, self.cnt[e2])
            for i in range(len(self.dsem)):
                if self.dcnt[i] > 0:
                    self._wait(e, i, self.dcnt[i])

    def final_wait(self, e="sp"):
        for i in range(len(self.dsem)):
            if self.dcnt[i] > 0:
                self._wait(e, i, self.dcnt[i])
        for e2 in self.ENG:
            if e2 != e and self.cnt[e2] > 0:
                self._wait(e, e2, self.cnt[e2])

    def sbuf(self, stack, name, shape, dt):
        t = stack.enter_context(self.nc.sbuf_tensor(name, list(shape), dt))
        return Tile(t, name)

    def psum(self, stack, name, shape, dt):
        t = stack.enter_context(self.nc.psum_tensor(name, list(shape), dt))
        return Tile(t, name)

    def _pe_mode(self, lhsT):
        def rnd(n):
            return 32 if n <= 32 else (64 if n <= 64 else 128)
        shp = lhsT.ap.shape
        k = rnd(shp[0])
        m = 1
        for d in shp[1:]:
            m *= d
        mode = (k, rnd(m))
        if getattr(self, "_last_pe_mode", None) not in (None, mode):
            self.nc.tensor.drain()
            self.ninst += 1
        self._last_pe_mode = mode

    def mm(self, out, lhsT, rhs, start=True, stop=True, **kw):
        self._pe_mode(lhsT)
        return self.op("pe", lambda: self.nc.tensor.matmul(out.ap, lhsT.ap, rhs.ap, start=start, stop=stop, **kw),
                       [out], [lhsT, rhs])

    def tr(self, out, in_, ident):
        self._pe_mode(in_)
        return self.op("pe", lambda: self.nc.tensor.transpose(out.ap, in_.ap, ident.ap), [out], [in_, ident])

    def act(self, out, in_, func, bias=None, scale=None, accum_out=None):
        kw = {}
        ins = [in_]
        if bias is not None:
            if isinstance(bias, V):
                kw["bias"] = bias.ap
                ins.append(bias)
            else:
                kw["bias"] = bias
        if scale is not None:
            if isinstance(scale, V):
                kw["scale"] = scale.ap
                ins.append(scale)
            else:
                kw["scale"] = scale
        outs = [out]
        if accum_out is not None:
            kw["accum_out"] = accum_out.ap
            outs.append(accum_out)
        return self.op("act", lambda: self.nc.scalar.activation(out.ap, in_.ap, func, **kw), outs, ins)

    def _ve(self, e):
        return self.nc.vector if e == "dve" else self.nc.gpsimd

    def tt(self, e, out, in0, in1, op):
        return self.op(e, lambda: self._ve(e).tensor_tensor(out.ap, in0.ap, in1.ap, op), [out], [in0, in1])

    def ts(self, e, out, in0, s1, op0, s2=None, op1=None, accum_out=None):
        ins = [in0]
        a1 = s1
        if isinstance(s1, V):
            ins.append(s1)
            a1 = s1.ap
        a2 = s2
        if isinstance(s2, V):
            ins.append(s2)
            a2 = s2.ap
        kw = {}
        outs = [out]
        if op1 is not None:
            kw["op1"] = op1
        if accum_out is not None:
            kw["accum_out"] = accum_out.ap
            outs.append(accum_out)
        return self.op(e, lambda: self._ve(e).tensor_scalar(out.ap, in0.ap, a1, a2, op0, **kw), outs, ins)

    def stt(self, out, in0, scalar, in1, op0, op1):
        ins = [in0, in1]
        sc = scalar
        if isinstance(scalar, V):
            ins.append(scalar)
            sc = scalar.ap
        return self.op("dve", lambda: self.nc.vector.scalar_tensor_tensor(out.ap, in0.ap, sc, in1.ap, op0, op1),
                       [out], ins)

    def copy(self, e, out, in_):
        if e == "act":
            return self.op("act", lambda: self.nc.scalar.copy(out.ap, in_.ap), [out], [in_])
        return self.op(e, lambda: self._ve(e).tensor_copy(out.ap, in_.ap), [out], [in_])

    def memset(self, e, out, val):
        return self.op(e, lambda: self._ve(e).memset(out.ap, val), [out], [])

    def reduce(self, out, in_, op=None, axis=None):
        op = op or ALU.add
        axis = axis or AX.X
        return self.op("dve", lambda: self.nc.vector.tensor_reduce(out.ap, in_.ap, axis, op), [out], [in_])

    def recip(self, out, in_):
        return self.op("dve", lambda: self.nc.vector.reciprocal(out.ap, in_.ap), [out], [in_])


import contextlib
from concourse.bass_utils import run_bass_kernel_spmd

D = 1024
DFF = 4096
ABIN = 4104
EPS = 1e-6
GN_EPS = 64e-5
NCORES = 8


class Ring:
    def __init__(self, S, st, name, shape, dt, n, psum=False):
        mk = S.psum if psum else S.sbuf
        self.tiles = [mk(st, "%s%d" % (name, i), shape, dt) for i in range(n)]
        self.i = 0

    def next(self):
        t = self.tiles[self.i % len(self.tiles)]
        self.i += 1
        return t


def dv(ap):
    return V(ap, (Buf("d"),))


class Ctx:
    pass


C_ID, C_LE, C_LT, C_GT, C_LE64, C_BD, C_ONES, C_SEL0, C_SEL1 = range(9)
NCONST = 9


def make_consts():
    p = np.arange(128)[:, None]
    f = np.arange(128)[None, :]
    blocks = [
        (p == f), (p <= f), (p < f), (p > f),
        (p <= f) & (p // 64 == f // 64), (p // 64 == f // 64),
        np.ones((128, 128), bool), (p // 64 == 0) & (f >= 0), (p // 64 == 1) & (f >= 0),
    ]
    return np.concatenate([b.astype(np.float32) for b in blocks], axis=1)


def load_w(S, dst, src_ap, K, N, q="pool"):
    src = src_ap.rearrange("(k p) n -> p k n", p=128)
    for k in range(K):
        for c0 in range(0, N, 1024):
            c1 = min(N, c0 + 1024)
            S.dma(q, dst[:, k, c0:c1], dv(src[:, k, c0:c1]))


def bload(S, dstv, src_row_ap, n):
    S.dma("sp", dstv, dv(src_row_ap.broadcast_to([128, n])))


def cblk(t, i):
    return t[:, i * 128:(i + 1) * 128]


def rms_rstd(S, C, ss, rstd, n, eps):
    S.ts("dve", rstd, ss, 1.0 / n, ALU.mult, eps, ALU.add)
    S.act(rstd, rstd, AF.Ln)
    S.act(rstd, rstd, AF.Exp, scale=-0.5)


def drive(a, b):
    gens = [g for g in (a, b) if g is not None]
    while gens:
        for g in list(gens):
            try:
                next(g)
            except StopIteration:
                gens.remove(g)


def run_pipelined(tiles, front, back):
    prev = None
    for t in tiles:
        X = {}
        drive(prev, front(t, X))
        prev = back(t, X)
    drive(prev, None)


def bcv(v, shape):
    return V(v.ap.unsqueeze(2).broadcast_to(list(shape)), v.bufs)


def phase_A(C):
    S = C.S
    T, NSEQ = C.T, C.NSEQ
    NT = T // 128
    cf, cb = C.cf, C.cb
    with contextlib.ExitStack() as st:
        w_in = S.sbuf(st, "w_in", [128, 8, ABIN], BF16)
        load_w(S, w_in, C.ab_w_in[0], 8, ABIN)
        gb = S.sbuf(st, "gbA", [128, D], F32)
        bload(S, gb.v(), C.norm_mix_g[0:1, :], D)
        G = S.sbuf(st, "G", [128, 3, 512], F32)
        for l in range(3):
            bload(S, G[:, l, :], C.hgrn_lb[l:l + 1, :], 512)
        lbb = S.sbuf(st, "lbb", [128, 512], F32)
        oml = S.sbuf(st, "oml", [128, 512], F32)
        S.act(G.v(), G.v(), AF.Exp)
        S.tt("dve", lbb.v(), G[:, 0, :], G[:, 1, :], ALU.add)
        S.tt("dve", lbb.v(), lbb.v(), G[:, 2, :], ALU.add)
        S.recip(lbb.v(), lbb.v())
        S.tt("dve", lbb.v(), lbb.v(), G[:, 0, :], ALU.mult)
        S.ts("dve", oml.v(), lbb.v(), -1.0, ALU.mult, 1.0, ALU.add)
        hng = S.sbuf(st, "hng", [128, 512], F32)
        bload(S, hng.v(), C.hgrn_norm_g[0:1, :], 512)
        fqg = S.sbuf(st, "fqg", [128, 8, 64], F32)
        fkg = S.sbuf(st, "fkg", [128, 8, 64], F32)
        for h in range(8):
            bload(S, fqg[:, h, :], C.fox_q_g[0:1, :], 64)
            bload(S, fkg[:, h, :], C.fox_k_g[0:1, :], 64)
        fbb = S.sbuf(st, "fbb", [128, 8], F32)
        bload(S, fbb.v(), C.fox_fb[0:1, :], 8)
        m64x4 = S.sbuf(st, "m64x4", [128, 4, 128], F32)
        for h in range(4):
            S.copy("pool", m64x4[:, h, :], cblk(cf, C_LE64))
        Sf = S.sbuf(st, "Sf", [128, 512], F32)
        SbA = S.sbuf(st, "SbA", [128, 512], BF16)
        SbB = S.sbuf(st, "SbB", [128, 512], BF16)
        ctot = S.sbuf(st, "ctot", [128, 8], F32)
        r_x = Ring(S, st, "xA", [128, D], F32, 2)
        r_junk = Ring(S, st, "junkA", [128, D], BF16, 1)
        r_stF = Ring(S, st, "stAF", [128, 8], F32, 6)
        r_stB = Ring(S, st, "stAB", [128, 8], F32, 2)
        r_hn = Ring(S, st, "hnA", [128, D], BF16, 2)
        r_hnT = Ring(S, st, "hnTA", [128, 8, 128], BF16, 2)
        r_qs = Ring(S, st, "qsA", [128, 512], F32, 2)
        r_gl = Ring(S, st, "glA", [128, 512], F32, 2)
        r_kf = Ring(S, st, "kfA", [128, 512], F32, 2)
        r_sg = Ring(S, st, "sgA", [128, 512], F32, 2)
        r_vt = Ring(S, st, "vtA", [128, 512], BF16, 2)
        r_fF = Ring(S, st, "fAF", [128, 512], F32, 4)
        r_fB = Ring(S, st, "fAB", [128, 512], F32, 6)
        r_bF = Ring(S, st, "bAF", [128, 512], BF16, 4)
        r_bB = Ring(S, st, "bAB", [128, 512], BF16, 4)
        r_qtT = Ring(S, st, "qtT", [128, 4, 128], BF16, 2)
        r_ktT = Ring(S, st, "ktT", [128, 4, 128], BF16, 2)
        r_q0 = Ring(S, st, "qtT0", [128, 4, 128], BF16, 2)
        r_q1 = Ring(S, st, "qtT1", [128, 4, 128], BF16, 2)
        for t in r_q0.tiles + r_q1.tiles:
            S.memset("pool", t.v(), 0.0)
        r_fT = Ring(S, st, "fT", [128, 4, 128], BF16, 4)
        tpF = Ring(S, st, "tpAF", [128, 1024], BF16, 1, psum=True)
        ppF = Ring(S, st, "ppAF", [128, 512], F32, 2, psum=True)
        gpF = Ring(S, st, "gpAF", [128, 512], F32, 1, psum=True)
        tpB = Ring(S, st, "tpAB", [128, 1024], BF16, 2, psum=True)
        gpB = Ring(S, st, "gpAB", [128, 512], F32, 2, psum=True)
        ID = cblk(cb, C_ID)

        def front(tile, X):
            s, i = tile
            r0 = s * T + i * 128
            if i == 0:
                S.memset("pool", ctot.v(), 0.0)
            xt = r_x.next()
            S.dma("sp", xt.v(), dv(C.x[r0:r0 + 128, :]))
            junk = r_junk.next()
            sv = r_stF.next()
            S.act(junk.v(), xt.v(), AF.Square, accum_out=sv[:, 0:1])
            rms_rstd(S, C, sv[:, 0:1], sv[:, 1:2], D, EPS)
            hn = r_hn.next()
            S.stt(hn.v(), xt.v(), sv[:, 1:2], gb.v(), ALU.mult, ALU.mult)
            yield
            pT = tpF.next()
            for k in range(8):
                S.tr(pT[:, k * 128:(k + 1) * 128], hn[:, k * 128:(k + 1) * 128], ID)
            hnT = r_hnT.next()
            S.copy("act", hnT.v().rr("p k t -> p (k t)"), pT.v())
            yield

            def proj(og, n=512):
                ps = ppF.next()
                for k in range(8):
                    S.mm(ps[:, 0:n], hnT[:, k, :], w_in[:, k, og * 512:og * 512 + n], start=(k == 0), stop=(k == 7))
                return ps
            ps = proj(0)
            qs = r_qs.next()
            S.act(qs.v(), ps.v(), AF.Sigmoid)
            S.tt("dve", qs.v(), ps.v(), qs.v(), ALU.mult)
            yield
            ps = proj(1)
            f = r_fF.next()
            S.act(f.v(), ps.v(), AF.Sigmoid)
            S.tt("dve", f.v(), f.v(), oml.v(), ALU.mult)
            S.tt("dve", f.v(), f.v(), lbb.v(), ALU.add)
            yield
            ps = proj(3)
            sgate = r_sg.next()
            S.act(sgate.v(), ps.v(), AF.Sigmoid)
            S.tt("dve", sgate.v(), ps.v(), sgate.v(), ALU.mult)
            yield
            ps = proj(7)
            fgt = r_bF.next()
            S.act(fgt.v(), ps.v(), AF.Sigmoid)
            S.dma("sp", dv(C.fg[r0:r0 + 128, :]), fgt.v())
            ps = proj(8, 8)
            lf = r_stF.next()
            S.tt("dve", lf.v(), ps[:, 0:8], fbb.v(), ALU.add)
            S.act(lf.v(), lf.v(), AF.Sigmoid)
            S.act(lf.v(), lf.v(), AF.Ln)
            gl = r_gl.next()
            S.act(gl.v(), f.v(), AF.Ln)
            kf = r_kf.next()
            S.ts("pool", kf.v(), f.v(), -1.0, ALU.mult, 1.0, ALU.add)
            yield
            cps = gpF.next()
            S.mm(cps[:, 0:8], cblk(cf, C_LE), lf.v())
            S.mm(cps[:, 8:16], cblk(cf, C_ONES), lf.v())
            cs = r_stF.next()
            S.tt("dve", cs.v(), cps[:, 0:8], ctot.v(), ALU.add)
            S.dma("sp", dv(C.fc[r0:r0 + 128, :]), cs.v())
            S.tt("dve", ctot.v(), ctot.v(), cps[:, 8:16], ALU.add)
            ps = proj(2)
            vt = r_vt.next()
            S.copy("act", vt.v(), ps.v())
            X.update(qs=qs, gl=gl, kf=kf, sgate=sgate, vt=vt, r0=r0, first=(i == 0))
            yield

            for og, gtile, dst_dram in ((4, fqg, C.fq), (5, fkg, C.fk)):
                ps = proj(og)
                bq = r_fF.next()
                S.copy("act", bq.v(), ps.v())
                sq = r_fF.next()
                S.tt("pool", sq.v(), bq.v(), bq.v(), ALU.mult)
                sv2 = r_stF.next()
                S.reduce(sv2.v(), sq.v().rr("p (h d) -> p h d", h=8))
                rms_rstd(S, C, sv2.v(), sv2.v(), 64, EPS)
                S.tt("dve", bq.v().rr("p (h d) -> p h d", h=8), bq.v().rr("p (h d) -> p h d", h=8),
                     bcv(sv2.v(), [128, 8, 64]), ALU.mult)
                qn = r_bF.next()
                S.tt("pool", qn.v(), bq.v(), gtile.v().rr("p h d -> p (h d)"), ALU.mult)
                yield
                pq = tpF.next()
                for hp in range(4):
                    S.tr(pq[:, hp * 128:(hp + 1) * 128], qn[:, hp * 128:(hp + 1) * 128], ID)
                qT = r_fT.next()
                S.copy("dve", qT.v().rr("p a t -> p (a t)"), pq[:, 0:512])
                S.dma("sp", dv(dst_dram[s, :, :, i * 128:(i + 1) * 128].rearrange("a p t -> p a t")), qT.v())
                yield
            ps = proj(6)
            fvt = r_bF.next()
            S.copy("act", fvt.v(), ps.v())
            S.dma("sp", dv(C.fv[r0:r0 + 128, :]), fvt.v())

        def back(tile, X):
            qs, gl, kf, sgate, vt, r0 = (X[k] for k in ("qs", "gl", "kf", "sgate", "vt", "r0"))
            if X["first"]:
                S.memset("pool", Sf.v(), 0.0)
                S.memset("pool", SbA.v(), 0.0)
            bps = gpB.next()
            S.mm(bps.v(), cblk(cf, C_LE64), gl.v())
            eb = r_fB.next()
            S.act(eb.v(), bps.v(), AF.Exp)
            enb = r_fB.next()
            S.act(enb.v(), bps.v(), AF.Exp, scale=-1.0)
            qt = r_bB.next()
            S.tt("dve", qt.v(), qs.v(), eb.v(), ALU.mult)
            kt = r_bB.next()
            S.tt("pool", kt.v(), kf.v(), enb.v(), ALU.mult)
            yield
            ebl = []
            for c in range(2):
                eps_ = gpB.next()
                for h in range(4):
                    S.mm(eps_[:, h * 128:(h + 1) * 128], gl[:, h * 128:(h + 1) * 128], cblk(cf, C_SEL0 + c))
                e_ = r_fB.next()
                S.act(e_.v(), eps_.v(), AF.Exp)
                ebl.append(e_)
            yield
            pqq = tpB.next()
            pkk = tpB.next()
            for h in range(4):
                S.tr(pqq[:, h * 128:(h + 1) * 128], qt[:, h * 128:(h + 1) * 128], ID)
            for h in range(4):
                S.tr(pkk[:, h * 128:(h + 1) * 128], kt[:, h * 128:(h + 1) * 128], ID)
            qtT = r_qtT.next()
            ktT = r_ktT.next()
            q0 = r_q0.next()
            q1 = r_q1.next()
            S.copy("act", qtT.v().rr("p h t -> p (h t)"), pqq[:, 0:512])
            S.copy("dve", ktT.v().rr("p h t -> p (h t)"), pkk[:, 0:512])
            pq3 = pqq[:, 0:512].rr("p (h t) -> p h t", h=4)
            S.copy("act", q0[:, :, 0:64], pq3[:, :, 0:64])
            S.copy("act", q1[:, :, 64:128], pq3[:, :, 64:128])
            yield
            scp = gpB.next()
            for h in range(4):
                S.mm(scp[:, h * 128:(h + 1) * 128], ktT[:, h, :], qtT[:, h, :])
            scT = r_bB.next()
            S.tt("dve", scT.v(), scp.v(), m64x4.v().rr("p h t -> p (h t)"), ALU.mult)
            yield

            def state_update(c, dstb):
                ups = gpB.next()
                for h in range(4):
                    S.mm(ups[:, h * 128:(h + 1) * 128], kt[c * 64:(c + 1) * 64, h * 128:(h + 1) * 128],
                         vt[c * 64:(c + 1) * 64, h * 128:(h + 1) * 128])
                S.tt("dve", Sf.v(), Sf.v(), ups.v(), ALU.add)
                S.tt("pool", Sf.v(), Sf.v(), ebl[c].v(), ALU.mult)
                S.copy("act", dstb.v(), Sf.v())
            state_update(0, SbB)
            yield
            ops = gpB.next()
            for h in range(4):
                hs = slice(h * 128, (h + 1) * 128)
                S.mm(ops[:, hs], scT[:, hs], vt[:, hs], start=True, stop=False)
                S.mm(ops[:, hs], q0[:, h, :], SbA[:, hs], start=False, stop=False)
                S.mm(ops[:, hs], q1[:, h, :], SbB[:, hs], start=False, stop=True)
            yield
            state_update(1, SbA)
            osb = r_fB.next()
            S.copy("act", osb.v(), ops.v())
            sq = r_fB.next()
            S.tt("pool", sq.v(), osb.v(), osb.v(), ALU.mult)
            yield
            sv3 = r_stB.next()
            S.reduce(sv3[:, 0:4], sq.v().rr("p (h d) -> p h d", h=4))
            rms_rstd(S, C, sv3[:, 0:4], sv3[:, 0:4], 128, EPS)
            S.tt("dve", osb.v().rr("p (h d) -> p h d", h=4), osb.v().rr("p (h d) -> p h d", h=4),
                 bcv(sv3[:, 0:4], [128, 4, 128]), ALU.mult)
            S.tt("pool", osb.v(), osb.v(), hng.v(), ALU.mult)
            yat = r_bB.next()
            S.tt("dve", yat.v(), osb.v(), sgate.v(), ALU.mult)
            S.dma("sp", dv(C.ya[r0:r0 + 128, :]), yat.v())

        tiles = [(s, i) for s in range(NSEQ) for i in range(NT)]
        run_pipelined(tiles, front, back)
        S.barrier()


def phase_B(C):
    S = C.S
    T, NSEQ = C.T, C.NSEQ
    NT = T // 128
    QG = min(4, NT)
    NG = NT // QG
    cf, cb = C.cf, C.cb
    with contextlib.ExitStack() as st:
        r_kT = Ring(S, st, "kTB", [128, 2, T], BF16, 2)
        for t in r_kT.tiles:
            S.memset("pool", t[64:128, 0, :], 0.0)
            S.memset("pool", t[0:64, 1, :], 0.0)
        r_qT = Ring(S, st, "qTB", [128, T], BF16, 2)
        r_v = Ring(S, st, "vB", [128, NT, 2, 65], BF16, 2)
        for t in r_v.tiles:
            S.memset("pool", t.v(), 1.0)
        r_g = Ring(S, st, "gB", [128, NT, 128], BF16, 2)
        r_y = Ring(S, st, "yB", [128, NT, 128], BF16, 2)
        cT = S.sbuf(st, "cT", [128, NT, 8], F32)
        r_cref = Ring(S, st, "cref", [128, 8], F32, 2)
        r_bias = Ring(S, st, "biasB", [128, NT], F32, 3)
        r_p = Ring(S, st, "pB", [128, QG * 128], BF16, 6)
        r_rc = Ring(S, st, "rcB", [128, QG], F32, 2)
        lemask = S.sbuf(st, "lemaskB", [128, 128], BF16)
        S.copy("pool", lemask.v(), cblk(cf, C_LE))
        stp = Ring(S, st, "stB", [128, 512], F32, 4, psum=True)
        accp = Ring(S, st, "accB", [128, 512], F32, 2, psum=True)
        for s in range(NSEQ):
            S.dma("sp", cT.v(), dv(C.fc[s * T:(s + 1) * T, :].rearrange("(b p) h -> p b h", p=128)))
            for hp in range(4):
                kT = r_kT.next()
                qT = r_qT.next()
                vt = r_v.next()
                gt = r_g.next()
                yt = r_y.next()
                S.dma("sp", kT[0:64, 0, :], dv(C.fk[s, hp, 0:64, :]))
                S.dma("sp", kT[64:128, 1, :], dv(C.fk[s, hp, 64:128, :]))
                S.dma("sp", qT.v(), dv(C.fq[s, hp, :, :]))
                for hh_ in range(2):
                    c0_ = hp * 128 + hh_ * 64
                    S.dma("sp", vt[:, :, hh_, 0:64],
                          dv(C.fv[s * T:(s + 1) * T, c0_:c0_ + 64].rearrange("(b p) d -> p b d", p=128)))
                S.dma("sp", gt.v(),
                      dv(C.fg[s * T:(s + 1) * T, hp * 128:(hp + 1) * 128].rearrange("(b p) d -> p b d", p=128)))
                items = []
                for qg in range(NG):
                    for hh in range(2):
                        for j in range(qg * QG + QG):
                            items.append((qg, hh, j))
                LOOK = 2
                stq = {}

                def emit_st(idx):
                    qg_, hh_, j_ = items[idx]
                    i0_ = qg_ * QG
                    pb_ = hh_ * 64
                    ilo_ = max(j_, i0_)
                    ncol_ = (i0_ + QG - ilo_) * 128
                    sp_ = stp.next()
                    S.mm(sp_[:, 0:ncol_], kT[:, hh_, j_ * 128:(j_ + 1) * 128],
                         qT[:, ilo_ * 128:(i0_ + QG) * 128])
                    stq[idx] = (sp_, ncol_, ilo_)
                for idx in range(min(LOOK, len(items))):
                    emit_st(idx)
                cref = None
                for idx, (qg, hh, j) in enumerate(items):
                    if idx + LOOK < len(items):
                        emit_st(idx + LOOK)
                    i0 = qg * QG
                    nj = i0 + QG
                    h = hp * 2 + hh
                    pb = hh * 64
                    if j == 0:
                        if hh == 0:
                            tok_ref = s * T + i0 * 128 + (QG * 128) // 2 - 1
                            cref = r_cref.next()
                            bload(S, cref.v(), C.fc[tok_ref:tok_ref + 1, :], 8)
                        bias = r_bias.next()
                        S.ts("dve", bias[:, 0:nj], cT[:, 0:nj, h], cref[:, h:h + 1], ALU.subtract, -1.0, ALU.mult)
                        acc = accp.next()[:, 0:QG * 65].rr("p (a d) -> p a d", a=QG)
                        first = True
                    sp_, ncol, ilo = stq.pop(idx)
                    pt = r_p.next()
                    S.act(pt[:, 0:ncol], sp_[:, 0:ncol], AF.Exp, bias=bias[:, j:j + 1], scale=0.125)
                    if j >= i0:
                        S.tt("dve", pt[:, 0:128], pt[:, 0:128], lemask.v(), ALU.mult)
                    for qi in range(ilo, i0 + QG):
                        S.mm(acc[:, qi - i0, :], pt[:, (qi - ilo) * 128:(qi - ilo + 1) * 128], vt[:, j, hh, :],
                             start=first, stop=(j == qi), skip_group_check=True)
                        first = False
                    if j == nj - 1:
                        rc = r_rc.next()
                        S.recip(rc.v(), acc[:, :, 64])
                        for qi in range(QG):
                            S.stt(yt[:, i0 + qi, pb:pb + 64], acc[:, qi, 0:64], rc[:, qi:qi + 1],
                                  gt[:, i0 + qi, pb:pb + 64], ALU.mult, ALU.mult)
                S.dma("sp", dv(C.yb[s * T:(s + 1) * T, hp * 128:(hp + 1) * 128].rearrange("(b p) d -> p b d", p=128)),
                      yt.v())
        S.barrier()


class MlpRings:
    def __init__(self, S, st, sfx, n_h):
        self.st = Ring(S, st, "stM" + sfx, [128, 8], F32, 4)
        self.hn = Ring(S, st, "hnM" + sfx, [128, D], BF16, 1)
        self.hnT = Ring(S, st, "hnTM" + sfx, [128, 8, 128], BF16, 1)
        self.hid = Ring(S, st, "hidM" + sfx, [128, 32, 128], BF16, 2)
        self.rl = Ring(S, st, "rlM" + sfx, [128, 512], BF16, 3)
        self.h = Ring(S, st, "hM" + sfx, [128, D], F32, n_h)
        self.o = Ring(S, st, "oM" + sfx, [128, D], F32, 1)
        self.tp = Ring(S, st, "tpM" + sfx, [128, 1024], BF16, 2, psum=True)
        self.ppF = Ring(S, st, "ppMF" + sfx, [128, 512], F32, 4, psum=True)
        self.ppB = Ring(S, st, "ppMB" + sfx, [128, 512], F32, 2, psum=True)


def mlp_front(C, R, h, gbf, w_up, hid):
    S = C.S
    ID = cblk(C.cb, C_ID)
    sv = R.st.next()
    hn = R.hn.next()
    S.act(hn.v(), h.v(), AF.Square, accum_out=sv[:, 0:1])
    rms_rstd(S, C, sv[:, 0:1], sv[:, 1:2], D, EPS)
    S.stt(hn.v(), h.v(), sv[:, 1:2], gbf.v(), ALU.mult, ALU.mult)
    yield
    pT = R.tp.next()
    for k in range(8):
        S.tr(pT[:, k * 128:(k + 1) * 128], hn[:, k * 128:(k + 1) * 128], ID)
    hnT = R.hnT.next()
    S.copy("act", hnT.v().rr("p k t -> p (k t)"), pT.v())
    yield
    for fg in range(8):
        ps = R.ppF.next()
        for fc in range(4):
            fcol = (fg * 4 + fc) * 128
            for k in range(8):
                S.mm(ps[:, fc * 128:(fc + 1) * 128], w_up[:, k, fcol:fcol + 128], hnT[:, k, :],
                     start=(k == 0), stop=(k == 7))
            if fc % 2 == 1 and fc < 3:
                yield
        rl = R.rl.next()
        S.act(rl.v(), ps.v(), AF.Relu)
        S.tt("pool" if fg % 2 == 0 else "dve", hid[:, fg * 4:(fg + 1) * 4, :].rr("p a t -> p (a t)"), rl.v(), rl.v(), ALU.mult)
        yield


def mlp_back(C, R, h, w_down, hid, r0):
    S = C.S
    o = R.o.next()
    for n in range(2):
        ps = R.ppB.next()
        for kf in range(32):
            S.mm(ps.v(), hid[:, kf, :], w_down[:, kf, n * 512:(n + 1) * 512], start=(kf == 0), stop=(kf == 31))
            if kf % 8 == 7 and kf < 31:
                yield
        S.tt("dve", o[:, n * 512:(n + 1) * 512], ps.v(), h[:, n * 512:(n + 1) * 512], ALU.add)
        yield
    S.dma("sp", dv(C.out[r0:r0 + 128, :]), o.v())


def phase_C(C):
    S = C.S
    T, NSEQ = C.T, C.NSEQ
    NT = T // 128
    cb = C.cb
    with contextlib.ExitStack() as st:
        w_out = S.sbuf(st, "w_out", [128, 8, D], BF16)
        load_w(S, w_out, C.ab_w_out[0], 8, D)
        w_up = S.sbuf(st, "w_up", [128, 8, DFF], BF16)
        load_w(S, w_up, C.mlp_w_up[0], 8, DFF)
        w_down = S.sbuf(st, "w_down", [128, 32, D], BF16)
        load_w(S, w_down, C.mlp_w_down[0], 32, D)
        gbf = S.sbuf(st, "gbfC", [128, D], F32)
        bload(S, gbf.v(), C.norm_ffn_g[0:1, :], D)
        R = MlpRings(S, st, "C", 2)
        r_x = Ring(S, st, "xC", [128, D], F32, 1)
        r_y = Ring(S, st, "yC", [128, D], BF16, 1)
        r_yT = Ring(S, st, "yTC", [128, 8, 128], BF16, 1)
        ID = cblk(cb, C_ID)

        def front(tile, X):
            s, i = tile
            r0 = s * T + i * 128
            xt = r_x.next()
            S.dma("sp", xt.v(), dv(C.x[r0:r0 + 128, :]))
            y = r_y.next()
            S.dma("sp", y.v(), dv(C.sB[r0:r0 + 128, :]))
            pT = R.tp.next()
            for k in range(8):
                S.tr(pT[:, k * 128:(k + 1) * 128], y[:, k * 128:(k + 1) * 128], ID)
            yT = r_yT.next()
            S.copy("act", yT.v().rr("p k t -> p (k t)"), pT.v())
            yield
            h1 = R.h.next()
            for n in range(2):
                ps = R.ppF.next()
                for k in range(8):
                    S.mm(ps.v(), yT[:, k, :], w_out[:, k, n * 512:(n + 1) * 512], start=(k == 0), stop=(k == 7))
                S.tt("dve", h1[:, n * 512:(n + 1) * 512], ps.v(), xt[:, n * 512:(n + 1) * 512], ALU.add)
                yield
            hid = R.hid.next()
            X.update(h=h1, hid=hid, r0=r0)
            yield from mlp_front(C, R, h1, gbf, w_up, hid)

        def back(tile, X):
            yield from mlp_back(C, R, X["h"], w_down, X["hid"], X["r0"])

        tiles = [(s, i) for s in range(NSEQ) for i in range(NT)]
        run_pipelined(tiles, front, back)
        S.barrier()


import math


def phase_D(C):
    S = C.S
    T, NSEQ = C.T, C.NSEQ
    NT = T // 128
    cf, cb = C.cf, C.cb
    A = C.aps
    ID = cblk(cb, C_ID)
    with contextlib.ExitStack() as st:
        w_r = S.sbuf(st, "w_r", [128, 8, D], BF16)
        w_k = S.sbuf(st, "w_k", [128, 8, D], BF16)
        w_v = S.sbuf(st, "w_v", [128, 8, D], BF16)
        load_w(S, w_r, A["rwkv_w_rkv"][0, 0], 8, D)
        load_w(S, w_k, A["rwkv_w_rkv"][0, 1], 8, D)
        load_w(S, w_v, A["rwkv_w_rkv"][0, 2], 8, D)
        w1 = S.sbuf(st, "w1", [128, 8, 64], BF16)
        a1 = S.sbuf(st, "a1", [128, 8, 64], BF16)
        g1 = S.sbuf(st, "g1", [128, 8, 128], BF16)
        load_w(S, w1, A["rwkv_w1"][0], 8, 64)
        load_w(S, a1, A["rwkv_a1"][0], 8, 64)
        load_w(S, g1, A["rwkv_g1"][0], 8, 128)
        w2 = S.sbuf(st, "w2", [128, D], BF16)
        a2 = S.sbuf(st, "a2", [128, D], BF16)
        g2 = S.sbuf(st, "g2", [128, D], BF16)
        S.dma("pool", w2[0:64, :], dv(A["rwkv_w2"][0]))
        S.dma("pool", a2[0:64, :], dv(A["rwkv_a2"][0]))
        S.dma("pool", g2.v(), dv(A["rwkv_g2"][0]))
        bc_ = {}
        for nm, ap in (("gmix", C.norm_mix_g[1:2, :]), ("w0b", A["rwkv_w0"][0:1, :]), ("a0b", A["rwkv_a0"][0:1, :]),
                       ("k_kb", A["rwkv_k_k"][0:1, :]), ("k_ab", A["rwkv_k_a"][0:1, :]),
                       ("r_kb", A["rwkv_r_k"].rearrange("a h n -> a (h n)"))):
            t_ = S.sbuf(st, nm, [128, D], F32)
            bload(S, t_.v(), ap, D)
            bc_[nm] = t_
        gmix, w0b, a0b, k_kb, k_ab, r_kb = (bc_[n] for n in ("gmix", "w0b", "a0b", "k_kb", "k_ab", "r_kb"))
        tpF = Ring(S, st, "tpD", [128, 1024], BF16, 2, psum=True)
        ppF = Ring(S, st, "ppD", [128, 512], F32, 2, psum=True)
        gpF = Ring(S, st, "gpDF", [128, 512], F32, 1, psum=True)
        gpB = Ring(S, st, "gpDB", [128, 512], F32, 3, psum=True)
        mu_rows = S.sbuf(st, "mu_rows", [128, D], F32)
        S.dma("sp", mu_rows[0:6, :], dv(A["rwkv_mu"][0]))
        mu_kc = S.sbuf(st, "mu_kc", [128, 8, 6], F32)
        mps = gpB.next()
        for k in range(8):
            S.tr(mps[:, k * 6:(k + 1) * 6], mu_rows[0:6, k * 128:(k + 1) * 128], cblk(cf, C_ID)[0:6, 0:6])
        S.copy("dve", mu_kc.v().rr("p k m -> p (k m)"), mps[:, 0:48])

        r_x = Ring(S, st, "xD", [128, D], F32, 2)
        r_junk = Ring(S, st, "junkD", [128, D], BF16, 1)
        r_stF = Ring(S, st, "stDF", [128, 16], F32, 4)
        r_stB = Ring(S, st, "stDB", [128, 16], F32, 6)
        r_fF = Ring(S, st, "fDF", [128, D], F32, 2)
        r_fB = Ring(S, st, "fDB", [128, D], F32, 3)
        r_rf = Ring(S, st, "rfD", [128, D], F32, 2)
        r_kf = Ring(S, st, "kfD", [128, D], F32, 2)
        r_lw = Ring(S, st, "lwD", [128, D], F32, 2)
        r_af = Ring(S, st, "afD", [128, D], F32, 2)
        r_e = Ring(S, st, "eD", [128, 512], F32, 4)
        r_b = Ring(S, st, "bD", [128, D], BF16, 6)
        r_mix = Ring(S, st, "mixD", [128, 8, 128], BF16, 6)
        r_o = Ring(S, st, "oD", [128, D], BF16, 8)
        r_s = Ring(S, st, "sD", [128, 128], BF16, 4)
        NEG = -math.exp(-0.5)

        def front(tile, X):
            s, i = tile
            r0 = s * T + i * 128
            xc = r_x.next()
            S.dma("sp", xc.v(), dv(C.out[r0:r0 + 128, :]))
            xp = r_x.next()
            if i == 0:
                S.memset("pool", xp[0:1, :], 0.0)
                S.dma("sp", xp[1:128, :], dv(C.out[r0:r0 + 127, :]))
            else:
                S.dma("sp", xp.v(), dv(C.out[r0 - 1:r0 + 127, :]))
            hc = r_fF.next()
            hp = r_fF.next()
            for xt_, ht_ in ((xc, hc), (xp, hp)):
                junk = r_junk.next()
                sv = r_stF.next()
                S.act(junk.v(), xt_.v(), AF.Square, accum_out=sv[:, 0:1])
                rms_rstd(S, C, sv[:, 0:1], sv[:, 1:2], D, EPS)
                S.stt(ht_.v(), xt_.v(), sv[:, 1:2], gmix.v(), ALU.mult, ALU.mult)
                yield
            hcb = r_b.next()
            xxb = r_b.next()
            S.copy("act", hcb.v(), hc.v())
            S.tt("pool", xxb.v(), hp.v(), hc.v(), ALU.subtract)
            pH = tpF.next()
            pX = tpF.next()
            for k in range(8):
                S.tr(pH[:, k * 128:(k + 1) * 128], hcb[:, k * 128:(k + 1) * 128], ID)
            for k in range(8):
                S.tr(pX[:, k * 128:(k + 1) * 128], xxb[:, k * 128:(k + 1) * 128], ID)
            hcT = r_b.next()
            xxT = r_b.next()
            S.copy("act", hcT.v(), pH.v())
            S.copy("dve", xxT.v(), pX.v())
            yield
            mixT = []
            for m in range(6):
                mt = r_mix.next()
                e = "pool" if m % 2 == 0 else "dve"
                S.tt(e, mt.v(), xxT.v().rr("p (k t) -> p k t", k=8), bcv(mu_kc[:, :, m], [128, 8, 128]), ALU.mult)
                S.tt(e, mt.v(), mt.v(), hcT.v().rr("p (k t) -> p k t", k=8), ALU.add)
                mixT.append(mt)
                if m % 2 == 1:
                    yield
            xr, xw, xk, xv, xa, xg = mixT

            def proj(w, xT, dst):
                for n in range(2):
                    ps = ppF.next()
                    for k in range(8):
                        S.mm(ps.v(), xT[:, k, :], w[:, k, n * 512:(n + 1) * 512], start=(k == 0), stop=(k == 7))
                    S.copy("act", dst[:, n * 512:(n + 1) * 512], ps.v())
            rf = r_rf.next()
            proj(w_r, xr, rf)
            yield
            kf = r_kf.next()
            proj(w_k, xk, kf)
            yield
            vb = r_b.next()
            proj(w_v, xv, vb)
            S.dma("sp", dv(C.sV[r0:r0 + 128, :]), vb.v())
            yield

            def lora(w_a, xT, w_b, nj, func, bias_t, dst, final_func):
                lp = gpF.next()
                for k in range(8):
                    S.mm(lp[0:nj, 0:128], w_a[:, k, :], xT[:, k, :], start=(k == 0), stop=(k == 7))
                th = r_s.next()
                S.act(th[0:nj, :], lp[0:nj, 0:128], func)
                for n in range(2):
                    ps = ppF.next()
                    S.mm(ps.v(), th[0:nj, :], w_b[0:nj, n * 512:(n + 1) * 512])
                    if bias_t is not None:
                        S.tt("dve", dst[:, n * 512:(n + 1) * 512], ps.v(), bias_t[:, n * 512:(n + 1) * 512], ALU.add)
                    else:
                        S.copy("act", dst[:, n * 512:(n + 1) * 512], ps.v())
                if final_func is not None:
                    S.act(dst.v(), dst.v(), final_func)
            lw = r_lw.next()
            lora(w1, xw, w2, 64, AF.Tanh, w0b, lw, AF.Sigmoid)
            S.ts("pool", lw.v(), lw.v(), NEG, ALU.mult, 0.0, ALU.add)
            yield
            af = r_af.next()
            lora(a1, xa, a2, 64, AF.Copy, a0b, af, AF.Sigmoid)
            yield
            gb_ = r_b.next()
            lora(g1, xg, g2, 128, AF.Sigmoid, None, gb_, None)
            S.dma("sp", dv(C.sG[r0:r0 + 128, :]), gb_.v())
            X.update(rf=rf, kf=kf, lw=lw, af=af, r0=r0, ti=s * NT + i)

        def back(tile, X):
            rf, kf, lw, af, r0 = (X[k] for k in ("rf", "kf", "lw", "af", "r0"))
            kk = r_fB.next()
            S.tt("dve", kk.v(), kf.v(), k_kb.v(), ALU.mult)
            sq = r_fB.next()
            S.tt("pool", sq.v(), kk.v(), kk.v(), ALU.mult)
            yield
            sv = r_stB.next()
            S.reduce(sv.v(), sq.v().rr("p (h d) -> p h d", h=16))
            S.ts("dve", sv.v(), sv.v(), 1e-24, ALU.max)
            S.act(sv.v(), sv.v(), AF.Ln)
            S.act(sv.v(), sv.v(), AF.Exp, scale=-0.5)
            S.tt("dve", kk.v().rr("p (h d) -> p h d", h=16), kk.v().rr("p (h d) -> p h d", h=16),
                 bcv(sv.v(), [128, 16, 64]), ALU.mult)
            yield
            S.stt(sq.v(), af.v(), -1.0, k_ab.v(), ALU.add, ALU.mult)
            S.stt(kf.v(), sq.v(), 1.0, kf.v(), ALU.add, ALU.mult)
            yield
            t2 = r_fB.next()
            S.tt("pool", t2.v(), rf.v(), kf.v(), ALU.mult)
            S.tt("pool", t2.v(), t2.v(), r_kb.v(), ALU.mult)
            rkv = r_stB.next()
            S.reduce(rkv.v(), t2.v().rr("p (h d) -> p h d", h=16))
            S.dma("sp", dv(C.rk[r0:r0 + 128, :]), rkv.v())
            S.tt("dve", t2.v(), kk.v(), af.v(), ALU.mult)
            yield
            Rt = r_o.next()
            Kt = r_o.next()
            Bt = r_o.next()
            At = r_o.next()
            for n in range(2):
                hs = slice(n * 512, (n + 1) * 512)
                cw = gpB.next()
                cwx = gpB.next()
                S.mm(cw.v(), cblk(cf, C_LE), lw[:, hs])
                S.mm(cwx.v(), cblk(cf, C_LT), lw[:, hs])
                ep = r_e.next()
                S.act(ep.v(), cw.v(), AF.Exp)
                en = r_e.next()
                S.act(en.v(), cw.v(), AF.Exp, scale=-1.0)
                epx = r_e.next()
                S.act(epx.v(), cwx.v(), AF.Exp)
                yield
                S.tt("dve", Rt[:, hs], rf[:, hs], ep.v(), ALU.mult)
                S.tt("pool", Kt[:, hs], kf[:, hs], en.v(), ALU.mult)
                S.tt("dve", Bt[:, hs], t2[:, hs], en.v(), ALU.mult)
                S.stt(At[:, hs], kk[:, hs], -1.0, epx.v(), ALU.mult, ALU.mult)
                yield
            S.dma("sp", dv(C.sR[r0:r0 + 128, :]), Rt.v())
            S.dma("sp", dv(C.sK[r0:r0 + 128, :]), Kt.v())
            S.dma("sp", dv(C.sB[r0:r0 + 128, :]), Bt.v())
            S.dma("sp", dv(C.sA[r0:r0 + 128, :]), At.v())
            pcp = gpB.next()
            for p in range(8):
                S.mm(pcp[:, p * 2:(p + 1) * 2], lw[:, p * 128:(p + 1) * 128], cblk(cf, C_ONES)[:, 0:2])
            pct = r_stB.next()
            S.act(pct.v(), pcp[:, 0:16], AF.Exp)
            S.dma("sp", dv(C.pcs[X["ti"]]), pct.v())

        tiles = [(s, i) for s in range(NSEQ) for i in range(NT)]
        run_pipelined(tiles, front, back)
        S.barrier()


def phase_E(C):
    S = C.S
    T, NSEQ = C.T, C.NSEQ
    NT = T // 128
    cf, cb = C.cf, C.cb
    A = C.aps
    ID = cblk(cb, C_ID)
    with contextlib.ExitStack() as st:
        w_o = S.sbuf(st, "w_o", [128, 8, D], BF16)
        load_w(S, w_o, A["rwkv_w_o"][0], 8, D)
        lnxg = S.sbuf(st, "lnxg", [128, D], F32)
        lnxb = S.sbuf(st, "lnxb", [128, D], F32)
        bload(S, lnxg.v(), A["rwkv_lnx_g"][0:1, :], D)
        bload(S, lnxb.v(), A["rwkv_lnx_b"][0:1, :], D)
        lt2 = S.sbuf(st, "lt2", [128, 2, 128], F32)
        le2 = S.sbuf(st, "le2", [128, 2, 128], F32)
        gt2 = S.sbuf(st, "gt2", [128, 2, 128], F32)
        id4 = S.sbuf(st, "id4", [128, 4, 128], BF16)
        for j in range(2):
            S.copy("pool", lt2[:, j, :], cblk(cf, C_LT))
            S.copy("pool", le2[:, j, :], cblk(cf, C_LE))
            S.copy("pool", gt2[:, j, :], cblk(cf, C_GT))
        for j in range(4):
            S.copy("pool", id4[:, j, :], cblk(cb, C_ID))
        BD = cblk(cf, C_BD)
        Mf = S.sbuf(st, "Mf", [128, 8, 128], F32)
        Mb = S.sbuf(st, "Mb", [128, 8, 128], BF16)
        Apad = S.sbuf(st, "Apad", [128, 16, 128], BF16)
        S.memset("pool", Apad.v(), 0.0)
        AMf = S.sbuf(st, "AMf", [128, 16, 2, 128], BF16)
        r_amb = Ring(S, st, "AMb", [128, 16, 2, 128], BF16, 2)
        r_in = Ring(S, st, "inE", [128, D], BF16, 12)
        r_x = Ring(S, st, "xE", [128, D], F32, 2)
        r_rp = Ring(S, st, "rpE", [128, 16], F32, 4)
        r_sm = Ring(S, st, "smE", [128, 16], F32, 4)
        r_art = Ring(S, st, "artE", [128, 8, 2, 128], BF16, 2)
        r_bk = Ring(S, st, "bkE", [128, 8, 128], BF16, 2)
        r_aht = Ring(S, st, "ahtE", [128, 8, 128], BF16, 2)
        r_yT = Ring(S, st, "yTE", [128, 8, 128], BF16, 1)
        r_pq = Ring(S, st, "pqE", [128, 4, 128], BF16, 16)
        r_tt = Ring(S, st, "ttE", [128, 4, 128], BF16, 10)
        r_av = Ring(S, st, "avE", [128, D], BF16, 1)
        r_w = Ring(S, st, "wE", [128, D], BF16, 2)
        r_yfin = Ring(S, st, "yfinE", [128, D], BF16, 1)
        r_u = Ring(S, st, "uE", [128, 128], BF16, 4)
        r_tmp = Ring(S, st, "tmpE", [128, 128], F32, 4)
        r_f = Ring(S, st, "fE", [128, D], F32, 3)
        r_h = Ring(S, st, "hE", [128, D], F32, 2)
        tpF = Ring(S, st, "tpEF", [128, 1024], BF16, 1, psum=True)
        tpB = Ring(S, st, "tpEB", [128, 1024], BF16, 1, psum=True)
        gpF = Ring(S, st, "gpEF", [128, 512], F32, 4, psum=True)
        gpB = Ring(S, st, "gpEB", [128, 512], F32, 2, psum=True)

        def front(tile, X):
            s, i = tile
            r0 = s * T + i * 128
            ins = {}
            for nm, src in (("R", C.sR), ("K", C.sK), ("B", C.sB), ("A", C.sA), ("V", C.sV), ("G", C.sG)):
                t_ = r_in.next()
                S.dma("sp", t_.v(), dv(src[r0:r0 + 128, :]))
                ins[nm] = t_
            Rt, Kt, Bt, At, Vb, Gb = (ins[n] for n in "RKBAVG")
            xc = r_x.next()
            S.dma("sp", xc.v(), dv(C.out[r0:r0 + 128, :]))
            rk = r_rp.next()
            S.dma("sp", rk.v(), dv(C.rk[r0:r0 + 128, :]))
            pc = r_rp.next()
            S.dma("sp", pc.v(), dv(C.pcs[s * NT + i]))
            X.update(Kt=Kt, Bt=Bt, Vb=Vb, Gb=Gb, xc=xc, rk=rk, pc=pc, r0=r0, first=(i == 0))
            yield
            ART = r_art.next()
            BtT = r_bk.next()
            KtT = r_bk.next()
            for src, dstv, eng in ((At, ART[:, :, 0, :], "act"), (Rt, ART[:, :, 1, :], "dve"),
                                   (Bt, BtT.v(), "act"), (Kt, KtT.v(), "dve")):
                pt_ = tpF.next()
                for p in range(8):
                    S.tr(pt_[:, p * 128:(p + 1) * 128], src[:, p * 128:(p + 1) * 128], ID)
                S.copy(eng, dstv, pt_.v().rr("q (p t) -> q p t", p=8))
                yield
            Apv = Apad.v().rr("t (p hh) c -> t p hh c", hh=2)
            Atv = At.v().rr("t (p hh k) -> t p hh k", hh=2, k=64)
            S.copy("pool", Apv[:, :, 0, 0:64], Atv[:, :, 0, :])
            S.copy("pool", Apv[:, :, 1, 64:128], Atv[:, :, 1, :])
            AMb = r_amb.next()
            X.update(ART=ART, AMb=AMb)
            P = [None] * 4
            Q = [None] * 4
            TT = [None] * 4
            for g in range(4):
                for hl in range(4):
                    h = 4 * g + hl
                    p, pb = h // 2, (h % 2) * 64
                    aps = gpF.next()
                    art2 = ART[pb:pb + 64, p, :, :].rr("k a t -> k (a t)")
                    S.mm(aps[:, 0:256], BtT[pb:pb + 64, p, :], art2)
                    S.mm(aps[:, 256:512], KtT[pb:pb + 64, p, :], art2)
                    a4 = aps.v().rr("q (b a t) -> q b a t", b=2, a=2)
                    S.tt("dve", AMf[:, h, :, :], a4[:, :, 0, :], lt2.v(), ALU.mult)
                    S.tt("dve", AMb[:, h, :, :], a4[:, :, 1, :], le2.v(), ALU.mult)
                    if hl % 2 == 1:
                        yield
                npsl = [gpF.next(), gpF.next()]
                for hl in range(4):
                    h = 4 * g + hl
                    p, pb = h // 2, (h % 2) * 64
                    S.mm(npsl[h % 2][:, (hl // 2) * 128:(hl // 2 + 1) * 128], ART[pb:pb + 64, p, 0, :], BtT[pb:pb + 64, p, :])
                P[g] = r_pq.next()
                Pv = P[g].v().rr("q (a two) t -> q a two t", two=2)
                for par in range(2):
                    S.tt("dve", Pv[:, :, par, :], npsl[par][:, 0:256].rr("q (a t) -> q a t", a=2), gt2.v(), ALU.mult)
                Q[g] = AMf[:, 4 * g:4 * g + 4, 0, :]
                TT[g] = r_tt.next()
                S.tt("pool", TT[g].v(), Q[g], id4.v(), ALU.add)
                yield
            def tt_update(g, Pn):
                tps = gpF.next()
                for hl in range(4):
                    S.mm(tps[:, hl * 128:(hl + 1) * 128], Pn[:, hl, :], TT[g][:, hl, :])
                TTn = r_tt.next()
                S.tt("dve", TTn.v().rr("q a t -> q (a t)"), tps.v(), TT[g].v().rr("q a t -> q (a t)"), ALU.add)
                TT[g] = TTn
            pend = None
            for j in range(1, 7):
                for g in range(4):
                    pps = gpF.next()
                    for hl in range(4):
                        S.mm(pps[:, hl * 128:(hl + 1) * 128], Q[g][:, hl, :], P[g][:, hl, :])
                    if j < 6:
                        qps = gpF.next()
                        for hl in range(4):
                            S.mm(qps[:, hl * 128:(hl + 1) * 128], P[g][:, hl, :], Q[g][:, hl, :])
                    Pn = r_pq.next()
                    S.copy("act", Pn.v().rr("q a t -> q (a t)"), pps.v())
                    if j < 6:
                        Qn = r_pq.next()
                        S.copy("act", Qn.v().rr("q a t -> q (a t)"), qps.v())
                        Q[g] = Qn.v()
                    P[g] = Pn
                    if pend is not None:
                        tt_update(*pend)
                    pend = (g, Pn)
                    yield
            tt_update(*pend)
            yield
            AVb = r_av.next()
            Wb = r_w.next()
            for half in range(2):
                avp = gpF.next()
                for hl in range(8):
                    h = half * 8 + hl
                    S.mm(avp[:, hl * 64:(hl + 1) * 64], AMf[:, h, 1, :], Vb[:, h * 64:(h + 1) * 64])
                S.copy("act", AVb[:, half * 512:(half + 1) * 512], avp.v())
                yield
            for half in range(2):
                wp = gpF.next()
                for hl in range(8):
                    h = half * 8 + hl
                    S.mm(wp[:, hl * 64:(hl + 1) * 64], TT[h // 4][:, h % 4, :], AVb[:, h * 64:(h + 1) * 64])
                S.copy("act", Wb[:, half * 512:(half + 1) * 512], wp.v())
                yield
            AhT = r_aht.next()
            for half in range(2):
                ahp = gpF.next()
                for pl in range(4):
                    p = half * 4 + pl
                    h0, h1 = 2 * p, 2 * p + 1
                    S.mm(ahp[:, pl * 128:(pl + 1) * 128], Apad[:, h0, :], TT[h0 // 4][:, h0 % 4, :], start=True, stop=False)
                    S.mm(ahp[:, pl * 128:(pl + 1) * 128], Apad[:, h1, :], TT[h1 // 4][:, h1 % 4, :], start=False, stop=True)
                S.copy("dve", AhT[:, half * 4:(half + 1) * 4, :].rr("q a t -> q (a t)"), ahp.v())
                yield
            X.update(Wb=Wb, AhT=AhT)

        def back(tile, X):
            Kt, Bt, Vb, Gb, xc, rk, pc, r0 = (X[k] for k in ("Kt", "Bt", "Vb", "Gb", "xc", "rk", "pc", "r0"))
            ART, AMb, Wb, AhT = X["ART"], X["AMb"], X["Wb"], X["AhT"]
            if X["first"]:
                S.memset("pool", Mf.v(), 0.0)
                S.memset("pool", Mb.v(), 0.0)
            Yf = r_f.next()
            for p in range(8):
                pcs_ = slice(p * 128, (p + 1) * 128)
                ups = gpB.next()
                S.mm(ups[:, 0:128], AhT[:, p, :], Mb[:, p, :])
                Ub = r_u.next()
                S.tt("dve", Ub.v(), ups[:, 0:128], Wb[:, pcs_], ALU.add)
                yield
                yps = gpB.next()
                S.mm(yps[:, 0:128], ART[:, p, 1, :], Mb[:, p, :], start=True, stop=False)
                for hh in range(2):
                    h = 2 * p + hh
                    S.mm(yps[:, hh * 64:(hh + 1) * 64], AMb[:, h, 0, :], Ub[:, hh * 64:(hh + 1) * 64], start=False, stop=False)
                    S.mm(yps[:, hh * 64:(hh + 1) * 64], AMb[:, h, 1, :], Vb[:, h * 64:(h + 1) * 64], start=False, stop=(hh == 1))
                S.copy("act", Yf[:, pcs_], yps[:, 0:128])
                mps = gpB.next()
                S.mm(mps[:, 0:128], Bt[:, pcs_], Ub.v(), start=True, stop=False)
                S.mm(mps[:, 0:128], Kt[:, pcs_], Vb[:, pcs_], start=False, stop=True)
                tmp = r_tmp.next()
                S.stt(tmp.v(), mps[:, 0:128], pc[:, 2 * p:2 * p + 1], BD, ALU.mult, ALU.mult)
                S.stt(Mf[:, p, :], Mf[:, p, :], pc[:, 2 * p:2 * p + 1], tmp.v(), ALU.mult, ALU.add)
                S.copy("act", Mb[:, p, :], Mf[:, p, :])
                yield
            Y3 = Yf.v().rr("q (h d) -> q h d", h=16)
            sm = r_sm.next()
            S.reduce(sm.v(), Y3)
            S.ts("dve", sm.v(), sm.v(), 1.0 / 64, ALU.mult)
            S.tt("dve", Y3, Y3, bcv(sm.v(), [128, 16, 64]), ALU.subtract)
            sq = r_f.next()
            S.tt("pool", sq.v(), Yf.v(), Yf.v(), ALU.mult)
            yield
            sm2 = r_sm.next()
            S.reduce(sm2.v(), sq.v().rr("q (h d) -> q h d", h=16))
            rms_rstd(S, C, sm2.v(), sm2.v(), 64, GN_EPS)
            S.tt("dve", Y3, Y3, bcv(sm2.v(), [128, 16, 64]), ALU.mult)
            S.tt("pool", Yf.v(), Yf.v(), lnxg.v(), ALU.mult)
            S.tt("pool", Yf.v(), Yf.v(), lnxb.v(), ALU.add)
            yield
            S.tt("dve", sq.v().rr("q (h d) -> q h d", h=16), Vb.v().rr("q (h d) -> q h d", h=16),
                 bcv(rk.v(), [128, 16, 64]), ALU.mult)
            S.tt("pool", Yf.v(), Yf.v(), sq.v(), ALU.add)
            yfin = r_yfin.next()
            S.tt("dve", yfin.v(), Yf.v(), Gb.v(), ALU.mult)
            yield
            pT = tpB.next()
            for k in range(8):
                S.tr(pT[:, k * 128:(k + 1) * 128], yfin[:, k * 128:(k + 1) * 128], ID)
            yT = r_yT.next()
            S.copy("act", yT.v().rr("q k t -> q (k t)"), pT.v())
            yield
            h3 = r_h.next()
            for n in range(2):
                ps = gpB.next()
                for k in range(8):
                    S.mm(ps.v(), yT[:, k, :], w_o[:, k, n * 512:(n + 1) * 512], start=(k == 0), stop=(k == 7))
                S.tt("dve", h3[:, n * 512:(n + 1) * 512], ps.v(), xc[:, n * 512:(n + 1) * 512], ALU.add)
                yield
            S.dma("sp", dv(C.out[r0:r0 + 128, :]), h3.v())

        tiles = [(s, i) for s in range(NSEQ) for i in range(NT)]
        run_pipelined(tiles, front, back)
        S.barrier()


def phase_F(C):
    S = C.S
    T, NSEQ = C.T, C.NSEQ
    NT = T // 128
    with contextlib.ExitStack() as st:
        w_up = S.sbuf(st, "w_upF", [128, 8, DFF], BF16)
        load_w(S, w_up, C.mlp_w_up[1], 8, DFF)
        w_down = S.sbuf(st, "w_downF", [128, 32, D], BF16)
        load_w(S, w_down, C.mlp_w_down[1], 32, D)
        gbf = S.sbuf(st, "gbfF", [128, D], F32)
        bload(S, gbf.v(), C.norm_ffn_g[1:2, :], D)
        R = MlpRings(S, st, "F", 3)

        def front(tile, X):
            s, i = tile
            r0 = s * T + i * 128
            xt = R.h.next()
            S.dma("sp", xt.v(), dv(C.out[r0:r0 + 128, :]))
            hid = R.hid.next()
            X.update(h=xt, hid=hid, r0=r0)
            yield from mlp_front(C, R, xt, gbf, w_up, hid)

        def back(tile, X):
            yield from mlp_back(C, R, X["h"], w_down, X["hid"], X["r0"])

        tiles = [(s, i) for s in range(NSEQ) for i in range(NT)]
        run_pipelined(tiles, front, back)
        S.barrier()


INPUT_SPECS = [
    ("norm_mix_g", [2, D]), ("norm_ffn_g", [2, D]), ("ab_w_in", [1, D, ABIN]),
    ("hgrn_lower_bounds", [3, 512]), ("hgrn_norm_g", [1, 512]), ("fox_forget_bias", [1, 8]),
    ("fox_q_norm_g", [1, 64]), ("fox_k_norm_g", [1, 64]), ("ab_w_out", [1, D, D]),
    ("rwkv_mu", [1, 6, D]), ("rwkv_w_rkv", [1, 3, D, D]), ("rwkv_w0", [1, D]),
    ("rwkv_w1", [1, D, 64]), ("rwkv_w2", [1, 64, D]), ("rwkv_a0", [1, D]),
    ("rwkv_a1", [1, D, 64]), ("rwkv_a2", [1, 64, D]), ("rwkv_g1", [1, D, 128]),
    ("rwkv_g2", [1, 128, D]), ("rwkv_k_k", [1, D]), ("rwkv_k_a", [1, D]),
    ("rwkv_r_k", [1, 16, 64]), ("rwkv_lnx_g", [1, D]), ("rwkv_lnx_b", [1, D]),
    ("rwkv_w_o", [1, D, D]), ("mlp_w_up", [2, D, DFF]), ("mlp_w_down", [2, DFF, D]),
]


def build(T, NSEQ, upto="F"):
    nc = bass.Bass("TRN2", target_bir_lowering=False)
    C = Ctx()
    C.nc = nc
    C.T, C.NSEQ = T, NSEQ
    C.cut = 0
    NTOK = T * NSEQ
    C.x = nc.dram_tensor("x", [NTOK, D], F32, kind="ExternalInput").ap()
    aps = {}
    for name, shp in INPUT_SPECS:
        aps[name] = nc.dram_tensor(name, shp, F32, kind="ExternalInput").ap()
    C.norm_mix_g = aps["norm_mix_g"]
    C.norm_ffn_g = aps["norm_ffn_g"]
    C.ab_w_in = aps["ab_w_in"]
    C.hgrn_lb = aps["hgrn_lower_bounds"]
    C.hgrn_norm_g = aps["hgrn_norm_g"]
    C.fox_fb = aps["fox_forget_bias"]
    C.fox_q_g = aps["fox_q_norm_g"]
    C.fox_k_g = aps["fox_k_norm_g"]
    C.ab_w_out = aps["ab_w_out"]
    C.mlp_w_up = aps["mlp_w_up"]
    C.mlp_w_down = aps["mlp_w_down"]
    C.aps = aps
    cst = nc.dram_tensor("cst", [128, NCONST * 128], F32, kind="ExternalInput").ap()
    C.out = nc.dram_tensor("out", [NTOK, D], F32, kind="ExternalOutput").ap()

    def scratch(name, shp, dt):
        return nc.dram_tensor(name, shp, dt, kind="ExternalOutput").ap()
    sR = scratch("sR", [NTOK, D], BF16)
    sK = scratch("sK", [NTOK, D], BF16)
    sB = scratch("sB", [NTOK, D], BF16)
    C.sR, C.sK, C.sB = sR, sK, sB
    C.sA = scratch("sA", [NTOK, D], BF16)
    C.sV = scratch("sV", [NTOK, D], BF16)
    C.sG = scratch("sG", [NTOK, D], BF16)
    C.rk = scratch("rk", [NTOK, 16], F32)
    C.pcs = scratch("pcs", [NTOK // 128, 128, 16], F32)
    sRv = sR.rearrange("(x r) d -> x (r d)", x=2).rearrange("x (s a p t) -> x s a p t", s=NSEQ, a=4, p=128)
    C.fq = sRv[0]
    C.fk = sRv[1]
    C.fv = sK[:, 0:512]
    C.fg = sK[:, 512:1024]
    C.fc = scratch("fc", [NTOK, 8], F32)
    C.ya = sB[:, 0:512]
    C.yb = sB[:, 512:1024]
    C.h2 = C.out
    with contextlib.ExitStack() as st:
        S = Sched(nc, st)
        C.S = S
        cf = S.sbuf(st, "cstf", [128, NCONST * 128], F32)
        cbt = S.sbuf(st, "cstb", [128, NCONST * 128], BF16)
        S.dma("sp", cf.v(), dv(cst))
        S.copy("dve", cbt.v(), cf.v())
        C.cf, C.cb = cf, cbt
        phase_A(C)
        if upto != "A":
            phase_B(C)
            phase_C(C)
        if upto not in ("A", "C"):
            phase_D(C)
            if upto != "D":
                phase_E(C)
        if upto == "F":
            phase_F(C)
        S.final_wait("sp")
        C.ninst = S.ninst
    return nc, C


_CACHE = {}


def kernel(**inputs):
    x = np.ascontiguousarray(inputs["x"], dtype=np.float32)
    B, T, _ = x.shape
    NSEQ = B // NCORES
    key = (T, NSEQ)
    if key not in _CACHE:
        _CACHE[key] = build(T, NSEQ)[0]
    nc = _CACHE[key]
    cst = make_consts()
    shared = {name: np.ascontiguousarray(inputs[name], dtype=np.float32) for name, _ in INPUT_SPECS}
    shared["cst"] = cst
    in_maps = []
    for c in range(NCORES):
        m = dict(shared)
        m["x"] = x[c * NSEQ:(c + 1) * NSEQ].reshape(NSEQ * T, D)
        in_maps.append(m)
    res = run_bass_kernel_spmd(nc, in_maps, core_ids=list(range(NCORES)))
    outs = [np.asarray(r["out"]).reshape(NSEQ, T, D) for r in res.results]
    return np.concatenate(outs, axis=0).astype(np.float32)
```
